# Optimizing a Trainium2 kernel written in Bass

```python
import math
import jax, jax.numpy as jnp
from jax import lax
import numpy as np


D_MODEL = 2048
BATCH = 2
SEQ = 8192
DEPTH = 4

D_MIX = D_MODEL
S5_WIDTH = D_MIX // 4
S5_GROUP = 16
S5_GROUPS = S5_WIDTH // S5_GROUP
S5_STATE = 64
S5_DT_MIN = 1e-3
S5_DT_MAX = 1e-1
S5_EIG_CLIP = -1e-4
GDN_HEAD_DIM = 128
GDN_WIDTH = D_MIX // 2
GDN_HEADS = GDN_WIDTH // GDN_HEAD_DIM
GDN_CHUNK = 64
LRU_WIDTH = D_MIX - S5_WIDTH - GDN_WIDTH
LRU_BLOCKS = 8
LRU_BLOCK = LRU_WIDTH // LRU_BLOCKS
LRU_C = 8.0
CONV_WIDTH = 4
D_FF = ((8 * D_MODEL // 3 + 255) // 256) * 256
IN_SPLITS = (S5_WIDTH, 3 * GDN_WIDTH, GDN_WIDTH, GDN_HEADS, GDN_HEADS, LRU_WIDTH, LRU_WIDTH)
D_IN_PROJ = sum(IN_SPLITS)
LN_EPS = 1e-5
RMS_EPS = 1e-6
DEEPNORM_ALPHA = (2.0 * DEPTH) ** 0.25
DEEPNORM_BETA = (8.0 * DEPTH) ** -0.25
MACARON_WEIGHT = 0.5

kernel_name = 'hybrid_s5_gdn_rglru_macaron_deepnorm'


def layer_norm(x, g, b):
    xf = x.astype(jnp.float32)
    mu = jnp.mean(xf, axis=-1, keepdims=True)
    xc = xf - mu
    var = jnp.mean(xc * xc, axis=-1, keepdims=True)
    return (xc * lax.rsqrt(var + LN_EPS) * g + b).astype(x.dtype)


def swiglu(x, w_gate, w_up, w_down):
    return (jax.nn.silu(x @ w_gate) * (x @ w_up)) @ w_down


def causal_depthwise_conv(x, w):
    k_width, seq = w.shape[0], x.shape[1]
    xp = jnp.pad(x, ((0, 0), (k_width - 1, 0), (0, 0)))
    y = xp[:, 0:seq] * w[0]
    for k in range(1, k_width):
        y = y + xp[:, k:k + seq] * w[k]
    return y


def linear_recurrence_combine(left, right):
    a_l, b_l = left
    a_r, b_r = right
    return a_l * a_r, a_r * b_l + b_r


def s5_mixer(u, lam_re, lam_im, b_re, b_im, c_re, c_im, d, log_step, w_glu, b_glu):
    bsz, seq, _ = u.shape
    uf = u.astype(jnp.float32).reshape(bsz, seq, S5_GROUPS, S5_GROUP)
    lam = lax.complex(jnp.minimum(lam_re.astype(jnp.float32), S5_EIG_CLIP), lam_im.astype(jnp.float32))
    dt = jnp.exp(log_step.astype(jnp.float32))[:, None]
    lam_bar = jnp.exp(lam * dt)
    b_mat = lax.complex(b_re.astype(jnp.float32), b_im.astype(jnp.float32))
    b_bar = ((lam_bar - 1.0) / lam)[..., None] * b_mat
    bu = jnp.einsum('blgp,gnp->blgn', uf.astype(jnp.complex64), b_bar)
    a = jnp.broadcast_to(lam_bar, bu.shape)
    _, h = lax.associative_scan(linear_recurrence_combine, (a, bu), axis=1)
    c_mat = lax.complex(c_re.astype(jnp.float32), c_im.astype(jnp.float32))
    y = jnp.einsum('blgn,gpn->blgp', h, c_mat).real + d.astype(jnp.float32).reshape(S5_GROUPS, S5_GROUP) * uf
    y = jax.nn.gelu(y.reshape(bsz, seq, S5_WIDTH))
    return y * jax.nn.sigmoid(y @ w_glu + b_glu)


def l2_normalize(x):
    return x * lax.rsqrt(jnp.sum(x * x, axis=-1, keepdims=True) + RMS_EPS)


def chunk_gated_delta_rule(q, k, v, g, beta):
    bsz, seq, heads, dk = q.shape
    dv = v.shape[-1]
    n_chunks = seq // GDN_CHUNK

    def to_chunks(t):
        return t.reshape(bsz, n_chunks, GDN_CHUNK, heads, -1).transpose(1, 0, 3, 2, 4)

    q, k, v = to_chunks(q), to_chunks(k), to_chunks(v)
    beta_c = beta.reshape(bsz, n_chunks, GDN_CHUNK, heads).transpose(1, 0, 3, 2)
    gc = jnp.cumsum(g.reshape(bsz, n_chunks, GDN_CHUNK, heads).transpose(1, 0, 3, 2), axis=-1)
    causal = jnp.tril(jnp.ones((GDN_CHUNK, GDN_CHUNK), dtype=bool))
    strict = jnp.tril(jnp.ones((GDN_CHUNK, GDN_CHUNK), dtype=bool), -1)
    decay = jnp.exp(jnp.where(causal, gc[..., :, None] - gc[..., None, :], -jnp.inf))
    k_beta = k * beta_c[..., None]
    v_beta = v * beta_c[..., None]
    m = jnp.where(strict, jnp.einsum('nbhid,nbhjd->nbhij', k_beta, k) * decay, 0.0)
    t_mat = jnp.eye(GDN_CHUNK, dtype=jnp.float32) + m
    rhs = jnp.concatenate([v_beta, k_beta * jnp.exp(gc)[..., None]], axis=-1)
    sol = lax.linalg.triangular_solve(t_mat, rhs, left_side=True, lower=True, unit_diagonal=True)
    u_val, w_val = sol[..., :dv], sol[..., dv:]
    qk = jnp.where(causal, jnp.einsum('nbhid,nbhjd->nbhij', q, k) * decay, 0.0)
    q_dec = q * jnp.exp(gc)[..., None]
    k_dec = k * jnp.exp(gc[..., -1:] - gc)[..., None]
    g_last = jnp.exp(gc[..., -1])

    def step(state, xs):
        u_i, w_i, qd_i, qk_i, kd_i, gl_i = xs
        v_new = u_i - jnp.einsum('bhck,bhkv->bhcv', w_i, state)
        o_i = jnp.einsum('bhck,bhkv->bhcv', qd_i, state) + jnp.einsum('bhij,bhjv->bhiv', qk_i, v_new)
        state = state * gl_i[..., None, None] + jnp.einsum('bhck,bhcv->bhkv', kd_i, v_new)
        return state, o_i

    state0 = jnp.zeros((bsz, heads, dk, dv), jnp.float32)
    _, o = lax.scan(step, state0, (u_val, w_val, q_dec, qk, k_dec, g_last))
    return o.transpose(1, 0, 3, 2, 4).reshape(bsz, seq, heads, dv)


def gdn_mixer(qkv, z, a_logit, b_logit, conv_w, a_log, dt_bias, norm_g):
    bsz, seq, _ = qkv.shape
    qkv = jax.nn.silu(causal_depthwise_conv(qkv, conv_w)).astype(jnp.float32)
    q, k, v = jnp.split(qkv, 3, axis=-1)
    heads_shape = (bsz, seq, GDN_HEADS, GDN_HEAD_DIM)
    q = l2_normalize(q.reshape(heads_shape)) * (GDN_HEAD_DIM ** -0.5)
    k = l2_normalize(k.reshape(heads_shape))
    v = v.reshape(heads_shape)
    beta = jax.nn.sigmoid(b_logit.astype(jnp.float32))
    g = -jnp.exp(a_log.astype(jnp.float32)) * jax.nn.softplus(a_logit.astype(jnp.float32) + dt_bias)
    o = chunk_gated_delta_rule(q, k, v, g, beta)
    o = o * lax.rsqrt(jnp.mean(o * o, axis=-1, keepdims=True) + RMS_EPS) * norm_g
    o = o * jax.nn.silu(z.astype(jnp.float32).reshape(heads_shape))
    return o.reshape(bsz, seq, GDN_WIDTH)


def rglru_mixer(xb, gate, conv_w, conv_b, w_a, b_a, w_x, b_x, lam):
    bsz, seq, _ = xb.shape
    xc = (causal_depthwise_conv(xb, conv_w) + conv_b).astype(jnp.float32)
    xh = xc.reshape(bsz, seq, LRU_BLOCKS, LRU_BLOCK)
    r = jax.nn.sigmoid(jnp.einsum('blhi,hij->blhj', xh, w_a).reshape(bsz, seq, LRU_WIDTH) + b_a)
    i = jax.nn.sigmoid(jnp.einsum('blhi,hij->blhj', xh, w_x).reshape(bsz, seq, LRU_WIDTH) + b_x)
    log_a = -LRU_C * r * jax.nn.softplus(-lam.astype(jnp.float32))
    a = jnp.exp(log_a)
    bt = jnp.sqrt(-jnp.expm1(2.0 * log_a)) * (i * xc)
    _, h = lax.associative_scan(linear_recurrence_combine, (a, bt), axis=1)
    return h * jax.nn.gelu(gate.astype(jnp.float32))


def setup_inputs(seed: int = 0) -> dict:
    key = jax.random.key(seed)
    keys = jax.random.split(key, 64)
    counter = [0]

    def nk():
        counter[0] += 1
        return keys[counter[0]]

    def normal(shape, scale):
        return jax.random.normal(nk(), shape, jnp.float32) * scale

    def uniform(shape, lo, hi):
        return jax.random.uniform(nk(), shape, jnp.float32, lo, hi)

    L = DEPTH
    inp = {}
    inp['x'] = normal((BATCH, SEQ, D_MODEL), 1.0)
    inp['ffn1_w_gate'] = normal((L, D_MODEL, D_FF), D_MODEL ** -0.5)
    inp['ffn1_w_up'] = normal((L, D_MODEL, D_FF), D_MODEL ** -0.5)
    inp['ffn1_w_down'] = normal((L, D_FF, D_MODEL), DEEPNORM_BETA * D_FF ** -0.5)
    inp['ln1_g'] = 1.0 + normal((L, D_MODEL), 0.02)
    inp['ln1_b'] = normal((L, D_MODEL), 0.02)
    inp['w_in'] = normal((L, D_MODEL, D_IN_PROJ), D_MODEL ** -0.5)
    n_idx = jnp.arange(S5_STATE, dtype=jnp.float32)
    inp['s5_lambda_re'] = -0.5 + normal((L, S5_GROUPS, S5_STATE), 0.01)
    inp['s5_lambda_im'] = math.pi * n_idx + normal((L, S5_GROUPS, S5_STATE), 0.01)
    inp['s5_b_re'] = normal((L, S5_GROUPS, S5_STATE, S5_GROUP), (2.0 * S5_GROUP) ** -0.5)
    inp['s5_b_im'] = normal((L, S5_GROUPS, S5_STATE, S5_GROUP), (2.0 * S5_GROUP) ** -0.5)
    inp['s5_c_re'] = normal((L, S5_GROUPS, S5_GROUP, S5_STATE), S5_STATE ** -0.5)
    inp['s5_c_im'] = normal((L, S5_GROUPS, S5_GROUP, S5_STATE), S5_STATE ** -0.5)
    inp['s5_d'] = normal((L, S5_WIDTH), 1.0)
    inp['s5_log_step'] = uniform((L, S5_GROUPS), math.log(S5_DT_MIN), math.log(S5_DT_MAX))
    inp['s5_w_glu'] = normal((L, S5_WIDTH, S5_WIDTH), S5_WIDTH ** -0.5)
    inp['s5_b_glu'] = normal((L, S5_WIDTH), 0.02)
    inp['gdn_conv_w'] = normal((L, CONV_WIDTH, 3 * GDN_WIDTH), CONV_WIDTH ** -0.5)
    inp['gdn_a_log'] = jnp.log(uniform((L, GDN_HEADS), 1.0, 16.0))
    dt = jnp.exp(uniform((L, GDN_HEADS), math.log(1e-3), math.log(1e-1)))
    inp['gdn_dt_bias'] = dt + jnp.log(-jnp.expm1(-dt))
    inp['gdn_norm_g'] = 1.0 + normal((L, GDN_HEAD_DIM), 0.02)
    inp['lru_conv_w'] = normal((L, CONV_WIDTH, LRU_WIDTH), CONV_WIDTH ** -0.5)
    inp['lru_conv_b'] = normal((L, LRU_WIDTH), 0.02)
    inp['lru_w_a'] = normal((L, LRU_BLOCKS, LRU_BLOCK, LRU_BLOCK), LRU_BLOCK ** -0.5)
    inp['lru_b_a'] = normal((L, LRU_WIDTH), 0.02)
    inp['lru_w_x'] = normal((L, LRU_BLOCKS, LRU_BLOCK, LRU_BLOCK), LRU_BLOCK ** -0.5)
    inp['lru_b_x'] = normal((L, LRU_WIDTH), 0.02)
    a0 = uniform((L, LRU_WIDTH), 0.9, 0.999) ** (1.0 / LRU_C)
    inp['lru_lambda'] = jnp.log(a0) - jnp.log1p(-a0)
    inp['w_out'] = normal((L, D_MIX, D_MODEL), DEEPNORM_BETA * D_MIX ** -0.5)
    inp['ln2_g'] = 1.0 + normal((L, D_MODEL), 0.02)
    inp['ln2_b'] = normal((L, D_MODEL), 0.02)
    inp['ffn2_w_gate'] = normal((L, D_MODEL, D_FF), D_MODEL ** -0.5)
    inp['ffn2_w_up'] = normal((L, D_MODEL, D_FF), D_MODEL ** -0.5)
    inp['ffn2_w_down'] = normal((L, D_FF, D_MODEL), DEEPNORM_BETA * D_FF ** -0.5)
    inp['ln3_g'] = 1.0 + normal((L, D_MODEL), 0.02)
    inp['ln3_b'] = normal((L, D_MODEL), 0.02)
    return inp


def reference(x, ffn1_w_gate, ffn1_w_up, ffn1_w_down, ln1_g, ln1_b, w_in,
              s5_lambda_re, s5_lambda_im, s5_b_re, s5_b_im, s5_c_re, s5_c_im, s5_d, s5_log_step,
              s5_w_glu, s5_b_glu, gdn_conv_w, gdn_a_log, gdn_dt_bias, gdn_norm_g,
              lru_conv_w, lru_conv_b, lru_w_a, lru_b_a, lru_w_x, lru_b_x, lru_lambda,
              w_out, ln2_g, ln2_b, ffn2_w_gate, ffn2_w_up, ffn2_w_down, ln3_g, ln3_b):
    split_points = []
    acc = 0
    for size in IN_SPLITS[:-1]:
        acc += size
        split_points.append(acc)
    for l in range(DEPTH):
        x = layer_norm(DEEPNORM_ALPHA * x + MACARON_WEIGHT * swiglu(x, ffn1_w_gate[l], ffn1_w_up[l], ffn1_w_down[l]),
                       ln1_g[l], ln1_b[l])
        proj = x @ w_in[l]
        s5_u, gdn_qkv, gdn_z, gdn_a, gdn_b, lru_x, lru_gate = jnp.split(proj, split_points, axis=-1)
        y_s5 = s5_mixer(s5_u, s5_lambda_re[l], s5_lambda_im[l], s5_b_re[l], s5_b_im[l], s5_c_re[l], s5_c_im[l],
                        s5_d[l], s5_log_step[l], s5_w_glu[l], s5_b_glu[l])
        y_gdn = gdn_mixer(gdn_qkv, gdn_z, gdn_a, gdn_b, gdn_conv_w[l], gdn_a_log[l], gdn_dt_bias[l], gdn_norm_g[l])
        y_lru = rglru_mixer(lru_x, lru_gate, lru_conv_w[l], lru_conv_b[l], lru_w_a[l], lru_b_a[l],
                            lru_w_x[l], lru_b_x[l], lru_lambda[l])
        mixed = jnp.concatenate([y_s5, y_gdn, y_lru], axis=-1) @ w_out[l]
        x = layer_norm(DEEPNORM_ALPHA * x + mixed.astype(x.dtype), ln2_g[l], ln2_b[l])
        x = layer_norm(DEEPNORM_ALPHA * x + MACARON_WEIGHT * swiglu(x, ffn2_w_gate[l], ffn2_w_up[l], ffn2_w_down[l]),
                       ln3_g[l], ln3_b[l])
    return x
```

```python
import numpy as np
import concourse.bass as bass
import concourse.mybir as mybir
from concourse.bass_utils import run_bass_kernel_spmd

F32 = mybir.dt.float32
BF16 = mybir.dt.bfloat16
AF = mybir.ActivationFunctionType
ALU = mybir.AluOpType

D_MODEL = 2048
D_FF = 5632
DEPTH = 4
N_CORES = 8
ALPHA = (2.0 * DEPTH) ** 0.25
LN_EPS = 1e-5


class Sched:
    def __init__(self, nc, n_dma_sems=24):
        self.nc = nc
        self.eng = {"pe": nc.tensor, "dve": nc.vector, "act": nc.scalar, "pool": nc.gpsimd, "sp": nc.sync}
        self.sem = {}
        self.cnt = {}
        self.seen = {k: {} for k in self.eng}
        self._cms = []
        for k in self.eng:
            cm = nc.semaphore("s_" + k)
            self.sem[k] = cm.__enter__()
            self._cms.append(cm)
            self.cnt[k] = 0
        self.dma_sems = []
        for i in range(n_dma_sems):
            cm = nc.semaphore("s_dma%d" % i)
            self.dma_sems.append([cm.__enter__(), 0])
            self._cms.append(cm)
        self.dma_rr = 0
        self.last_w = {}
        self.readers = {}
        self.semobj = {k: self.sem[k] for k in self.eng}
        for i, (s, _) in enumerate(self.dma_sems):
            self.semobj["dma%d" % i] = s

    def close(self):
        for cm in reversed(self._cms):
            cm.__exit__(None, None, None)

    def _wait(self, e, semkey, val):
        if self.seen[e].get(semkey, 0) >= val:
            return
        self.eng[e].wait_ge(self.semobj[semkey], val)
        self.seen[e][semkey] = val

    def _deps(self, e, reads, writes):
        for r in reads:
            w = self.last_w.get(r)
            if w is not None:
                self._wait(e, *w)
        for r in writes:
            w = self.last_w.get(r)
            if w is not None:
                self._wait(e, *w)
            for sk, v in self.readers.get(r, {}).items():
                self._wait(e, sk, v)

    def _commit(self, semkey, val, reads, writes):
        for r in reads:
            self.readers.setdefault(r, {})[semkey] = val
        for r in writes:
            self.last_w[r] = (semkey, val)
            self.readers[r] = {}

    def op(self, e, fn, reads=(), writes=()):
        self._deps(e, reads, writes)
        ins = fn(self.eng[e])
        self.cnt[e] += 1
        ins.then_inc(self.sem[e], 1)
        self._commit(e, self.cnt[e], reads, writes)

    def dma(self, q, out, in_, reads=(), writes=()):
        i = self.dma_rr
        self.dma_rr = (self.dma_rr + 1) % len(self.dma_sems)
        sk = "dma%d" % i
        self._wait(q, sk, self.dma_sems[i][1])
        self._deps(q, reads, writes)
        ins = self.eng[q].dma_start(out=out, in_=in_)
        self.dma_sems[i][1] += 16
        ins.then_inc(self.dma_sems[i][0], 16)
        self._commit(sk, self.dma_sems[i][1], reads, writes)

    def finish(self, e="sp"):
        for r, (sk, v) in list(self.last_w.items()):
            self._wait(e, sk, v)

    def barrier(self):
        tot = {k: self.cnt[k] for k in self.eng}
        for i, (s, v) in enumerate(self.dma_sems):
            tot["dma%d" % i] = v
        for e in self.eng:
            for sk, v in tot.items():
                if v > 0:
                    self._wait(e, sk, v)
        self.last_w = {}
        self.readers = {}

    def tt(self, e, out, a, b, op, r, w):
        self.op(e, lambda g: g.tensor_tensor(out=out, in0=a, in1=b, op=op), reads=r, writes=w)

    def ts(self, e, out, a, s1, op0, r, w, s2=None, op1=None):
        if op1 is None:
            self.op(e, lambda g: g.tensor_scalar(out=out, in0=a, scalar1=s1, scalar2=None, op0=op0), reads=r, writes=w)
        else:
            self.op(e, lambda g: g.tensor_scalar(out=out, in0=a, scalar1=s1, scalar2=s2, op0=op0, op1=op1), reads=r, writes=w)

    def stt(self, out, a, s, b, op0, op1, r, w):
        self.op("dve", lambda g: g.scalar_tensor_tensor(out=out, in0=a, scalar=s, in1=b, op0=op0, op1=op1), reads=r, writes=w)

    def act(self, out, a, func, r, w, scale=None, bias=None):
        kw = {}
        if scale is not None:
            kw["scale"] = scale
        if bias is not None:
            kw["bias"] = bias
        self.op("act", lambda g: g.activation(out=out, in_=a, func=func, **kw), reads=r, writes=w)

    def mm(self, out, lhsT, rhs, r, w, start=True, stop=True):
        self.op("pe", lambda g: g.matmul(out, lhsT=lhsT, rhs=rhs, start=start, stop=stop), reads=r, writes=w)


def _alloc(nc, stack, kind, name, shape, dt):
    cm = (nc.sbuf_tensor if kind == "sb" else nc.psum_tensor)(name, shape, dt)
    t = cm.__enter__()
    stack.append(cm)
    return t


def build_ffn(NT, D=D_MODEL, F=D_FF, TB=None, mode="ffn", NPJ=0, GW=512):
    nc = bass.Bass("TRN2", target_bir_lowering=False)
    TB = TB or min(512, NT)
    QW = min(512, TB)
    rscale = 0.5 if mode == "ffn" else 1.0
    DC, FC = D // 128, F // 128
    FW = 256
    NFB = F // FW
    FPB = FW // 128
    xT = nc.dram_tensor("xT", [D, NT], F32, kind="ExternalInput").ap()
    if mode == "ffn":
        wg = nc.dram_tensor("wg", [D, F], F32, kind="ExternalInput").ap()
        wu = nc.dram_tensor("wu", [D, F], F32, kind="ExternalInput").ap()
        wg_v = wg.rearrange("(c p) f -> p c f", p=128)
        wu_v = wu.rearrange("(c p) f -> p c f", p=128)
    else:
        ymT = nc.dram_tensor("ymT", [F, NT], F32, kind="ExternalInput").ap()
        wglu = nc.dram_tensor("wglu", [GW, GW], F32, kind="ExternalInput").ap()
        bglu = nc.dram_tensor("bglu", [128, GW // 128], F32, kind="ExternalInput").ap()
        ymT_v = ymT.rearrange("(c p) t -> p c t", p=128)
        wglu_v = wglu.rearrange("(c p) f -> p c f", p=128)
    if NPJ:
        win = nc.dram_tensor("win", [D, NPJ], F32, kind="ExternalInput").ap()
        win_v = win.rearrange("(c p) f -> p c f", p=128)
        pjT = nc.dram_tensor("pjT", [NPJ, NT], F32, kind="ExternalOutput").ap()
    wd = nc.dram_tensor("wd", [F, D], F32, kind="ExternalInput").ap()
    gb = nc.dram_tensor("gb", [128, 2 * DC], F32, kind="ExternalInput").ap()
    yT = nc.dram_tensor("yT", [D, NT], F32, kind="ExternalOutput").ap()
    xT_v = xT.rearrange("(c p) t -> p c t", p=128)
    wd_v = wd.rearrange("(c p) d -> p c d", p=128)
    yT_v = yT.rearrange("(c p) t -> p c t", p=128)

    st = []
    S = Sched(nc)
    xb = _alloc(nc, st, "sb", "xb", [128, DC, TB], BF16)
    aT = _alloc(nc, st, "sb", "aT", [128, FC, TB], BF16)
    rT = _alloc(nc, st, "sb", "rT", [128, DC, TB], F32)
    wgs = [_alloc(nc, st, "sb", "wgs%d" % i, [128, DC, FW], BF16) for i in range(2)]
    wus = [_alloc(nc, st, "sb", "wus%d" % i, [128, DC, FW], BF16) for i in range(2)]
    WDG = [w for w in (11, 4, 2, 1) if FC % w == 0][0]
    NDG = FC // WDG
    wds = [_alloc(nc, st, "sb", "wds%d" % i, [128, WDG, 128], BF16) for i in range(3)]
    gbs = _alloc(nc, st, "sb", "gbs", [128, 2 * DC], F32)
    ones = _alloc(nc, st, "sb", "ones", [128, 128], F32)
    sg = [_alloc(nc, st, "sb", "sg%d" % i, [128, 512], F32) for i in range(2)]
    GC = GW // 128
    if mode == "mix":
        y5b = _alloc(nc, st, "sb", "y5b", [128, GC, TB], BF16)
        y5f = _alloc(nc, st, "sb", "y5f", [128, GC, TB], F32)
        wgl = _alloc(nc, st, "sb", "wgl", [128, GC, GW], BF16)
        bgl = _alloc(nc, st, "sb", "bgl", [128, GC], F32)
    if NPJ:
        ybf = _alloc(nc, st, "sb", "ybf", [128, DC, TB], BF16)
        po = [_alloc(nc, st, "sb", "po%d" % i, [128, TB], F32) for i in range(2)]
    xf = [_alloc(nc, st, "sb", "xf%d" % i, [128, TB], F32) for i in range(2)]
    sq = [_alloc(nc, st, "sb", "sq%d" % i, [128, TB], F32) for i in range(2)]
    mean = _alloc(nc, st, "sb", "mean", [128, TB], F32)
    rstd = _alloc(nc, st, "sb", "rstd", [128, TB], F32)
    yo = [_alloc(nc, st, "sb", "yo%d" % i, [128, TB], F32) for i in range(2)]
    pg = [_alloc(nc, st, "ps", "pg%d" % i, [128, 512], F32) for i in range(2)]
    pu = [_alloc(nc, st, "ps", "pu%d" % i, [128, 512], F32) for i in range(2)]
    pd = [_alloc(nc, st, "ps", "pd%d" % i, [128, 512], F32) for i in range(2)]
    pm = _alloc(nc, st, "ps", "pm", [128, 512], F32)
    pv = _alloc(nc, st, "ps", "pv", [128, 512], F32)
    NQ = TB // QW

    S.dma("sp", gbs[:], gb, writes=["gbs"])
    S.op("dve", lambda e: e.memset(ones[:], 1.0 / D), writes=["ones"])

    kgu = 0
    kd = 0
    kx = 0
    ko = 0
    for tb in range(NT // TB):
        t0 = tb * TB
        S.dma("pool", xb[:], xT_v[:, :, t0:t0 + TB], writes=["xb"])
        if mode == "mix":
            if tb == 0:
                S.dma("pool", wgl[:], wglu_v, writes=["wgl"])
                S.dma("sp", bgl[:], bglu, writes=["bgl"])
            S.dma("pool", y5b[:], ymT_v[:, 0:GC, t0:t0 + TB], writes=["y5b"])
            S.dma("sp", y5f[:], ymT_v[:, 0:GC, t0:t0 + TB], writes=["y5f"])
            for q in range(NQ):
                S.dma("pool", aT[:, GC:FC, q * QW:(q + 1) * QW], ymT_v[:, GC:FC, t0 + q * QW:t0 + (q + 1) * QW],
                      writes=[("aT", fc, q) for fc in range(GC, FC)])
            for jc in range(GC):
                for q in range(NQ):
                    pb = kgu % 2
                    kgu += 1
                    for ic in range(GC):
                        S.mm(pg[pb][:, 0:QW], wgl[:, ic, jc * 128:(jc + 1) * 128], y5b[:, ic, q * QW:(q + 1) * QW],
                             ["wgl", "y5b"], [("pg", pb)], start=(ic == 0), stop=(ic == GC - 1))
                    S.act(sg[pb][:, 0:QW], pg[pb][:, 0:QW], AF.Sigmoid, [("pg", pb), "bgl"], [("sg", pb)], bias=bgl[:, jc:jc + 1])
                    S.tt("dve", aT[:, jc, q * QW:(q + 1) * QW], sg[pb][:, 0:QW], y5f[:, jc, q * QW:(q + 1) * QW], ALU.mult,
                         [("sg", pb), "y5f"], [("aT", jc, q)])
        if mode == "ffn":
            for fb in range(NFB):
                b = fb % 2
                S.dma("pool", wgs[b][:], wg_v[:, :, fb * FW:(fb + 1) * FW], writes=[("wg", b)])
                S.dma("pool", wus[b][:], wu_v[:, :, fb * FW:(fb + 1) * FW], writes=[("wu", b)])
                for fi in range(FPB):
                    fc = fb * FPB + fi
                    for q in range(NQ):
                        pb = kgu % 2
                        kgu += 1
                        for c in range(DC):
                            S.op("pe", lambda e, c=c: e.matmul(pg[pb][:, 0:QW], lhsT=wgs[b][:, c, fi * 128:(fi + 1) * 128],
                                                             rhs=xb[:, c, q * QW:(q + 1) * QW], start=(c == 0), stop=(c == DC - 1)),
                                 reads=[("wg", b), "xb"], writes=[("pg", pb)])
                        for c in range(DC):
                            S.op("pe", lambda e, c=c: e.matmul(pu[pb][:, 0:QW], lhsT=wus[b][:, c, fi * 128:(fi + 1) * 128],
                                                             rhs=xb[:, c, q * QW:(q + 1) * QW], start=(c == 0), stop=(c == DC - 1)),
                                 reads=[("wu", b), "xb"], writes=[("pu", pb)])
                        S.op("act", lambda e: e.activation(out=sg[pb][:, 0:QW], in_=pg[pb][:, 0:QW], func=AF.Silu),
                             reads=[("pg", pb)], writes=[("sg", pb)])
                        S.op("dve", lambda e: e.tensor_tensor(out=aT[:, fc, q * QW:(q + 1) * QW], in0=sg[pb][:, 0:QW], in1=pu[pb][:, 0:QW], op=ALU.mult),
                             reads=[("sg", pb), ("pu", pb)], writes=[("aT", fc, q)])
        for dco in range(DC):
            xb_ = kx % 2
            kx += 1
            S.dma("sp", xf[xb_][:], xT_v[:, dco, t0:t0 + TB], writes=[("xf", xb_)])
            for q in range(NQ):
                pb = kd % 2
                for g in range(NDG):
                    wb = kd % 3 if False else (kd * NDG + g) % 3
                    S.dma("pool", wds[wb][:], wd_v[:, g * WDG:(g + 1) * WDG, dco * 128:(dco + 1) * 128], writes=[("wd", wb)])
                    for j in range(WDG):
                        fc = g * WDG + j
                        S.op("pe", lambda e, fc=fc, j=j: e.matmul(pd[pb][:, 0:QW], lhsT=wds[wb][:, j, :], rhs=aT[:, fc, q * QW:(q + 1) * QW],
                                                                 start=(fc == 0), stop=(fc == FC - 1)),
                             reads=[("wd", wb), ("aT", fc, q)], writes=[("pd", pb)])
                kd += 1
                S.op("act", lambda e: e.activation(out=rT[:, dco, q * QW:(q + 1) * QW], in_=pd[pb][:, 0:QW], func=AF.Copy, scale=rscale),
                     reads=[("pd", pb)], writes=[("rT", dco, q)])
                S.op("dve", lambda e: e.scalar_tensor_tensor(out=rT[:, dco, q * QW:(q + 1) * QW], in0=xf[xb_][:, q * QW:(q + 1) * QW],
                                                           scalar=ALPHA, in1=rT[:, dco, q * QW:(q + 1) * QW], op0=ALU.mult, op1=ALU.add),
                     reads=[("xf", xb_), ("rT", dco, q)], writes=[("rT", dco, q)])
        for q in range(NQ):
            qs = slice(q * QW, (q + 1) * QW)
            for c in range(DC):
                S.op("pe", lambda e, c=c: e.matmul(pm[:, 0:QW], lhsT=ones[:], rhs=rT[:, c, qs], start=(c == 0), stop=(c == DC - 1)),
                     reads=["ones", ("rT", c, q)], writes=["pm"])
            S.op("act", lambda e: e.activation(out=mean[:, qs], in_=pm[:, 0:QW], func=AF.Copy), reads=["pm"], writes=[("mean", q)])
            for c in range(DC):
                S.op("dve", lambda e, c=c: e.tensor_tensor(out=rT[:, c, qs], in0=rT[:, c, qs], in1=mean[:, qs], op=ALU.subtract),
                     reads=[("rT", c, q), ("mean", q)], writes=[("rT", c, q)])
                sb_ = c % 2
                S.op("act", lambda e, c=c: e.activation(out=sq[sb_][:, qs], in_=rT[:, c, qs], func=AF.Square),
                     reads=[("rT", c, q)], writes=[("sq", sb_, q)])
                S.op("pe", lambda e, c=c: e.matmul(pv[:, 0:QW], lhsT=ones[:], rhs=sq[sb_][:, qs], start=(c == 0), stop=(c == DC - 1)),
                     reads=["ones", ("sq", sb_, q)], writes=["pv"])
            S.op("act", lambda e: e.activation(out=rstd[:, qs], in_=pv[:, 0:QW], func=AF.Sqrt, bias=LN_EPS, scale=1.0), reads=["pv"], writes=[("rstd", q)])
            S.op("dve", lambda e: e.reciprocal(out=rstd[:, qs], in_=rstd[:, qs]), reads=[("rstd", q)], writes=[("rstd", q)])
        for c in range(DC):
            ob = ko % 2
            ko += 1
            for q in range(NQ):
                qs = slice(q * QW, (q + 1) * QW)
                S.op("dve", lambda e: e.tensor_tensor(out=yo[ob][:, qs], in0=rT[:, c, qs], in1=rstd[:, qs], op=ALU.mult),
                     reads=[("rT", c, q), ("rstd", q)], writes=[("yo", ob)])
            S.op("pool", lambda e: e.tensor_scalar(out=yo[ob][:], in0=yo[ob][:], scalar1=gbs[:, c:c + 1], scalar2=gbs[:, DC + c:DC + c + 1],
                                                   op0=ALU.mult, op1=ALU.add),
                 reads=[("yo", ob), "gbs"], writes=[("yo", ob)])
            S.dma("sp", yT_v[:, c, t0:t0 + TB], yo[ob][:], reads=[("yo", ob)], writes=[("yT", c, tb)])
            if NPJ:
                S.act(ybf[:, c, :], yo[ob][:], AF.Copy, [("yo", ob)], [("ybf", c)])
        if NPJ:
            kp = 0
            for pb0 in range(0, NPJ, FW):
                pw = min(FW, NPJ - pb0)
                b = (pb0 // FW) % 2
                S.dma("pool", wgs[b][:, :, 0:pw], win_v[:, :, pb0:pb0 + pw], writes=[("wg", b)])
                for pc in range(0, pw, 128):
                    m = min(128, pw - pc)
                    for q in range(NQ):
                        pb = kp % 2
                        kp += 1
                        for c in range(DC):
                            S.mm(pg[pb][0:m, 0:QW], wgs[b][:, c, pc:pc + m], ybf[:, c, q * QW:(q + 1) * QW],
                                 [("wg", b)] + ([("ybf", cc) for cc in range(DC)] if c == 0 else []), [("pg", pb)], start=(c == 0), stop=(c == DC - 1))
                        S.act(po[pb][0:m, q * QW:(q + 1) * QW], pg[pb][0:m, 0:QW], AF.Copy, [("pg", pb)], [("po", pb)])
                        S.dma("sp", pjT[pb0 + pc:pb0 + pc + m, t0 + q * QW:t0 + (q + 1) * QW], po[pb][0:m, q * QW:(q + 1) * QW],
                              reads=[("po", pb)], writes=[("pjT", pb0 + pc, tb, q)])
    S.finish("sp")
    S.close()
    for cm in reversed(st):
        cm.__exit__(None, None, None)
    return nc


class Pool_:
    def __init__(self, nc, st, prefix, shape, n, dt=F32):
        self.tiles = [_alloc(nc, st, "sb", "%s%d" % (prefix, i), shape, dt) for i in range(n)]
        self.names = ["%s%d" % (prefix, i) for i in range(n)]
        self.k = 0

    def get(self):
        i = self.k % len(self.tiles)
        self.k += 1
        return self.tiles[i], self.names[i]


def emit_softplus(S, P, x, xn, out, outn, sl):
    ax, axn = P.get()
    S.act(sl(ax), x, AF.Abs, [xn], [axn])
    y, yn = P.get()
    S.act(sl(y), sl(ax), AF.Exp, [axn], [yn], scale=-1.0)
    t, tn = P.get()
    S.ts("dve", sl(t), sl(y), 2.0, ALU.add, [yn], [tn])
    S.op("dve", lambda g: g.reciprocal(out=sl(t), in_=sl(t)), reads=[tn], writes=[tn])
    s, sn = P.get()
    S.tt("dve", sl(s), sl(y), sl(t), ALU.mult, [yn, tn], [sn])
    s2, s2n = P.get()
    S.tt("dve", sl(s2), sl(s), sl(s), ALU.mult, [sn], [s2n])
    p, pn = P.get()
    S.ts("dve", sl(p), sl(s2), 1.0 / 11.0, ALU.mult, [s2n], [pn], s2=1.0 / 9.0, op1=ALU.add)
    for cst in (1.0 / 7.0, 1.0 / 5.0, 1.0 / 3.0, 1.0):
        S.tt("dve", sl(p), sl(p), sl(s2), ALU.mult, [pn, s2n], [pn])
        S.ts("dve", sl(p), sl(p), cst, ALU.add, [pn], [pn])
    S.tt("dve", sl(p), sl(p), sl(s), ALU.mult, [pn, sn], [pn])
    S.ts("dve", sl(ax), x, 0.0, ALU.max, [xn], [axn])
    S.stt(out, sl(p), 2.0, sl(ax), ALU.mult, ALU.add, [pn, axn], [outn])


def emit_gelu(S, out, outn, x, xn, tmp, tmpn, eng2="pool"):
    S.act(tmp, x, AF.Square, [xn], [tmpn])
    S.ts("dve", tmp, tmp, 0.044715, ALU.mult, [tmpn], [tmpn], s2=1.0, op1=ALU.add)
    S.tt("dve", tmp, tmp, x, ALU.mult, [tmpn, xn], [tmpn])
    S.act(tmp, tmp, AF.Sigmoid, [tmpn], [tmpn], scale=1.5957691216057308)
    S.tt(eng2, out, tmp, x, ALU.mult, [tmpn, xn], [outn])


def build_mix(L, NB=2, do_lru=True, do_s5=True, do_gdn=True):
    nc = bass.Bass("TRN2", target_bir_lowering=False)
    NTOK = NB * L
    S = Sched(nc)
    dr = {}

    def din(name, shape):
        dr[name] = nc.dram_tensor(name, shape, F32, kind="ExternalInput").ap()
        return dr[name]

    def dout(name, shape):
        dr[name] = nc.dram_tensor(name, shape, F32, kind="ExternalOutput").ap()
        return dr[name]

    if do_lru:
        emit_lru(nc, S, L, NB, din, dout)
        S.barrier()
    if do_s5:
        emit_s5(nc, S, L, NB, din, dout)
        S.barrier()
    if do_gdn:
        emit_gdn(nc, S, L, NB, din, dout)
    S.finish("sp")
    S.close()
    return nc


def emit_lru(nc, S, L, NB, din, dout):
    assert NB == 2
    TS = min(L, 2048)
    NSEG = L // TS
    lx = din("lx", [128, L])
    lg = din("lg", [128, L])
    lpar = din("lpar", [128, 8])
    lwa = din("lwa", [128, 128])
    lwx = din("lwx", [128, 128])
    ly = dout("ly", [128, L])
    st = []
    par = _alloc(nc, st, "sb", "l_par", [128, 8], F32)
    wa = _alloc(nc, st, "sb", "l_wa", [128, 128], BF16)
    wx = _alloc(nc, st, "sb", "l_wx", [128, 128], BF16)
    c12 = _alloc(nc, st, "sb", "l_c12", [128, 2], F32)
    carry = _alloc(nc, st, "sb", "l_carry", [128, 1], F32)
    tiny = Pool_(nc, st, "l_tiny", [128, 1], 8)
    big = Pool_(nc, st, "l_big", [128, TS], 9)
    xin = [_alloc(nc, st, "sb", "l_xin%d" % i, [128, TS + 3], F32) for i in range(2)]
    xcb = _alloc(nc, st, "sb", "l_xcb", [128, TS], BF16)
    pr = [_alloc(nc, st, "ps", "l_pr%d" % i, [128, 512], F32) for i in range(2)]
    pi = [_alloc(nc, st, "ps", "l_pi%d" % i, [128, 512], F32) for i in range(2)]

    S.dma("sp", par[:], lpar, writes=["l_par"])
    S.dma("pool", wa[:], lwa, writes=["l_wa"])
    S.dma("pool", wx[:], lwx, writes=["l_wx"])
    nl, nln = tiny.get()
    S.ts("dve", nl[:], par[:, 5:6], -1.0, ALU.mult, ["l_par"], [nln])
    sp_, spn = tiny.get()
    emit_softplus(S, tiny, nl[:], nln, sp_[:], spn, lambda t: t[:])
    S.ts("dve", c12[:, 0:1], sp_[:], -8.0, ALU.mult, [spn], ["l_c12"])
    S.ts("dve", c12[:, 1:2], sp_[:], -16.0, ALU.mult, [spn, "l_c12"], ["l_c12"])
    S.op("dve", lambda g: g.memset(carry[:], 0.0), writes=["l_carry"])

    for s in range(NSEG):
        xi = xin[s % 2]
        xn = "l_xin%d" % (s % 2)
        if s == 0:
            S.op("dve", lambda g: g.memset(xi[:, 0:3], 0.0), writes=[xn])
            S.dma("sp", xi[:, 3:3 + TS], lx[:, 0:TS], reads=[xn], writes=[xn])
        else:
            S.dma("sp", xi[:, :], lx[:, s * TS - 3:(s + 1) * TS], writes=[xn])
        xc, xcn = big.get()
        S.ts("dve", xc[:], xi[:, 3:3 + TS], par[:, 3:4], ALU.mult, [xn, "l_par"], [xcn], s2=par[:, 4:5], op1=ALU.add)
        for k in (2, 1, 0):
            S.stt(xc[:], xi[:, k:k + TS], par[:, k:k + 1], xc[:], ALU.mult, ALU.add, [xn, "l_par", xcn], [xcn])
        S.act(xcb[:], xc[:], AF.Copy, [xcn], ["l_xcb"])
        r, rn = big.get()
        ig, ign = big.get()
        for q in range(TS // 512):
            qs = slice(q * 512, (q + 1) * 512)
            b = q % 2
            S.mm(pr[b][:], wa[:], xcb[:, qs], ["l_wa", "l_xcb"], [("l_pr", b)])
            S.mm(pi[b][:], wx[:], xcb[:, qs], ["l_wx", "l_xcb"], [("l_pi", b)])
            S.act(r[:, qs], pr[b][:], AF.Sigmoid, [("l_pr", b), "l_par"], [rn], bias=par[:, 6:7])
            S.act(ig[:, qs], pi[b][:], AF.Sigmoid, [("l_pi", b), "l_par"], [ign], bias=par[:, 7:8])
        a, an = big.get()
        S.act(a[:], r[:], AF.Exp, [rn, "l_c12"], [an], scale=c12[:, 0:1])
        s2, s2n = big.get()
        S.act(s2[:], r[:], AF.Exp, [rn, "l_c12"], [s2n], scale=c12[:, 1:2])
        S.act(s2[:], s2[:], AF.Sqrt, [s2n], [s2n], scale=-1.0, bias=1.0)
        S.tt("dve", s2[:], s2[:], ig[:], ALU.mult, [s2n, ign], [s2n])
        S.tt("dve", s2[:], s2[:], xc[:], ALU.mult, [s2n, xcn], [s2n])
        h, hn = big.get()
        S.op("dve", lambda g: g.tensor_tensor_scan(out=h[:], data0=a[:], data1=s2[:], initial=carry[:, 0:1], op0=ALU.mult, op1=ALU.add),
             reads=[an, s2n, "l_carry"], writes=[hn])
        S.op("dve", lambda g: g.tensor_copy(out=carry[:], in_=h[:, TS - 1:TS]), reads=[hn], writes=["l_carry"])
        gt, gtn = big.get()
        S.dma("sp", gt[:], lg[:, s * TS:(s + 1) * TS], writes=[gtn])
        tmp, tmpn = big.get()
        ge, gen = big.get()
        emit_gelu(S, ge[:], gen, gt[:], gtn, tmp[:], tmpn)
        S.tt("dve", ge[:], ge[:], h[:], ALU.mult, [gen, hn], [gen])
        S.dma("sp", ly[:, s * TS:(s + 1) * TS], ge[:], reads=[gen], writes=[("ly", s)])
    S.barrier()
    for cm in reversed(st):
        cm.__exit__(None, None, None)


class Ex:
    def __init__(self, S, pool, ipool, sl):
        self.S, self.pool, self.ipool, self.sl = S, pool, ipool, sl

    def new(self):
        t, n = self.pool.get()
        return self.sl(t), n

    def bin(self, a, b, op, eng="dve"):
        o = self.new()
        self.S.tt(eng, o[0], a[0], b[0], op, [a[1], b[1]], [o[1]])
        return o

    def mul(self, a, b): return self.bin(a, b, ALU.mult)
    def add(self, a, b): return self.bin(a, b, ALU.add)
    def sub(self, a, b): return self.bin(a, b, ALU.subtract)

    def sc(self, a, c1, op0, c2=None, op1=None):
        o = self.new()
        self.S.ts("dve", o[0], a[0], c1, op0, [a[1]], [o[1]], s2=c2, op1=op1)
        return o

    def act(self, a, func, scale=None, bias=None):
        o = self.new()
        self.S.act(o[0], a[0], func, [a[1]], [o[1]], scale=scale, bias=bias)
        return o

    def recip(self, a):
        o = self.new()
        self.S.op("dve", lambda g: g.reciprocal(out=o[0], in_=a[0]), reads=[a[1]], writes=[o[1]])
        return o

    def frac(self, k):
        it, itn = self.ipool.get()
        it = self.sl(it)
        self.S.op("dve", lambda g: g.tensor_copy(out=it, in_=k[0]), reads=[k[1]], writes=[itn])
        kf = self.new()
        self.S.op("dve", lambda g: g.tensor_copy(out=kf[0], in_=it), reads=[itn], writes=[kf[1]])
        f = self.sub(k, kf)
        m = self.sc(f, 0.5, ALU.is_gt)
        f = self.sub(f, m)
        m = self.sc(f, -0.5, ALU.is_lt)
        f = self.add(f, m)
        return self.sc(f, 0.4999999, ALU.min, -0.4999999, ALU.max)

    def sincos(self, th):
        k = self.sc(th, 1.0 / (2.0 * np.pi), ALU.mult)
        return self.sincos_k(k)

    def sincos_k(self, k):
        s = self.act(self.frac(k), AF.Sin, scale=2.0 * np.pi)
        c = self.act(self.frac(self.sc(k, 0.25, ALU.add)), AF.Sin, scale=2.0 * np.pi)
        return s, c

    def s5_params(self, lre_in, lim, lstep):
        lre = self.sc(lre_in, -1e-4, ALU.min)
        dt = self.act(lstep, AF.Exp)
        rmag = self.act(self.mul(lre, dt), AF.Exp)
        sn, cs = self.sincos(self.mul(lim, dt))
        ar = self.mul(rmag, cs)
        ai = self.mul(rmag, sn)
        am1 = self.sc(ar, -1.0, ALU.add)
        num_r = self.add(self.mul(am1, lre), self.mul(ai, lim))
        num_i = self.sub(self.mul(ai, lre), self.mul(am1, lim))
        den = self.add(self.mul(lre, lre), self.mul(lim, lim))
        rden = self.recip(den)
        return rmag, cs, sn, self.mul(num_r, rden), self.mul(num_i, rden)


def emit_s5(nc, S, L, NB, din, dout):
    NTOK = NB * L
    T = 512
    NBLK = L // T
    su = din("su", [64, NTOK])
    s5row = din("s5row", [64, 3, 256])
    s5col = din("s5col", [128, 2, 3])
    s5bT = din("s5bT", [64, 2, 256])
    s5cT = din("s5cT", [128, 2, 2, 64])
    s5d = din("s5d", [64, 1])
    y5 = dout("y5", [64, NTOK])
    I32 = mybir.dt.int32
    st = []
    rowp = _alloc(nc, st, "sb", "s_rowp", [64, 3, 256], F32)
    colp = _alloc(nc, st, "sb", "s_colp", [128, 2, 3], F32)
    bT = _alloc(nc, st, "sb", "s_bT", [64, 2, 256], F32)
    cT = _alloc(nc, st, "sb", "s_cT", [128, 2, 2, 64], F32)
    sd = _alloc(nc, st, "sb", "s_sd", [64, 1], F32)
    BTr = _alloc(nc, st, "sb", "s_BTr", [64, 256], BF16)
    BTi = _alloc(nc, st, "sb", "s_BTi", [64, 256], BF16)
    cTr = _alloc(nc, st, "sb", "s_cTr", [128, 2, 64], BF16)
    cTi = _alloc(nc, st, "sb", "s_cTi", [128, 2, 64], BF16)
    rpool = Pool_(nc, st, "s_rp", [64, 256], 40)
    ripool = Pool_(nc, st, "s_rpi", [64, 256], 2, I32)
    cpool = Pool_(nc, st, "s_cp", [128, 2], 40)
    cipool = Pool_(nc, st, "s_cpi", [128, 2], 2, I32)
    Er = [_alloc(nc, st, "sb", "s_Er%d" % j, [128, T], F32) for j in range(2)]
    Ei = [_alloc(nc, st, "sb", "s_Ei%d" % j, [128, T], F32) for j in range(2)]
    Rt = [_alloc(nc, st, "sb", "s_Rt%d" % j, [128, T], F32) for j in range(2)]
    ttmp = _alloc(nc, st, "sb", "s_ttmp", [128, T], F32)
    wpow = [_alloc(nc, st, "sb", "s_wpow%d" % i, [128, 2, 2], F32) for i in range(2)]
    W512 = _alloc(nc, st, "sb", "s_W512", [128, 2, 2], F32)
    car = _alloc(nc, st, "sb", "s_car", [128, 2, 2 * NB], F32)
    ctmp = Pool_(nc, st, "s_ct", [128, 1], 6)
    uf = [_alloc(nc, st, "sb", "s_uf%d" % i, [64, T], F32) for i in range(2)]
    ub = [_alloc(nc, st, "sb", "s_ub%d" % i, [64, T], BF16) for i in range(2)]
    wk = Pool_(nc, st, "s_wk", [128, T], 12)
    hb = [[_alloc(nc, st, "sb", "s_hb%d%d" % (j, k), [128, T], BF16) for k in range(2)] for j in range(2)]
    yo = Pool_(nc, st, "s_yo", [64, T], 4)
    pbr = [_alloc(nc, st, "ps", "s_pbr%d" % j, [128, 512], F32) for j in range(2)]
    pbi = [_alloc(nc, st, "ps", "s_pbi%d" % j, [128, 512], F32) for j in range(2)]
    py = [_alloc(nc, st, "ps", "s_py%d" % j, [64, 512], F32) for j in range(2)]

    S.dma("sp", rowp[:], s5row, writes=["s_rowp"])
    S.dma("sp", colp[:], s5col, writes=["s_colp"])
    S.dma("sp", bT[:], s5bT, writes=["s_bT"])
    S.dma("sp", cT[:], s5cT, writes=["s_cT"])
    S.dma("sp", sd[:], s5d, writes=["s_sd"])
    ex = Ex(S, rpool, ripool, lambda t: t[:])
    _, _, _, kr, ki = ex.s5_params((rowp[:, 0, :], "s_rowp"), (rowp[:, 1, :], "s_rowp"), (rowp[:, 2, :], "s_rowp"))
    br, bi = (bT[:, 0, :], "s_bT"), (bT[:, 1, :], "s_bT")
    t1 = ex.sub(ex.mul(kr, br), ex.mul(ki, bi))
    t2 = ex.add(ex.mul(kr, bi), ex.mul(ki, br))
    S.act(BTr[:], t1[0], AF.Copy, [t1[1]], ["s_BTr"])
    S.act(BTi[:], t2[0], AF.Copy, [t2[1]], ["s_BTi"])
    S.act(cTr[:], cT[:, :, 0, :], AF.Copy, ["s_cT"], ["s_cTr"])
    S.act(cTi[:], cT[:, :, 1, :], AF.Copy, ["s_cT"], ["s_cTi"], scale=-1.0)
    ex = Ex(S, cpool, cipool, lambda t: t[:])
    rmag, cs, sn, _, _ = ex.s5_params((colp[:, :, 0], "s_colp"), (colp[:, :, 1], "s_colp"), (colp[:, :, 2], "s_colp"))
    k0 = _alloc(nc, st, "sb", "s_k0", [128, 2], F32)
    rmg = _alloc(nc, st, "sb", "s_rmg", [128, 2], F32)
    S.op("dve", lambda g: g.tensor_copy(out=rmg[:], in_=rmag[0]), reads=[rmag[1]], writes=["s_rmg"])
    dtc = ex.act((colp[:, :, 2], "s_colp"), AF.Exp)
    kk = ex.sc(ex.mul((colp[:, :, 1], "s_colp"), dtc), 1.0 / (2.0 * np.pi), ALU.mult)
    kf = ex.frac(kk)
    S.op("dve", lambda g: g.tensor_copy(out=k0[:], in_=kf[0]), reads=[kf[1]], writes=["s_k0"])
    sT, cT_ = ex.sincos_k(ex.sc((k0[:], "s_k0"), float(T), ALU.mult))
    S.op("dve", lambda g: g.tensor_copy(out=W512[:, 0, :], in_=cT_[0]), reads=[cT_[1]], writes=["s_W512"])
    S.op("dve", lambda g: g.tensor_copy(out=W512[:, 1, :], in_=sT[0]), reads=[sT[1], "s_W512"], writes=["s_W512"])
    tau = _alloc(nc, st, "sb", "s_tau", [128, T], F32)
    S.dma("sp", tau[:], din("s5tau", [128, T]), writes=["s_tau"])
    tpool = Pool_(nc, st, "s_tp", [128, T], 10)
    tipool = Pool_(nc, st, "s_tpi", [128, T], 2, I32)
    ext = Ex(S, tpool, tipool, lambda t: t[:])
    for j in range(2):
        ang = ext.new()
        S.ts("dve", ang[0], tau[:], k0[:, j:j + 1], ALU.mult, ["s_tau", "s_k0"], [ang[1]])
        sj, cj = ext.sincos_k(ang)
        S.op("dve", lambda g: g.tensor_copy(out=Er[j][:], in_=cj[0]), reads=[cj[1]], writes=["s_Er%d" % j])
        S.ts("dve", Ei[j][:], sj[0], -1.0, ALU.mult, [sj[1]], ["s_Ei%d" % j])
        S.ts("dve", Rt[j][:], tau[:], 0.0, ALU.mult, ["s_tau", "s_rmg"], ["s_Rt%d" % j], s2=rmg[:, j:j + 1], op1=ALU.add)
    S.op("dve", lambda g: g.memset(car[:], 0.0), writes=["s_car"])
    import os
    if os.environ.get("S5DBG"):
        d_er = dout("d_er", [128, T]); d_ei = dout("d_ei", [128, T]); d_rt = dout("d_rt", [128, T])
        d_col = dout("d_col", [128, 6]); d_bt = dout("d_bt", [64, 512])
        S.dma("sp", d_er, Er[0][:], reads=["s_Er0"], writes=["d_er"])
        S.dma("sp", d_ei, Ei[0][:], reads=["s_Ei0"], writes=["d_ei"])
        S.dma("sp", d_rt, Rt[0][:], reads=["s_Rt0"], writes=["d_rt"])
        S.dma("sp", d_col[:, 0:2], rmag[0], reads=[rmag[1]], writes=["d_col"])
        S.dma("sp", d_col[:, 2:4], cs[0], reads=[cs[1]], writes=["d_col2"])
        S.dma("sp", d_col[:, 4:6], sn[0], reads=[sn[1]], writes=["d_col3"])
        S.dma("sp", d_bt[:, 0:256], t1[0], reads=[t1[1]], writes=["d_bt"])
        S.dma("sp", d_bt[:, 256:512], t2[0], reads=[t2[1]], writes=["d_bt2"])

    k = 0
    for blk in range(NBLK):
        for b in range(NB):
            t0 = b * L + blk * T
            ui = k % 2
            k += 1
            ufn, ubn = "s_uf%d" % ui, "s_ub%d" % ui
            S.dma("sp", uf[ui][:], su[:, t0:t0 + T], writes=[ufn])
            S.act(ub[ui][:], uf[ui][:], AF.Copy, [ufn], [ubn])
            for j in range(2):
                js = slice(j * 128, (j + 1) * 128)
                S.mm(pbr[j][:], BTr[:, js], ub[ui][:], ["s_BTr", ubn], [("s_pbr", j)])
                S.mm(pbi[j][:], BTi[:, js], ub[ui][:], ["s_BTi", ubn], [("s_pbi", j)])
                ern, ein, rtn = "s_Er%d" % j, "s_Ei%d" % j, "s_Rt%d" % j
                a1, a1n = wk.get(); a2, a2n = wk.get(); a3, a3n = wk.get(); a4, a4n = wk.get()
                S.tt("dve", a1[:], pbr[j][:], Er[j][:], ALU.mult, [("s_pbr", j), ern], [a1n])
                S.tt("dve", a2[:], pbi[j][:], Ei[j][:], ALU.mult, [("s_pbi", j), ein], [a2n])
                S.tt("dve", a3[:], pbr[j][:], Ei[j][:], ALU.mult, [("s_pbr", j), ein], [a3n])
                S.tt("dve", a4[:], pbi[j][:], Er[j][:], ALU.mult, [("s_pbi", j), ern], [a4n])
                S.tt("pool", a1[:], a1[:], a2[:], ALU.subtract, [a1n, a2n], [a1n])
                S.tt("pool", a3[:], a3[:], a4[:], ALU.add, [a3n, a4n], [a3n])
                ci = j * NB + b
                Gr, Grn = wk.get(); Gi, Gin = wk.get()
                S.op("dve", lambda g: g.tensor_tensor_scan(out=Gr[:], data0=Rt[j][:], data1=a1[:], initial=car[:, 0, ci:ci + 1], op0=ALU.mult, op1=ALU.add),
                     reads=[rtn, a1n, "s_car"], writes=[Grn])
                S.op("dve", lambda g: g.tensor_tensor_scan(out=Gi[:], data0=Rt[j][:], data1=a3[:], initial=car[:, 1, ci:ci + 1], op0=ALU.mult, op1=ALU.add),
                     reads=[rtn, a3n, "s_car"], writes=[Gin])
                c1, c1n = ctmp.get(); c2, c2n = ctmp.get()
                Wr, Wi = W512[:, 0, j:j + 1], W512[:, 1, j:j + 1]
                S.ts("dve", c1[:], Gi[:, T - 1:T], Wi, ALU.mult, [Gin, "s_W512"], [c1n])
                S.ts("dve", c2[:], Gr[:, T - 1:T], Wi, ALU.mult, [Grn, "s_W512"], [c2n])
                S.stt(car[:, 0, ci:ci + 1], Gr[:, T - 1:T], Wr, c1[:], ALU.mult, ALU.subtract, [Grn, "s_W512", c1n, "s_car"], ["s_car"])
                S.stt(car[:, 1, ci:ci + 1], Gi[:, T - 1:T], Wr, c2[:], ALU.mult, ALU.add, [Gin, "s_W512", c2n, "s_car"], ["s_car"])
                S.tt("dve", a1[:], Gr[:], Er[j][:], ALU.mult, [Grn, ern], [a1n])
                S.tt("dve", a2[:], Gi[:], Ei[j][:], ALU.mult, [Gin, ein], [a2n])
                S.tt("dve", a3[:], Gi[:], Er[j][:], ALU.mult, [Gin, ern], [a3n])
                S.tt("dve", a4[:], Gr[:], Ei[j][:], ALU.mult, [Grn, ein], [a4n])
                S.tt("pool", hb[j][0][:], a1[:], a2[:], ALU.add, [a1n, a2n], [("s_hb", j, 0)])
                S.tt("pool", hb[j][1][:], a3[:], a4[:], ALU.subtract, [a3n, a4n], [("s_hb", j, 1)])
            pyi = k % 2
            n_ = 0
            for j in range(2):
                for c, ct in ((0, cTr), (1, cTi)):
                    S.mm(py[pyi][:], ct[:, j, :], hb[j][c][:], ["s_cTr", "s_cTi", ("s_hb", j, c)], [("s_py", pyi)], start=(n_ == 0), stop=(n_ == 3))
                    n_ += 1
            yv, yvn = yo.get(); tmp, tmpn = yo.get()
            S.stt(yv[:], uf[ui][:], sd[:, 0:1], py[pyi][:], ALU.mult, ALU.add, [ufn, "s_sd", ("s_py", pyi)], [yvn])
            emit_gelu(S, yv[:], yvn, yv[:], yvn, tmp[:], tmpn, eng2="dve")
            S.dma("sp", y5[:, t0:t0 + T], yv[:], reads=[yvn], writes=[("y5", t0)])
    S.barrier()
    for cm in reversed(st):
        cm.__exit__(None, None, None)


def emit_gdn(nc, S, L, NB, din, dout):
    NTOK = NB * L
    NT_ = NTOK // 128
    TSG = min(L, 1024)
    NSEG = L // TSG
    TPS = TSG // 128
    gq = din("gq", [128, 3, NTOK])
    gcw = din("gcw", [128, 3, 4])
    gz = din("gz", [NT_, 128, 128])
    gab = din("gab", [128, 2, NT_])
    gpar = din("gpar", [128, 2])
    gng = din("gng", [128, 128])
    gconst = din("gconst", [128, 8, 128])
    gy = dout("gy", [NT_, 128, 128])
    st = []
    A_ = lambda name, shape, dt=F32: _alloc(nc, st, "sb", name, shape, dt)
    cst = A_("g_cst", [128, 8, 128])
    ident, Ubd, Cbd, sel0, sel1, mSU, mIU, ones = [cst[:, i, :] for i in range(8)]
    cw = A_("g_cw", [128, 3, 4])
    ab = A_("g_ab", [128, 2, NT_])
    par = A_("g_par", [128, 2])
    ngb = A_("g_ngb", [128, 128])
    S.dma("sp", cst[:], gconst, writes=["g_cst"])
    S.dma("sp", cw[:], gcw, writes=["g_cw"])
    S.dma("sp", ab[:], gab, writes=["g_ab"])
    S.dma("sp", par[:], gpar, writes=["g_par"])
    S.dma("sp", ngb[:], gng, writes=["g_ngb"])
    pbig = _alloc(nc, st, "ps", "g_pbig", [128, 512], F32)
    pbanks = [_alloc(nc, st, "ps", "g_pb%d" % i, [128, 4, 128], F32) for i in range(5)]
    pslot = [pbanks[i // 4][:, i % 4, :] for i in range(20)]
    (ptk, ptv, pKK, pQK, pR1, pR2, pM, pP, pQ, pR, pwT, pu) = pslot[:12]
    rslots = [pslot[12:16], pslot[16:20]]
    B0, B1, B2 = "g_pb0", "g_pb1", "g_pb2"

    gp = Pool_(nc, st, "g_gp", [128, NT_], 12)
    sl = lambda t: t[:]
    beta = A_("g_beta", [128, NT_]); gc = A_("g_gc", [128, NT_]); kdsc = A_("g_kdsc", [128, NT_])
    egc = A_("g_egc", [128, NT_]); gcb = A_("g_gcb", [128, NT_]); begc = A_("g_begc", [128, NT_])
    gl = [A_("g_gl%d" % c, [128, NT_]) for c in range(2)]
    nea = A_("g_nea", [128, 1])
    S.act(beta[:], ab[:, 1, :], AF.Sigmoid, ["g_ab"], ["g_beta"])
    x_, xn_ = gp.get()
    S.ts("dve", x_[:], ab[:, 0, :], par[:, 1:2], ALU.add, ["g_ab", "g_par"], [xn_])
    sp_, spn_ = gp.get()
    emit_softplus(S, gp, x_[:], xn_, sp_[:], spn_, sl)
    S.act(nea[:], par[:, 0:1], AF.Exp, ["g_par"], ["g_nea"])
    NTP = max(NT_, 128)
    gpad = A_("g_gpad", [128, NTP])
    S.op("dve", lambda g: g.memset(gpad[:], 0.0), writes=["g_g"])
    g_ = gpad[:, 0:NT_]
    S.ts("dve", g_, sp_[:], nea[:, 0:1], ALU.mult, [spn_, "g_nea", "g_g"], ["g_g"], s2=-1.0, op1=ALU.mult)
    S.mm(pbig[:, 0:NTP], Ubd, gpad[:], ["g_cst", "g_g"], ["g_pbig"])
    S.act(gc[:], pbig[:, 0:NT_], AF.Copy, ["g_pbig"], ["g_gc"])
    S.mm(pbig[:, 0:NTP], Cbd, gpad[:], ["g_cst", "g_g"], ["g_pbig"])
    t_, tn_ = gp.get()
    S.tt("dve", t_[:], pbig[:, 0:NT_], gc[:], ALU.subtract, ["g_pbig", "g_gc"], [tn_])
    S.act(kdsc[:], t_[:], AF.Exp, [tn_], ["g_kdsc"])
    S.act(egc[:], gc[:], AF.Exp, ["g_gc"], ["g_egc"])
    S.act(t_[:], beta[:], AF.Ln, ["g_beta"], [tn_])
    S.tt("dve", gcb[:], gc[:], t_[:], ALU.add, ["g_gc", tn_], ["g_gcb"])
    S.tt("dve", begc[:], beta[:], egc[:], ALU.mult, ["g_beta", "g_egc"], ["g_begc"])
    for c, sel in ((0, sel0), (1, sel1)):
        S.mm(pbig[:, 0:NTP], sel, gpad[:], ["g_cst", "g_g"], ["g_pbig"])
        S.act(gl[c][:], pbig[:, 0:NT_], AF.Exp, ["g_pbig"], ["g_gl%d" % c])

    xin = [A_("g_xin%d" % w, [128, TSG + 3]) for w in range(3)]
    cv_ = [A_("g_cv%d" % w, [128, TSG]) for w in range(3)]
    sq = A_("g_sq", [128, TSG])
    rn = A_("g_rn", [128, 512])
    qTb = A_("g_qTb", [128, TSG], BF16)
    kTb = A_("g_kTb", [128, TSG], BF16)
    tp = Pool_(nc, st, "g_tp", [128, 128], 28)
    tpb = Pool_(nc, st, "g_tpb", [128, 128], 8, BF16)
    Sst = A_("g_S", [128, 128]); Sbf = A_("g_Sbf", [128, 128], BF16)
    col = Pool_(nc, st, "g_col", [128, 1], 6)
    zt = [A_("g_zt%d" % i, [128, 128]) for i in range(2)]

    import os
    GSTOP = float(os.environ.get("GSTOP", "9"))
    for b in range(NB if GSTOP > 1 else 0):
        S.op("dve", lambda g: g.memset(Sst[:], 0.0), writes=["g_S"])
        S.op("dve", lambda g: g.memset(Sbf[:], 0.0), writes=["g_Sbf"])
        for s in range(NSEG):
            tok0 = b * L + s * TSG
            for w in range(3):
                xn = "g_xin%d" % w
                cn = "g_cv%d" % w
                if s == 0:
                    S.op("dve", lambda g: g.memset(xin[w][:, 0:3], 0.0), writes=[xn])
                    S.dma("sp", xin[w][:, 3:3 + TSG], gq[:, w, tok0:tok0 + TSG], reads=[xn], writes=[xn])
                else:
                    S.dma("sp", xin[w][:, :], gq[:, w, tok0 - 3:tok0 + TSG], writes=[xn])
                S.ts("dve", cv_[w][:], xin[w][:, 3:3 + TSG], cw[:, w, 3:4], ALU.mult, [xn, "g_cw"], [cn])
                for k in (2, 1, 0):
                    S.stt(cv_[w][:], xin[w][:, k:k + TSG], cw[:, w, k:k + 1], cv_[w][:], ALU.mult, ALU.add, [xn, "g_cw", cn], [cn])
                S.act(cv_[w][:], cv_[w][:], AF.Silu, [cn], [cn])
            for w in range(2):
                cn = "g_cv%d" % w
                S.act(sq[:], cv_[w][:], AF.Square, [cn], ["g_sq"])
                QB = min(512, TSG)
                for q in range(TSG // QB):
                    qs = slice(q * QB, (q + 1) * QB)
                    S.mm(pbig[:, 0:QB], ones, sq[:, qs], ["g_cst", "g_sq"], ["g_pbig"])
                    if w == 0:
                        S.act(rn[:, 0:QB], pbig[:, 0:QB], AF.Sqrt, ["g_pbig"], ["g_rn"], scale=128.0, bias=128.0 * 1e-6)
                    else:
                        S.act(rn[:, 0:QB], pbig[:, 0:QB], AF.Sqrt, ["g_pbig"], ["g_rn"], scale=1.0, bias=1e-6)
                    S.op("dve", lambda g: g.reciprocal(out=rn[:, 0:QB], in_=rn[:, 0:QB]), reads=["g_rn"], writes=["g_rn"])
                    S.tt("dve", cv_[w][:, qs], cv_[w][:, qs], rn[:, 0:QB], ALU.mult, [cn, "g_rn"], [cn])
                S.act((qTb if w == 0 else kTb)[:], cv_[w][:], AF.Copy, [cn], ["g_qTb" if w == 0 else "g_kTb"])
            kn_, vn_ = cv_[1], cv_[2]
            for it in range(TPS if GSTOP > 2 else 0):
                ti = tok0 // 128 + it
                cs = slice(it * 128, (it + 1) * 128)
                tc_ = lambda a: a[:, ti:ti + 1]
                zi = ti % 2
                S.dma("sp", zt[zi][:], gz[ti], writes=["g_zt%d" % zi])
                S.op("pe", lambda g: g.transpose(out=ptk[:], in_=kn_[:, cs], identity=ident), reads=["g_cv1", "g_cst"], writes=[B0])
                kbg, kbgn = tp.get()
                S.ts("dve", kbg[:], ptk[:], tc_(begc), ALU.mult, [B0, "g_begc"], [kbgn])
                if GSTOP <= 2.05:
                    S.dma("sp", gy[ti], kbg[:], reads=[kbgn], writes=[("gy", ti)])
                    continue
                kdec, kdecn = tpb.get()
                S.ts("dve", kdec[:], ptk[:], tc_(kdsc), ALU.mult, [B0, "g_kdsc"], [kdecn])
                if GSTOP <= 2.1:
                    S.dma("sp", gy[ti], kbg[:], reads=[kbgn], writes=[("gy", ti)])
                    continue
                S.op("pe", lambda g: g.transpose(out=ptv[:], in_=vn_[:, cs], identity=ident), reads=["g_cv2", "g_cst"], writes=[B0])
                bv, bvn = tp.get()
                S.ts("dve", bv[:], ptv[:], tc_(beta), ALU.mult, [B0, "g_beta"], [bvn])
                if GSTOP <= 2.2:
                    S.dma("sp", gy[ti], bv[:], reads=[bvn], writes=[("gy", ti)])
                    continue
                S.mm(pKK[:], kTb[:, cs], kTb[:, cs], ["g_kTb"], [B0])
                S.mm(pQK[:], kTb[:, cs], qTb[:, cs], ["g_kTb", "g_qTb"], [B0])
                dg1, dg1n = tp.get(); dg2, dg2n = tp.get()
                S.ts("pool", dg1[:], ident, tc_(gcb), ALU.mult, ["g_cst", "g_gcb"], [dg1n])
                S.ts("pool", dg2[:], ident, tc_(gc), ALU.mult, ["g_cst", "g_gc"], [dg2n])
                S.mm(pR1[:], ones, dg1[:], ["g_cst", dg1n], [B1])
                S.mm(pR2[:], ones, dg2[:], ["g_cst", dg2n], [B1])
                E1, E1n = tp.get(); E2, E2n = tp.get()
                S.ts("dve", E1[:], pR1[:], tc_(gc), ALU.subtract, [B1, "g_gc"], [E1n], s2=0.0, op1=ALU.min)
                S.act(E1[:], E1[:], AF.Exp, [E1n], [E1n])
                S.ts("dve", E2[:], pR2[:], tc_(gc), ALU.subtract, [B1, "g_gc"], [E2n], s2=0.0, op1=ALU.min)
                S.act(E2[:], E2[:], AF.Exp, [E2n], [E2n])
                if GSTOP <= 2.4:
                    S.dma("sp", gy[ti], E1[:], reads=[E1n], writes=[("gy", ti)])
                    continue
                A0, A0n = tp.get()
                S.tt("dve", A0[:], pKK[:], E1[:], ALU.mult, [B0, E1n], [A0n])
                S.tt("pool", A0[:], A0[:], mSU, ALU.mult, [A0n, "g_cst"], [A0n])
                S.tt("dve", E2[:], pQK[:], E2[:], ALU.mult, [B0, E2n], [E2n])
                qkT, qkTn = tpb.get()
                S.tt("pool", qkT[:], E2[:], mIU, ALU.mult, [E2n, "g_cst"], [qkTn])
                S.op("pe", lambda g: g.transpose(out=pM[:], in_=A0[:], identity=ident), reads=[A0n, "g_cst"], writes=[B1])
                M0, M0n = tp.get()
                S.act(M0[:], pM[:], AF.Copy, [B1], [M0n])
                R, Rn = tp.get()
                S.tt("pool", R[:], ident, A0[:], ALU.subtract, ["g_cst", A0n], [Rn])
                if GSTOP <= 2.6:
                    S.dma("sp", gy[ti], R[:], reads=[Rn], writes=[("gy", ti)])
                    continue
                Pp, Ppn, Qp, Qpn = A0, A0n, M0, M0n
                for l in range(1, 6):
                    S.mm(pQ[:], Pp[:], Qp[:], [Ppn, Qpn], [B2])
                    Qn_, Qnn = tp.get()
                    S.act(Qn_[:], pQ[:], AF.Copy, [B2], [Qnn])
                    if l < 5:
                        S.mm(pP[:], Qp[:], Pp[:], [Ppn, Qpn], [B1])
                        Pn_, Pnn = tp.get()
                        S.op("dve", lambda g: g.tensor_copy(out=Pn_[:], in_=pP[:]), reads=[B1], writes=[Pnn])
                    S.mm(pR[:], Qn_[:], R[:], [Qnn, Rn], [B2])
                    R2, R2n = tp.get()
                    S.tt("dve", R2[:], R[:], pR[:], ALU.add, [Rn, B2], [R2n])
                    R, Rn = R2, R2n
                    Qp, Qpn = Qn_, Qnn
                    if l < 5:
                        Pp, Ppn = Pn_, Pnn
                if GSTOP <= 2.8:
                    S.dma("sp", gy[ti], R[:], reads=[Rn], writes=[("gy", ti)])
                    continue
                S.mm(pwT[:], kbg[:], R[:], [kbgn, Rn], [B2])
                wTb, wTbn = tpb.get()
                S.act(wTb[:], pwT[:], AF.Copy, [B2], [wTbn])
                S.mm(pu[:], R[:], bv[:], [Rn, bvn], [B2])
                u_, un_ = tp.get()
                S.op("dve", lambda g: g.tensor_copy(out=u_[:], in_=pu[:]), reads=[B2], writes=[un_])
                o_, on_ = tp.get()
                vnew, vnewn = tpb.get()
                tmp, tmpn = tp.get()
                if GSTOP <= 3:
                    S.dma("sp", gy[ti], u_[:], reads=[un_], writes=[("gy", ti)])
                    continue
                for c in range(2):
                    p = slice(64 * c, 64 * c + 64)
                    pwS, pqS, pqv, pdS = rslots[c]
                    RB = "g_pb%d" % (3 + c)
                    tk = slice(it * 128 + 64 * c, it * 128 + 64 * c + 64)
                    S.mm(pwS[p, :], wTb[:, p], Sbf[:], [wTbn, "g_Sbf"], [RB])
                    S.mm(pqS[p, :], qTb[:, tk], Sbf[:], ["g_qTb", "g_Sbf"], [RB])
                    S.tt("dve", vnew[p, :], u_[p, :], pwS[p, :], ALU.subtract, [un_, RB], [(vnewn, c)])
                    S.mm(pdS[:], kdec[p, :], vnew[p, :], [kdecn, (vnewn, c)], [RB])
                    S.mm(pqv[p, :], qkT[p, p], vnew[p, :], [qkTn, (vnewn, c)], [RB])
                    S.stt(Sst[:], Sst[:], gl[c][:, ti:ti + 1], pdS[:], ALU.mult, ALU.add, ["g_S", "g_gl%d" % c, RB], ["g_S"])
                    S.act(Sbf[:], Sst[:], AF.Copy, ["g_S"], ["g_Sbf"])
                    S.act(tmp[p, :], pqv[p, :], AF.Copy, [RB], [(tmpn, c)])
                    S.stt(o_[p, :], pqS[p, :], egc[p, ti:ti + 1], tmp[p, :], ALU.mult, ALU.add, [RB, "g_egc", (tmpn, c)], [(on_, c)])
                import os
                if os.environ.get("GDBG") and ti == 0:
                    def dbg(name, ap, rn_, shape=[128, 128]):
                        d = dout(name, shape)
                        S.dma("sp", d, ap, reads=rn_, writes=["dbg_" + name])
                    dbg("d_A", A0[:], [A0n]); dbg("d_R", R[:], [Rn]); dbg("d_u", u_[:], [un_]); dbg("d_o", o_[:], [(on_, 0), (on_, 1)])
                    dbg("d_kbg", kbg[:], [kbgn]); dbg("d_bv", bv[:], [bvn]); dbg("d_E1", E1[:], [E1n]); dbg("d_M", M0[:], [M0n])
                    dbg("d_gc", gc[:], ["g_gc"], [128, NT_]); dbg("d_beta", beta[:], ["g_beta"], [128, NT_]); dbg("d_g", g_, ["g_g"], [128, NT_])
                    dbg("d_kn", cv_[1][:, 0:128], ["g_cv1"]); dbg("d_qn", cv_[0][:, 0:128], ["g_cv0"]); dbg("d_S", Sst[:], ["g_S"])
                if GSTOP <= 4:
                    S.dma("sp", gy[ti], o_[:], reads=[(on_, 0), (on_, 1)], writes=[("gy", ti)])
                    continue
                ss, ssn = col.get()
                junk, junkn = tp.get()
                S.op("act", lambda g: g.activation(out=junk[:], in_=o_[:], func=AF.Square, accum_out=ss[:]),
                     reads=[(on_, 0), (on_, 1)], writes=[junkn, ssn])
                S.ts("dve", ss[:], ss[:], 1.0 / 128.0, ALU.mult, [ssn], [ssn], s2=1e-6, op1=ALU.add)
                S.act(ss[:], ss[:], AF.Sqrt, [ssn], [ssn])
                S.op("dve", lambda g: g.reciprocal(out=ss[:], in_=ss[:]), reads=[ssn], writes=[ssn])
                S.act(zt[zi][:], zt[zi][:], AF.Silu, ["g_zt%d" % zi], ["g_zt%d" % zi])
                y_, yn_ = tp.get()
                S.stt(y_[:], o_[:], ss[:, 0:1], ngb[:], ALU.mult, ALU.mult, [(on_, 0), (on_, 1), ssn, "g_ngb"], [yn_])
                S.tt("pool", y_[:], y_[:], zt[zi][:], ALU.mult, [yn_, "g_zt%d" % zi], [yn_])
                S.dma("sp", gy[ti], y_[:], reads=[yn_], writes=[("gy", ti)])
    S.barrier()
    for cm in reversed(st):
        cm.__exit__(None, None, None)


def gdn_consts():
    i = np.arange(128)
    same = (i[:, None] // 64) == (i[None, :] // 64)
    c = np.zeros((128, 8, 128), np.float32)
    c[:, 0] = np.eye(128)
    c[:, 1] = (i[:, None] <= i[None, :]) & same
    c[:, 2] = same
    c[:, 3] = (i[:, None] < 64) * np.ones((1, 128))
    c[:, 4] = (i[:, None] >= 64) * np.ones((1, 128))
    c[:, 5] = (i[:, None] < i[None, :]) & same
    c[:, 6] = (i[:, None] <= i[None, :]) & same
    c[:, 7] = 1.0
    return c


_PROGS = {}


def _prog(key, fn):
    if key not in _PROGS:
        _PROGS[key] = fn()
    return _PROGS[key]


def _gbpack(g, b):
    return np.ascontiguousarray(np.concatenate([g.reshape(-1, 128).T, b.reshape(-1, 128).T], axis=1), dtype=np.float32)


def _run(nc, in_maps):
    res = run_bass_kernel_spmd(nc, in_maps, core_ids=list(range(len(in_maps))))
    return res.results


def _c(a):
    return np.ascontiguousarray(a, dtype=np.float32)


def kernel(x, ffn1_w_gate, ffn1_w_up, ffn1_w_down, ln1_g, ln1_b, w_in,
           s5_lambda_re, s5_lambda_im, s5_b_re, s5_b_im, s5_c_re, s5_c_im, s5_d, s5_log_step,
           s5_w_glu, s5_b_glu, gdn_conv_w, gdn_a_log, gdn_dt_bias, gdn_norm_g,
           lru_conv_w, lru_conv_b, lru_w_a, lru_b_a, lru_w_x, lru_b_x, lru_lambda,
           w_out, ln2_g, ln2_b, ffn2_w_gate, ffn2_w_up, ffn2_w_down, ln3_g, ln3_b, depth=None):
    x = np.asarray(x)
    B, L, D = x.shape
    depth = DEPTH if depth is None else depth
    NC = N_CORES
    NTOK = B * L
    NTc = NTOK // NC
    NT_ = NTOK // 128
    NPJ = w_in.shape[-1]
    A = lambda v: np.asarray(v)
    XT = _c(x.reshape(NTOK, D).T)
    tsl = lambda i: slice(i * NTc, (i + 1) * NTc)
    p_ffn1 = _prog(("ffn", NTc, NPJ), lambda: build_ffn(NTc, NPJ=NPJ))
    p_ffn2 = _prog(("ffn", NTc, 0), lambda: build_ffn(NTc))
    p_out = _prog(("out", NTc), lambda: build_ffn(NTc, F=D_MODEL, mode="mix"))
    p_mix = _prog(("mix", L, B), lambda: build_mix(L, B))
    gconst = gdn_consts()
    s5tau = _c(np.broadcast_to(np.arange(512, dtype=np.float32), (128, 512)))
    z64 = np.zeros((64, 64), np.float32)
    for l in range(depth):
        wg, wu, wd, wi = _c(A(ffn1_w_gate[l])), _c(A(ffn1_w_up[l])), _c(A(ffn1_w_down[l])), _c(A(w_in[l]))
        gb = _gbpack(A(ln1_g[l]), A(ln1_b[l]))
        r = _run(p_ffn1, [dict(xT=_c(XT[:, tsl(i)]), wg=wg, wu=wu, wd=wd, gb=gb, win=wi) for i in range(NC)])
        X1T = np.concatenate([r[i]["yT"] for i in range(NC)], axis=1)
        PJ = np.concatenate([r[i]["pjT"] for i in range(NC)], axis=1)
        del r, wg, wu, wd, wi
        lre, lim, lst = A(s5_lambda_re[l]), A(s5_lambda_im[l]), A(s5_log_step[l])
        bre, bim, cre, cim = A(s5_b_re[l]), A(s5_b_im[l]), A(s5_c_re[l]), A(s5_c_im[l])
        gcwl = A(gdn_conv_w[l])
        lcw = A(lru_conv_w[l])
        ims = []
        for c in range(NC):
            r0, r1 = 4624 + c * 64, 5136 + c * 64
            im = {}
            im["lx"] = _c(np.concatenate([PJ[r0:r0 + 64, b * L:(b + 1) * L] for b in range(B)], axis=0))
            im["lg"] = _c(np.concatenate([PJ[r1:r1 + 64, b * L:(b + 1) * L] for b in range(B)], axis=0))
            cs = slice(c * 64, (c + 1) * 64)
            cols = [lcw[0, cs], lcw[1, cs], lcw[2, cs], lcw[3, cs], A(lru_conv_b[l])[cs], A(lru_lambda[l])[cs],
                    A(lru_b_a[l])[cs], A(lru_b_x[l])[cs]]
            im["lpar"] = _c(np.stack([np.tile(v, B) for v in cols], axis=1))
            wa, wx = A(lru_w_a[l][c]), A(lru_w_x[l][c])
            im["lwa"] = _c(np.block([[wa, z64], [z64, wa]]))
            im["lwx"] = _c(np.block([[wx, z64], [z64, wx]]))
            gs = slice(4 * c, 4 * c + 4)
            im["su"] = _c(PJ[c * 64:(c + 1) * 64, :])
            row = np.stack([lre[gs].reshape(-1), lim[gs].reshape(-1), np.repeat(lst[gs], 64)])
            im["s5row"] = _c(np.broadcast_to(row[None], (64, 3, 256)))
            col = np.zeros((128, 2, 3), np.float32)
            bT = np.zeros((64, 2, 256), np.float32)
            cT = np.zeros((128, 2, 2, 64), np.float32)
            for j in range(2):
                g2 = slice(4 * c + 2 * j, 4 * c + 2 * j + 2)
                col[:, j, 0] = lre[g2].reshape(-1)
                col[:, j, 1] = lim[g2].reshape(-1)
                col[:, j, 2] = np.repeat(lst[g2], 64)
            for g in range(4):
                gg = 4 * c + g
                bT[16 * g:16 * g + 16, 0, 64 * g:64 * g + 64] = bre[gg].T
                bT[16 * g:16 * g + 16, 1, 64 * g:64 * g + 64] = bim[gg].T
                j, rr = divmod(g, 2)
                cT[64 * rr:64 * rr + 64, j, 0, 16 * g:16 * g + 16] = cre[gg].T
                cT[64 * rr:64 * rr + 64, j, 1, 16 * g:16 * g + 16] = cim[gg].T
            im["s5col"], im["s5bT"], im["s5cT"] = col, bT, cT
            im["s5d"] = _c(A(s5_d[l])[c * 64:(c + 1) * 64].reshape(64, 1))
            im["s5tau"] = s5tau
            im["gq"] = _c(np.stack([PJ[512 + w * 1024 + c * 128:512 + w * 1024 + (c + 1) * 128, :] for w in range(3)], axis=1))
            im["gcw"] = _c(np.stack([gcwl[:, w * 1024 + c * 128:w * 1024 + (c + 1) * 128].T for w in range(3)], axis=1))
            im["gz"] = _c(PJ[3584 + c * 128:3584 + (c + 1) * 128, :].T.reshape(NT_, 128, 128))
            im["gab"] = _c(np.stack([PJ[4608 + c, :].reshape(NT_, 128).T, PJ[4616 + c, :].reshape(NT_, 128).T], axis=1))
            im["gpar"] = _c(np.tile(np.array([[A(gdn_a_log[l])[c], A(gdn_dt_bias[l])[c]]], np.float32), (128, 1)))
            im["gng"] = _c(np.tile(A(gdn_norm_g[l])[None, :], (128, 1)))
            im["gconst"] = gconst
            ims.append(im)
        r = _run(p_mix, ims)
        del ims, PJ
        YMT = np.empty((D_MODEL, NTOK), np.float32)
        for c in range(NC):
            YMT[c * 64:(c + 1) * 64, :] = r[c]["y5"]
            YMT[512 + c * 128:512 + (c + 1) * 128, :] = r[c]["gy"].reshape(NTOK, 128).T
            for b in range(B):
                YMT[1536 + c * 64:1536 + (c + 1) * 64, b * L:(b + 1) * L] = r[c]["ly"][b * 64:(b + 1) * 64, :]
        del r
        wgl, wo = _c(A(s5_w_glu[l])), _c(A(w_out[l]))
        bgl = _c(A(s5_b_glu[l]).reshape(-1, 128).T)
        gb = _gbpack(A(ln2_g[l]), A(ln2_b[l]))
        r = _run(p_out, [dict(xT=_c(X1T[:, tsl(i)]), ymT=_c(YMT[:, tsl(i)]), wglu=wgl, bglu=bgl, wd=wo, gb=gb) for i in range(NC)])
        X2T = np.concatenate([r[i]["yT"] for i in range(NC)], axis=1)
        del r, YMT, X1T
        wg, wu, wd = _c(A(ffn2_w_gate[l])), _c(A(ffn2_w_up[l])), _c(A(ffn2_w_down[l]))
        gb = _gbpack(A(ln3_g[l]), A(ln3_b[l]))
        r = _run(p_ffn2, [dict(xT=_c(X2T[:, tsl(i)]), wg=wg, wu=wu, wd=wd, gb=gb) for i in range(NC)])
        XT = np.concatenate([r[i]["yT"] for i in range(NC)], axis=1)
        del r, wg, wu, wd, X2T
    return np.ascontiguousarray(XT.T.reshape(B, L, D)).astype(np.float32)
```

```python
import numpy as np
import concourse.bass as bass
import concourse.mybir as mybir
from concourse.bass_utils import run_bass_kernel_spmd

F32 = mybir.dt.float32
BF16 = mybir.dt.bfloat16
AF = mybir.ActivationFunctionType
ALU = mybir.AluOpType

D_MODEL = 2048
D_FF = 5632
DEPTH = 4
N_CORES = 8
ALPHA = (2.0 * DEPTH) ** 0.25
LN_EPS = 1e-5


class Sched:
    def __init__(self, nc, n_dma_sems=24):
        self.nc = nc
        self.eng = {"pe": nc.tensor, "dve": nc.vector, "act": nc.scalar, "pool": nc.gpsimd, "sp": nc.sync}
        self.sem = {}
        self.cnt = {}
        self.seen = {k: {} for k in self.eng}
        self._cms = []
        for k in self.eng:
            cm = nc.semaphore("s_" + k)
            self.sem[k] = cm.__enter__()
            self._cms.append(cm)
            self.cnt[k] = 0
        self.dma_sems = []
        for i in range(n_dma_sems):
            cm = nc.semaphore("s_dma%d" % i)
            self.dma_sems.append([cm.__enter__(), 0])
            self._cms.append(cm)
        self.dma_rr = 0
        self.last_w = {}
        self.readers = {}
        self.semobj = {k: self.sem[k] for k in self.eng}
        for i, (s, _) in enumerate(self.dma_sems):
            self.semobj["dma%d" % i] = s

    def close(self):
        for cm in reversed(self._cms):
            cm.__exit__(None, None, None)

    def _wait(self, e, semkey, val):
        if self.seen[e].get(semkey, 0) >= val:
            return
        if semkey == e and val > self.cnt[e]:
            return
        self.eng[e].wait_ge(self.semobj[semkey], val)
        self.seen[e][semkey] = val

    def _deps(self, e, reads, writes):
        for r in reads:
            w = self.last_w.get(r)
            if w is not None:
                self._wait(e, *w)
        for r in writes:
            w = self.last_w.get(r)
            if w is not None:
                self._wait(e, *w)
            for sk, v in self.readers.get(r, {}).items():
                self._wait(e, sk, v)

    def _commit(self, semkey, val, reads, writes):
        for r in reads:
            self.readers.setdefault(r, {})[semkey] = val
        for r in writes:
            self.last_w[r] = (semkey, val)
            self.readers[r] = {}

    def op(self, e, fn, reads=(), writes=(), inc=True):
        self._deps(e, reads, writes)
        ins = fn(self.eng[e])
        if inc:
            self.cnt[e] += 1
            ins.then_inc(self.sem[e], 1)
            self._commit(e, self.cnt[e], reads, writes)
        else:
            self._commit(e, self.cnt[e] + 1, reads, writes)

    def dma(self, q, out, in_, reads=(), writes=()):
        i = self.dma_rr
        self.dma_rr = (self.dma_rr + 1) % len(self.dma_sems)
        sk = "dma%d" % i
        self._wait(q, sk, self.dma_sems[i][1])
        self._deps(q, reads, writes)
        ins = self.eng[q].dma_start(out=out, in_=in_)
        self.dma_sems[i][1] += 16
        ins.then_inc(self.dma_sems[i][0], 16)
        self._commit(sk, self.dma_sems[i][1], reads, writes)

    def finish(self, e="sp"):
        for r, (sk, v) in list(self.last_w.items()):
            self._wait(e, sk, v)

    def barrier(self):
        tot = {k: self.cnt[k] for k in self.eng}
        for i, (s, v) in enumerate(self.dma_sems):
            tot["dma%d" % i] = v
        for e in self.eng:
            for sk, v in tot.items():
                if v > 0:
                    self._wait(e, sk, v)
        self.last_w = {}
        self.readers = {}

    def tt(self, e, out, a, b, op, r, w):
        self.op(e, lambda g: g.tensor_tensor(out=out, in0=a, in1=b, op=op), reads=r, writes=w)

    def ts(self, e, out, a, s1, op0, r, w, s2=None, op1=None):
        if op1 is None:
            self.op(e, lambda g: g.tensor_scalar(out=out, in0=a, scalar1=s1, scalar2=None, op0=op0), reads=r, writes=w)
        else:
            self.op(e, lambda g: g.tensor_scalar(out=out, in0=a, scalar1=s1, scalar2=s2, op0=op0, op1=op1), reads=r, writes=w)

    def stt(self, out, a, s, b, op0, op1, r, w):
        self.op("dve", lambda g: g.scalar_tensor_tensor(out=out, in0=a, scalar=s, in1=b, op0=op0, op1=op1), reads=r, writes=w)

    def act(self, out, a, func, r, w, scale=None, bias=None):
        kw = {}
        if scale is not None:
            kw["scale"] = scale
        if bias is not None:
            kw["bias"] = bias
        self.op("act", lambda g: g.activation(out=out, in_=a, func=func, **kw), reads=r, writes=w)

    def mm(self, out, lhsT, rhs, r, w, start=True, stop=True, inc=None):
        self.op("pe", lambda g: g.matmul(out, lhsT=lhsT, rhs=rhs, start=start, stop=stop), reads=r, writes=w,
                inc=(stop if inc is None else inc))


def _alloc(nc, stack, kind, name, shape, dt):
    cm = (nc.sbuf_tensor if kind == "sb" else nc.psum_tensor)(name, shape, dt)
    t = cm.__enter__()
    stack.append(cm)
    return t


def build_ffn(NT, D=D_MODEL, F=D_FF, TB=None, mode="ffn", NPJ=0, GW=512):
    nc = bass.Bass("TRN2", target_bir_lowering=False)
    TB = TB or min(512, NT)
    QW = min(512, TB)
    rscale = 0.5 if mode == "ffn" else 1.0
    DC, FC = D // 128, F // 128
    FW = 256
    NFB = F // FW
    FPB = FW // 128
    xT = nc.dram_tensor("xT", [D, NT], F32, kind="ExternalInput").ap()
    if mode == "ffn":
        wg = nc.dram_tensor("wg", [NFB, 128, DC, FW], F32, kind="ExternalInput").ap()
        wu = nc.dram_tensor("wu", [NFB, 128, DC, FW], F32, kind="ExternalInput").ap()
    else:
        ymT = nc.dram_tensor("ymT", [F, NT], F32, kind="ExternalInput").ap()
        wglu = nc.dram_tensor("wglu", [GW, GW], F32, kind="ExternalInput").ap()
        bglu = nc.dram_tensor("bglu", [128, GW // 128], F32, kind="ExternalInput").ap()
        ymT_v = ymT.rearrange("(c p) t -> p c t", p=128)
        wglu_v = wglu.rearrange("(c p) f -> p c f", p=128)
    if NPJ:
        NPB = (NPJ + FW - 1) // FW
        win = nc.dram_tensor("win", [NPB, 128, DC, FW], F32, kind="ExternalInput").ap()
        pjT = nc.dram_tensor("pjT", [NPJ, NT], F32, kind="ExternalOutput").ap()
    WDG = [w for w in (11, 4, 2, 1) if FC % w == 0][0]
    NDG = FC // WDG
    wd = nc.dram_tensor("wd", [DC, NDG, 128, WDG, 128], F32, kind="ExternalInput").ap()
    gb = nc.dram_tensor("gb", [128, 2 * DC], F32, kind="ExternalInput").ap()
    yT = nc.dram_tensor("yT", [D, NT], F32, kind="ExternalOutput").ap()
    xT_v = xT.rearrange("(c p) t -> p c t", p=128)
    yT_v = yT.rearrange("(c p) t -> p c t", p=128)

    st = []
    S = Sched(nc)
    xb = _alloc(nc, st, "sb", "xb", [128, DC, TB], BF16)
    aT = _alloc(nc, st, "sb", "aT", [128, FC, TB], BF16)
    rT = _alloc(nc, st, "sb", "rT", [128, DC, TB], F32)
    wgs = [_alloc(nc, st, "sb", "wgs%d" % i, [128, DC, FW], BF16) for i in range(2)]
    wus = [_alloc(nc, st, "sb", "wus%d" % i, [128, DC, FW], BF16) for i in range(2)]
    wds = [_alloc(nc, st, "sb", "wds%d" % i, [128, WDG, 128], BF16) for i in range(3)]
    gbs = _alloc(nc, st, "sb", "gbs", [128, 2 * DC], F32)
    ones = _alloc(nc, st, "sb", "ones", [128, 128], F32)
    sg = [_alloc(nc, st, "sb", "sg%d" % i, [128, 512], F32) for i in range(2)]
    GC = GW // 128
    if mode == "mix":
        y5b = _alloc(nc, st, "sb", "y5b", [128, GC, TB], BF16)
        y5f = _alloc(nc, st, "sb", "y5f", [128, GC, TB], F32)
        wgl = _alloc(nc, st, "sb", "wgl", [128, GC, GW], BF16)
        bgl = _alloc(nc, st, "sb", "bgl", [128, GC], F32)
    if NPJ:
        ybf = _alloc(nc, st, "sb", "ybf", [128, DC, TB], BF16)
        po = [_alloc(nc, st, "sb", "po%d" % i, [128, TB], F32) for i in range(2)]
    xf = [_alloc(nc, st, "sb", "xf%d" % i, [128, TB], F32) for i in range(2)]
    sq = [_alloc(nc, st, "sb", "sq%d" % i, [128, TB], F32) for i in range(2)]
    mean = _alloc(nc, st, "sb", "mean", [128, TB], F32)
    rstd = _alloc(nc, st, "sb", "rstd", [128, TB], F32)
    yo = [_alloc(nc, st, "sb", "yo%d" % i, [128, TB], F32) for i in range(2)]
    pg = [_alloc(nc, st, "ps", "pg%d" % i, [128, 512], F32) for i in range(2)]
    pu = [_alloc(nc, st, "ps", "pu%d" % i, [128, 512], F32) for i in range(2)]
    pd = [_alloc(nc, st, "ps", "pd%d" % i, [128, 512], F32) for i in range(2)]
    pm = _alloc(nc, st, "ps", "pm", [128, 512], F32)
    pv = _alloc(nc, st, "ps", "pv", [128, 512], F32)
    NQ = TB // QW

    S.dma("sp", gbs[:], gb, writes=["gbs"])
    S.op("dve", lambda e: e.memset(ones[:], 1.0 / D), writes=["ones"])

    kgu = 0
    kd = 0
    kx = 0
    ko = 0
    for tb in range(NT // TB):
        t0 = tb * TB
        S.dma("pool", xb[:], xT_v[:, :, t0:t0 + TB], writes=["xb"])
        if mode == "mix":
            if tb == 0:
                S.dma("pool", wgl[:], wglu_v, writes=["wgl"])
                S.dma("sp", bgl[:], bglu, writes=["bgl"])
            S.dma("pool", y5b[:], ymT_v[:, 0:GC, t0:t0 + TB], writes=["y5b"])
            S.dma("sp", y5f[:], ymT_v[:, 0:GC, t0:t0 + TB], writes=["y5f"])
            for q in range(NQ):
                S.dma("pool", aT[:, GC:FC, q * QW:(q + 1) * QW], ymT_v[:, GC:FC, t0 + q * QW:t0 + (q + 1) * QW],
                      writes=[("aT", fc, q) for fc in range(GC, FC)])
            for jc in range(GC):
                for q in range(NQ):
                    pb = kgu % 2
                    kgu += 1
                    for ic in range(GC):
                        S.mm(pg[pb][:, 0:QW], wgl[:, ic, jc * 128:(jc + 1) * 128], y5b[:, ic, q * QW:(q + 1) * QW],
                             ["wgl", "y5b"], [("pg", pb)], start=(ic == 0), stop=(ic == GC - 1))
                    S.act(sg[pb][:, 0:QW], pg[pb][:, 0:QW], AF.Sigmoid, [("pg", pb), "bgl"], [("sg", pb)], bias=bgl[:, jc:jc + 1])
                    S.tt("dve", aT[:, jc, q * QW:(q + 1) * QW], sg[pb][:, 0:QW], y5f[:, jc, q * QW:(q + 1) * QW], ALU.mult,
                         [("sg", pb), "y5f"], [("aT", jc, q)])
        if mode == "ffn":
            for fb in range(NFB):
                b = fb % 2
                S.dma("pool", wgs[b][:], wg[fb], writes=[("wg", b)])
                S.dma("pool", wus[b][:], wu[fb], writes=[("wu", b)])
                for fi in range(FPB):
                    fc = fb * FPB + fi
                    for q in range(NQ):
                        pb = kgu % 2
                        kgu += 1
                        for c in range(DC):
                            S.op("pe", lambda e, c=c: e.matmul(pg[pb][:, 0:QW], lhsT=wgs[b][:, c, fi * 128:(fi + 1) * 128],
                                                             rhs=xb[:, c, q * QW:(q + 1) * QW], start=(c == 0), stop=(c == DC - 1)),
                                 reads=[("wg", b), "xb"], writes=[("pg", pb)], inc=(c == DC - 1))
                        for c in range(DC):
                            S.op("pe", lambda e, c=c: e.matmul(pu[pb][:, 0:QW], lhsT=wus[b][:, c, fi * 128:(fi + 1) * 128],
                                                             rhs=xb[:, c, q * QW:(q + 1) * QW], start=(c == 0), stop=(c == DC - 1)),
                                 reads=[("wu", b), "xb"], writes=[("pu", pb)], inc=(c == DC - 1))
                        S.op("act", lambda e: e.activation(out=sg[pb][:, 0:QW], in_=pg[pb][:, 0:QW], func=AF.Silu),
                             reads=[("pg", pb)], writes=[("sg", pb)])
                        S.op("dve", lambda e: e.tensor_tensor(out=aT[:, fc, q * QW:(q + 1) * QW], in0=sg[pb][:, 0:QW], in1=pu[pb][:, 0:QW], op=ALU.mult),
                             reads=[("sg", pb), ("pu", pb)], writes=[("aT", fc, q)])
        for dco in range(DC):
            xb_ = kx % 2
            kx += 1
            S.dma("sp", xf[xb_][:], xT_v[:, dco, t0:t0 + TB], writes=[("xf", xb_)])
            for q in range(NQ):
                pb = kd % 2
                for g in range(NDG):
                    wb = kd % 3 if False else (kd * NDG + g) % 3
                    S.dma("pool", wds[wb][:], wd[dco, g], writes=[("wd", wb)])
                    for j in range(WDG):
                        fc = g * WDG + j
                        S.op("pe", lambda e, fc=fc, j=j: e.matmul(pd[pb][:, 0:QW], lhsT=wds[wb][:, j, :], rhs=aT[:, fc, q * QW:(q + 1) * QW],
                                                                 start=(fc == 0), stop=(fc == FC - 1)),
                             reads=[("wd", wb), ("aT", fc, q)], writes=[("pd", pb)], inc=(j == WDG - 1))
                kd += 1
                S.op("act", lambda e: e.activation(out=rT[:, dco, q * QW:(q + 1) * QW], in_=pd[pb][:, 0:QW], func=AF.Copy, scale=rscale),
                     reads=[("pd", pb)], writes=[("rT", dco, q)])
                S.op("dve", lambda e: e.scalar_tensor_tensor(out=rT[:, dco, q * QW:(q + 1) * QW], in0=xf[xb_][:, q * QW:(q + 1) * QW],
                                                           scalar=ALPHA, in1=rT[:, dco, q * QW:(q + 1) * QW], op0=ALU.mult, op1=ALU.add),
                     reads=[("xf", xb_), ("rT", dco, q)], writes=[("rT", dco, q)])
        for q in range(NQ):
            qs = slice(q * QW, (q + 1) * QW)
            for c in range(DC):
                S.op("pe", lambda e, c=c: e.matmul(pm[:, 0:QW], lhsT=ones[:], rhs=rT[:, c, qs], start=(c == 0), stop=(c == DC - 1)),
                     reads=["ones", ("rT", c, q)], writes=["pm"], inc=(c == DC - 1))
            S.op("act", lambda e: e.activation(out=mean[:, qs], in_=pm[:, 0:QW], func=AF.Copy), reads=["pm"], writes=[("mean", q)])
            for c in range(DC):
                S.op("dve", lambda e, c=c: e.tensor_tensor(out=rT[:, c, qs], in0=rT[:, c, qs], in1=mean[:, qs], op=ALU.subtract),
                     reads=[("rT", c, q), ("mean", q)], writes=[("rT", c, q)])
                sb_ = c % 2
                S.op("act", lambda e, c=c: e.activation(out=sq[sb_][:, qs], in_=rT[:, c, qs], func=AF.Square),
                     reads=[("rT", c, q)], writes=[("sq", sb_, q)])
                S.op("pe", lambda e, c=c: e.matmul(pv[:, 0:QW], lhsT=ones[:], rhs=sq[sb_][:, qs], start=(c == 0), stop=(c == DC - 1)),
                     reads=["ones", ("sq", sb_, q)], writes=["pv"])
            S.op("act", lambda e: e.activation(out=rstd[:, qs], in_=pv[:, 0:QW], func=AF.Sqrt, bias=LN_EPS, scale=1.0), reads=["pv"], writes=[("rstd", q)])
            S.op("dve", lambda e: e.reciprocal(out=rstd[:, qs], in_=rstd[:, qs]), reads=[("rstd", q)], writes=[("rstd", q)])
        for c in range(DC):
            ob = ko % 2
            ko += 1
            for q in range(NQ):
                qs = slice(q * QW, (q + 1) * QW)
                S.op("dve", lambda e: e.tensor_tensor(out=yo[ob][:, qs], in0=rT[:, c, qs], in1=rstd[:, qs], op=ALU.mult),
                     reads=[("rT", c, q), ("rstd", q)], writes=[("yo", ob)])
            S.op("pool", lambda e: e.tensor_scalar(out=yo[ob][:], in0=yo[ob][:], scalar1=gbs[:, c:c + 1], scalar2=gbs[:, DC + c:DC + c + 1],
                                                   op0=ALU.mult, op1=ALU.add),
                 reads=[("yo", ob), "gbs"], writes=[("yo", ob)])
            S.dma("sp", yT_v[:, c, t0:t0 + TB], yo[ob][:], reads=[("yo", ob)], writes=[("yT", c, tb)])
            if NPJ:
                S.act(ybf[:, c, :], yo[ob][:], AF.Copy, [("yo", ob)], [("ybf", c)])
        if NPJ:
            kp = 0
            for pb0 in range(0, NPJ, FW):
                pw = min(FW, NPJ - pb0)
                b = (pb0 // FW) % 2
                S.dma("pool", wgs[b][:], win[pb0 // FW], writes=[("wg", b)])
                for pc in range(0, pw, 128):
                    m = min(128, pw - pc)
                    for q in range(NQ):
                        pb = kp % 2
                        kp += 1
                        for c in range(DC):
                            S.mm(pg[pb][0:m, 0:QW], wgs[b][:, c, pc:pc + m], ybf[:, c, q * QW:(q + 1) * QW],
                                 [("wg", b)] + ([("ybf", cc) for cc in range(DC)] if c == 0 else []), [("pg", pb)], start=(c == 0), stop=(c == DC - 1))
                        S.act(po[pb][0:m, q * QW:(q + 1) * QW], pg[pb][0:m, 0:QW], AF.Copy, [("pg", pb)], [("po", pb)])
                        S.dma("sp", pjT[pb0 + pc:pb0 + pc + m, t0 + q * QW:t0 + (q + 1) * QW], po[pb][0:m, q * QW:(q + 1) * QW],
                              reads=[("po", pb)], writes=[("pjT", pb0 + pc, tb, q)])
    S.finish("sp")
    S.close()
    for cm in reversed(st):
        cm.__exit__(None, None, None)
    return nc


class Pool_:
    def __init__(self, nc, st, prefix, shape, n, dt=F32):
        self.tiles = [_alloc(nc, st, "sb", "%s%d" % (prefix, i), shape, dt) for i in range(n)]
        self.names = ["%s%d" % (prefix, i) for i in range(n)]
        self.k = 0

    def get(self):
        i = self.k % len(self.tiles)
        self.k += 1
        return self.tiles[i], self.names[i]


def emit_softplus(S, P, x, xn, out, outn, sl):
    ax, axn = P.get()
    S.act(sl(ax), x, AF.Abs, [xn], [axn])
    y, yn = P.get()
    S.act(sl(y), sl(ax), AF.Exp, [axn], [yn], scale=-1.0)
    t, tn = P.get()
    S.ts("dve", sl(t), sl(y), 2.0, ALU.add, [yn], [tn])
    S.op("dve", lambda g: g.reciprocal(out=sl(t), in_=sl(t)), reads=[tn], writes=[tn])
    s, sn = P.get()
    S.tt("dve", sl(s), sl(y), sl(t), ALU.mult, [yn, tn], [sn])
    s2, s2n = P.get()
    S.tt("dve", sl(s2), sl(s), sl(s), ALU.mult, [sn], [s2n])
    p, pn = P.get()
    S.ts("dve", sl(p), sl(s2), 1.0 / 11.0, ALU.mult, [s2n], [pn], s2=1.0 / 9.0, op1=ALU.add)
    for cst in (1.0 / 7.0, 1.0 / 5.0, 1.0 / 3.0, 1.0):
        S.tt("dve", sl(p), sl(p), sl(s2), ALU.mult, [pn, s2n], [pn])
        S.ts("dve", sl(p), sl(p), cst, ALU.add, [pn], [pn])
    S.tt("dve", sl(p), sl(p), sl(s), ALU.mult, [pn, sn], [pn])
    S.ts("dve", sl(ax), x, 0.0, ALU.max, [xn], [axn])
    S.stt(out, sl(p), 2.0, sl(ax), ALU.mult, ALU.add, [pn, axn], [outn])


def emit_gelu(S, out, outn, x, xn, tmp, tmpn, eng2="pool"):
    S.act(tmp, x, AF.Square, [xn], [tmpn])
    S.ts("dve", tmp, tmp, 0.044715, ALU.mult, [tmpn], [tmpn], s2=1.0, op1=ALU.add)
    S.tt("dve", tmp, tmp, x, ALU.mult, [tmpn, xn], [tmpn])
    S.act(tmp, tmp, AF.Sigmoid, [tmpn], [tmpn], scale=1.5957691216057308)
    S.tt(eng2, out, tmp, x, ALU.mult, [tmpn, xn], [outn])


def build_mix(L, NB=2, do_lru=True, do_s5=True, do_gdn=True):
    nc = bass.Bass("TRN2", target_bir_lowering=False)
    NTOK = NB * L
    S = Sched(nc)
    dr = {}

    def din(name, shape):
        dr[name] = nc.dram_tensor(name, shape, F32, kind="ExternalInput").ap()
        return dr[name]

    def dout(name, shape):
        dr[name] = nc.dram_tensor(name, shape, F32, kind="ExternalOutput").ap()
        return dr[name]

    if do_lru:
        emit_lru(nc, S, L, NB, din, dout)
        S.barrier()
    if do_s5:
        emit_s5(nc, S, L, NB, din, dout)
        S.barrier()
    if do_gdn:
        emit_gdn(nc, S, L, NB, din, dout)
    S.finish("sp")
    S.close()
    return nc


def emit_lru(nc, S, L, NB, din, dout):
    assert NB == 2
    TS = min(L, 2048)
    NSEG = L // TS
    lx = din("lx", [128, L])
    lg = din("lg", [128, L])
    lpar = din("lpar", [128, 8])
    lwa = din("lwa", [128, 128])
    lwx = din("lwx", [128, 128])
    ly = dout("ly", [128, L])
    st = []
    par = _alloc(nc, st, "sb", "l_par", [128, 8], F32)
    wa = _alloc(nc, st, "sb", "l_wa", [128, 128], BF16)
    wx = _alloc(nc, st, "sb", "l_wx", [128, 128], BF16)
    c12 = _alloc(nc, st, "sb", "l_c12", [128, 2], F32)
    carry = _alloc(nc, st, "sb", "l_carry", [128, 1], F32)
    tiny = Pool_(nc, st, "l_tiny", [128, 1], 8)
    big = Pool_(nc, st, "l_big", [128, TS], 9)
    xin = [_alloc(nc, st, "sb", "l_xin%d" % i, [128, TS + 3], F32) for i in range(2)]
    xcb = _alloc(nc, st, "sb", "l_xcb", [128, TS], BF16)
    pr = [_alloc(nc, st, "ps", "l_pr%d" % i, [128, 512], F32) for i in range(2)]
    pi = [_alloc(nc, st, "ps", "l_pi%d" % i, [128, 512], F32) for i in range(2)]

    S.dma("sp", par[:], lpar, writes=["l_par"])
    S.dma("pool", wa[:], lwa, writes=["l_wa"])
    S.dma("pool", wx[:], lwx, writes=["l_wx"])
    nl, nln = tiny.get()
    S.ts("dve", nl[:], par[:, 5:6], -1.0, ALU.mult, ["l_par"], [nln])
    sp_, spn = tiny.get()
    emit_softplus(S, tiny, nl[:], nln, sp_[:], spn, lambda t: t[:])
    S.ts("dve", c12[:, 0:1], sp_[:], -8.0, ALU.mult, [spn], ["l_c12"])
    S.ts("dve", c12[:, 1:2], sp_[:], -16.0, ALU.mult, [spn, "l_c12"], ["l_c12"])
    S.op("dve", lambda g: g.memset(carry[:], 0.0), writes=["l_carry"])

    for s in range(NSEG):
        xi = xin[s % 2]
        xn = "l_xin%d" % (s % 2)
        if s == 0:
            S.op("dve", lambda g: g.memset(xi[:, 0:3], 0.0), writes=[xn])
            S.dma("sp", xi[:, 3:3 + TS], lx[:, 0:TS], reads=[xn], writes=[xn])
        else:
            S.dma("sp", xi[:, :], lx[:, s * TS - 3:(s + 1) * TS], writes=[xn])
        xc, xcn = big.get()
        S.ts("dve", xc[:], xi[:, 3:3 + TS], par[:, 3:4], ALU.mult, [xn, "l_par"], [xcn], s2=par[:, 4:5], op1=ALU.add)
        for k in (2, 1, 0):
            S.stt(xc[:], xi[:, k:k + TS], par[:, k:k + 1], xc[:], ALU.mult, ALU.add, [xn, "l_par", xcn], [xcn])
        S.act(xcb[:], xc[:], AF.Copy, [xcn], ["l_xcb"])
        r, rn = big.get()
        ig, ign = big.get()
        for q in range(TS // 512):
            qs = slice(q * 512, (q + 1) * 512)
            b = q % 2
            S.mm(pr[b][:], wa[:], xcb[:, qs], ["l_wa", "l_xcb"], [("l_pr", b)])
            S.mm(pi[b][:], wx[:], xcb[:, qs], ["l_wx", "l_xcb"], [("l_pi", b)])
            S.act(r[:, qs], pr[b][:], AF.Sigmoid, [("l_pr", b), "l_par"], [rn], bias=par[:, 6:7])
            S.act(ig[:, qs], pi[b][:], AF.Sigmoid, [("l_pi", b), "l_par"], [ign], bias=par[:, 7:8])
        a, an = big.get()
        S.act(a[:], r[:], AF.Exp, [rn, "l_c12"], [an], scale=c12[:, 0:1])
        s2, s2n = big.get()
        S.act(s2[:], r[:], AF.Exp, [rn, "l_c12"], [s2n], scale=c12[:, 1:2])
        S.act(s2[:], s2[:], AF.Sqrt, [s2n], [s2n], scale=-1.0, bias=1.0)
        S.tt("dve", s2[:], s2[:], ig[:], ALU.mult, [s2n, ign], [s2n])
        S.tt("dve", s2[:], s2[:], xc[:], ALU.mult, [s2n, xcn], [s2n])
        h, hn = big.get()
        S.op("dve", lambda g: g.tensor_tensor_scan(out=h[:], data0=a[:], data1=s2[:], initial=carry[:, 0:1], op0=ALU.mult, op1=ALU.add),
             reads=[an, s2n, "l_carry"], writes=[hn])
        S.op("dve", lambda g: g.tensor_copy(out=carry[:], in_=h[:, TS - 1:TS]), reads=[hn], writes=["l_carry"])
        gt, gtn = big.get()
        S.dma("sp", gt[:], lg[:, s * TS:(s + 1) * TS], writes=[gtn])
        tmp, tmpn = big.get()
        ge, gen = big.get()
        emit_gelu(S, ge[:], gen, gt[:], gtn, tmp[:], tmpn)
        S.tt("dve", ge[:], ge[:], h[:], ALU.mult, [gen, hn], [gen])
        S.dma("sp", ly[:, s * TS:(s + 1) * TS], ge[:], reads=[gen], writes=[("ly", s)])
    S.barrier()
    for cm in reversed(st):
        cm.__exit__(None, None, None)


class Ex:
    def __init__(self, S, pool, ipool, sl):
        self.S, self.pool, self.ipool, self.sl = S, pool, ipool, sl

    def new(self):
        t, n = self.pool.get()
        return self.sl(t), n

    def bin(self, a, b, op, eng="dve"):
        o = self.new()
        self.S.tt(eng, o[0], a[0], b[0], op, [a[1], b[1]], [o[1]])
        return o

    def mul(self, a, b): return self.bin(a, b, ALU.mult)
    def add(self, a, b): return self.bin(a, b, ALU.add)
    def sub(self, a, b): return self.bin(a, b, ALU.subtract)

    def sc(self, a, c1, op0, c2=None, op1=None):
        o = self.new()
        self.S.ts("dve", o[0], a[0], c1, op0, [a[1]], [o[1]], s2=c2, op1=op1)
        return o

    def act(self, a, func, scale=None, bias=None):
        o = self.new()
        self.S.act(o[0], a[0], func, [a[1]], [o[1]], scale=scale, bias=bias)
        return o

    def recip(self, a):
        o = self.new()
        self.S.op("dve", lambda g: g.reciprocal(out=o[0], in_=a[0]), reads=[a[1]], writes=[o[1]])
        return o

    def frac(self, k):
        it, itn = self.ipool.get()
        it = self.sl(it)
        self.S.op("dve", lambda g: g.tensor_copy(out=it, in_=k[0]), reads=[k[1]], writes=[itn])
        kf = self.new()
        self.S.op("dve", lambda g: g.tensor_copy(out=kf[0], in_=it), reads=[itn], writes=[kf[1]])
        f = self.sub(k, kf)
        m = self.sc(f, 0.5, ALU.is_gt)
        f = self.sub(f, m)
        m = self.sc(f, -0.5, ALU.is_lt)
        f = self.add(f, m)
        return self.sc(f, 0.4999999, ALU.min, -0.4999999, ALU.max)

    def sincos(self, th):
        k = self.sc(th, 1.0 / (2.0 * np.pi), ALU.mult)
        return self.sincos_k(k)

    def sincos_k(self, k):
        s = self.act(self.frac(k), AF.Sin, scale=2.0 * np.pi)
        c = self.act(self.frac(self.sc(k, 0.25, ALU.add)), AF.Sin, scale=2.0 * np.pi)
        return s, c

    def s5_params(self, lre_in, lim, lstep):
        lre = self.sc(lre_in, -1e-4, ALU.min)
        dt = self.act(lstep, AF.Exp)
        rmag = self.act(self.mul(lre, dt), AF.Exp)
        sn, cs = self.sincos(self.mul(lim, dt))
        ar = self.mul(rmag, cs)
        ai = self.mul(rmag, sn)
        am1 = self.sc(ar, -1.0, ALU.add)
        num_r = self.add(self.mul(am1, lre), self.mul(ai, lim))
        num_i = self.sub(self.mul(ai, lre), self.mul(am1, lim))
        den = self.add(self.mul(lre, lre), self.mul(lim, lim))
        rden = self.recip(den)
        return rmag, cs, sn, self.mul(num_r, rden), self.mul(num_i, rden)


def emit_s5(nc, S, L, NB, din, dout):
    NTOK = NB * L
    T = 512
    NBLK = L // T
    su = din("su", [64, NTOK])
    s5row = din("s5row", [64, 3, 256])
    s5col = din("s5col", [128, 2, 3])
    s5bT = din("s5bT", [64, 2, 256])
    s5cT = din("s5cT", [128, 2, 2, 64])
    s5d = din("s5d", [64, 1])
    y5 = dout("y5", [64, NTOK])
    I32 = mybir.dt.int32
    st = []
    rowp = _alloc(nc, st, "sb", "s_rowp", [64, 3, 256], F32)
    colp = _alloc(nc, st, "sb", "s_colp", [128, 2, 3], F32)
    bT = _alloc(nc, st, "sb", "s_bT", [64, 2, 256], F32)
    cT = _alloc(nc, st, "sb", "s_cT", [128, 2, 2, 64], F32)
    sd = _alloc(nc, st, "sb", "s_sd", [64, 1], F32)
    BTr = _alloc(nc, st, "sb", "s_BTr", [64, 256], BF16)
    BTi = _alloc(nc, st, "sb", "s_BTi", [64, 256], BF16)
    cTr = _alloc(nc, st, "sb", "s_cTr", [128, 2, 64], BF16)
    cTi = _alloc(nc, st, "sb", "s_cTi", [128, 2, 64], BF16)
    rpool = Pool_(nc, st, "s_rp", [64, 256], 40)
    ripool = Pool_(nc, st, "s_rpi", [64, 256], 2, I32)
    cpool = Pool_(nc, st, "s_cp", [128, 2], 40)
    cipool = Pool_(nc, st, "s_cpi", [128, 2], 2, I32)
    Er = [_alloc(nc, st, "sb", "s_Er%d" % j, [128, T], F32) for j in range(2)]
    Ei = [_alloc(nc, st, "sb", "s_Ei%d" % j, [128, T], F32) for j in range(2)]
    Rt = [_alloc(nc, st, "sb", "s_Rt%d" % j, [128, T], F32) for j in range(2)]
    ttmp = _alloc(nc, st, "sb", "s_ttmp", [128, T], F32)
    wpow = [_alloc(nc, st, "sb", "s_wpow%d" % i, [128, 2, 2], F32) for i in range(2)]
    W512 = _alloc(nc, st, "sb", "s_W512", [128, 2, 2], F32)
    car = _alloc(nc, st, "sb", "s_car", [128, 2, 2 * NB], F32)
    ctmp = Pool_(nc, st, "s_ct", [128, 1], 6)
    uf = [_alloc(nc, st, "sb", "s_uf%d" % i, [64, T], F32) for i in range(2)]
    ub = [_alloc(nc, st, "sb", "s_ub%d" % i, [64, T], BF16) for i in range(2)]
    wk = Pool_(nc, st, "s_wk", [128, T], 12)
    hb = [[_alloc(nc, st, "sb", "s_hb%d%d" % (j, k), [128, T], BF16) for k in range(2)] for j in range(2)]
    yo = Pool_(nc, st, "s_yo", [64, T], 4)
    pbr = [_alloc(nc, st, "ps", "s_pbr%d" % j, [128, 512], F32) for j in range(2)]
    pbi = [_alloc(nc, st, "ps", "s_pbi%d" % j, [128, 512], F32) for j in range(2)]
    py = [_alloc(nc, st, "ps", "s_py%d" % j, [64, 512], F32) for j in range(2)]

    S.dma("sp", rowp[:], s5row, writes=["s_rowp"])
    S.dma("sp", colp[:], s5col, writes=["s_colp"])
    S.dma("sp", bT[:], s5bT, writes=["s_bT"])
    S.dma("sp", cT[:], s5cT, writes=["s_cT"])
    S.dma("sp", sd[:], s5d, writes=["s_sd"])
    ex = Ex(S, rpool, ripool, lambda t: t[:])
    _, _, _, kr, ki = ex.s5_params((rowp[:, 0, :], "s_rowp"), (rowp[:, 1, :], "s_rowp"), (rowp[:, 2, :], "s_rowp"))
    br, bi = (bT[:, 0, :], "s_bT"), (bT[:, 1, :], "s_bT")
    t1 = ex.sub(ex.mul(kr, br), ex.mul(ki, bi))
    t2 = ex.add(ex.mul(kr, bi), ex.mul(ki, br))
    S.act(BTr[:], t1[0], AF.Copy, [t1[1]], ["s_BTr"])
    S.act(BTi[:], t2[0], AF.Copy, [t2[1]], ["s_BTi"])
    S.act(cTr[:], cT[:, :, 0, :], AF.Copy, ["s_cT"], ["s_cTr"])
    S.act(cTi[:], cT[:, :, 1, :], AF.Copy, ["s_cT"], ["s_cTi"], scale=-1.0)
    ex = Ex(S, cpool, cipool, lambda t: t[:])
    rmag, cs, sn, _, _ = ex.s5_params((colp[:, :, 0], "s_colp"), (colp[:, :, 1], "s_colp"), (colp[:, :, 2], "s_colp"))
    k0 = _alloc(nc, st, "sb", "s_k0", [128, 2], F32)
    rmg = _alloc(nc, st, "sb", "s_rmg", [128, 2], F32)
    S.op("dve", lambda g: g.tensor_copy(out=rmg[:], in_=rmag[0]), reads=[rmag[1]], writes=["s_rmg"])
    dtc = ex.act((colp[:, :, 2], "s_colp"), AF.Exp)
    kk = ex.sc(ex.mul((colp[:, :, 1], "s_colp"), dtc), 1.0 / (2.0 * np.pi), ALU.mult)
    kf = ex.frac(kk)
    S.op("dve", lambda g: g.tensor_copy(out=k0[:], in_=kf[0]), reads=[kf[1]], writes=["s_k0"])
    sT, cT_ = ex.sincos_k(ex.sc((k0[:], "s_k0"), float(T), ALU.mult))
    S.op("dve", lambda g: g.tensor_copy(out=W512[:, 0, :], in_=cT_[0]), reads=[cT_[1]], writes=["s_W512"])
    S.op("dve", lambda g: g.tensor_copy(out=W512[:, 1, :], in_=sT[0]), reads=[sT[1], "s_W512"], writes=["s_W512"])
    tau = _alloc(nc, st, "sb", "s_tau", [128, T], F32)
    S.dma("sp", tau[:], din("s5tau", [128, T]), writes=["s_tau"])
    tpool = Pool_(nc, st, "s_tp", [128, T], 10)
    tipool = Pool_(nc, st, "s_tpi", [128, T], 2, I32)
    ext = Ex(S, tpool, tipool, lambda t: t[:])
    for j in range(2):
        ang = ext.new()
        S.ts("dve", ang[0], tau[:], k0[:, j:j + 1], ALU.mult, ["s_tau", "s_k0"], [ang[1]])
        sj, cj = ext.sincos_k(ang)
        S.op("dve", lambda g: g.tensor_copy(out=Er[j][:], in_=cj[0]), reads=[cj[1]], writes=["s_Er%d" % j])
        S.ts("dve", Ei[j][:], sj[0], -1.0, ALU.mult, [sj[1]], ["s_Ei%d" % j])
        S.ts("dve", Rt[j][:], tau[:], 0.0, ALU.mult, ["s_tau", "s_rmg"], ["s_Rt%d" % j], s2=rmg[:, j:j + 1], op1=ALU.add)
    S.op("dve", lambda g: g.memset(car[:], 0.0), writes=["s_car"])
    import os
    if os.environ.get("S5DBG"):
        d_er = dout("d_er", [128, T]); d_ei = dout("d_ei", [128, T]); d_rt = dout("d_rt", [128, T])
        d_col = dout("d_col", [128, 6]); d_bt = dout("d_bt", [64, 512])
        S.dma("sp", d_er, Er[0][:], reads=["s_Er0"], writes=["d_er"])
        S.dma("sp", d_ei, Ei[0][:], reads=["s_Ei0"], writes=["d_ei"])
        S.dma("sp", d_rt, Rt[0][:], reads=["s_Rt0"], writes=["d_rt"])
        S.dma("sp", d_col[:, 0:2], rmag[0], reads=[rmag[1]], writes=["d_col"])
        S.dma("sp", d_col[:, 2:4], cs[0], reads=[cs[1]], writes=["d_col2"])
        S.dma("sp", d_col[:, 4:6], sn[0], reads=[sn[1]], writes=["d_col3"])
        S.dma("sp", d_bt[:, 0:256], t1[0], reads=[t1[1]], writes=["d_bt"])
        S.dma("sp", d_bt[:, 256:512], t2[0], reads=[t2[1]], writes=["d_bt2"])

    k = 0
    for blk in range(NBLK):
        for b in range(NB):
            t0 = b * L + blk * T
            ui = k % 2
            k += 1
            ufn, ubn = "s_uf%d" % ui, "s_ub%d" % ui
            S.dma("sp", uf[ui][:], su[:, t0:t0 + T], writes=[ufn])
            S.act(ub[ui][:], uf[ui][:], AF.Copy, [ufn], [ubn])
            for j in range(2):
                js = slice(j * 128, (j + 1) * 128)
                S.mm(pbr[j][:], BTr[:, js], ub[ui][:], ["s_BTr", ubn], [("s_pbr", j)])
                S.mm(pbi[j][:], BTi[:, js], ub[ui][:], ["s_BTi", ubn], [("s_pbi", j)])
                ern, ein, rtn = "s_Er%d" % j, "s_Ei%d" % j, "s_Rt%d" % j
                a1, a1n = wk.get(); a2, a2n = wk.get(); a3, a3n = wk.get(); a4, a4n = wk.get()
                S.tt("dve", a1[:], pbr[j][:], Er[j][:], ALU.mult, [("s_pbr", j), ern], [a1n])
                S.tt("dve", a2[:], pbi[j][:], Ei[j][:], ALU.mult, [("s_pbi", j), ein], [a2n])
                S.tt("dve", a3[:], pbr[j][:], Ei[j][:], ALU.mult, [("s_pbr", j), ein], [a3n])
                S.tt("dve", a4[:], pbi[j][:], Er[j][:], ALU.mult, [("s_pbi", j), ern], [a4n])
                S.tt("pool", a1[:], a1[:], a2[:], ALU.subtract, [a1n, a2n], [a1n])
                S.tt("pool", a3[:], a3[:], a4[:], ALU.add, [a3n, a4n], [a3n])
                ci = j * NB + b
                Gr, Grn = wk.get(); Gi, Gin = wk.get()
                S.op("dve", lambda g: g.tensor_tensor_scan(out=Gr[:], data0=Rt[j][:], data1=a1[:], initial=car[:, 0, ci:ci + 1], op0=ALU.mult, op1=ALU.add),
                     reads=[rtn, a1n, "s_car"], writes=[Grn])
                S.op("dve", lambda g: g.tensor_tensor_scan(out=Gi[:], data0=Rt[j][:], data1=a3[:], initial=car[:, 1, ci:ci + 1], op0=ALU.mult, op1=ALU.add),
                     reads=[rtn, a3n, "s_car"], writes=[Gin])
                c1, c1n = ctmp.get(); c2, c2n = ctmp.get()
                Wr, Wi = W512[:, 0, j:j + 1], W512[:, 1, j:j + 1]
                S.ts("dve", c1[:], Gi[:, T - 1:T], Wi, ALU.mult, [Gin, "s_W512"], [c1n])
                S.ts("dve", c2[:], Gr[:, T - 1:T], Wi, ALU.mult, [Grn, "s_W512"], [c2n])
                S.stt(car[:, 0, ci:ci + 1], Gr[:, T - 1:T], Wr, c1[:], ALU.mult, ALU.subtract, [Grn, "s_W512", c1n, "s_car"], ["s_car"])
                S.stt(car[:, 1, ci:ci + 1], Gi[:, T - 1:T], Wr, c2[:], ALU.mult, ALU.add, [Gin, "s_W512", c2n, "s_car"], ["s_car"])
                S.tt("dve", a1[:], Gr[:], Er[j][:], ALU.mult, [Grn, ern], [a1n])
                S.tt("dve", a2[:], Gi[:], Ei[j][:], ALU.mult, [Gin, ein], [a2n])
                S.tt("dve", a3[:], Gi[:], Er[j][:], ALU.mult, [Gin, ern], [a3n])
                S.tt("dve", a4[:], Gr[:], Ei[j][:], ALU.mult, [Grn, ein], [a4n])
                S.tt("pool", hb[j][0][:], a1[:], a2[:], ALU.add, [a1n, a2n], [("s_hb", j, 0)])
                S.tt("pool", hb[j][1][:], a3[:], a4[:], ALU.subtract, [a3n, a4n], [("s_hb", j, 1)])
            pyi = k % 2
            n_ = 0
            for j in range(2):
                for c, ct in ((0, cTr), (1, cTi)):
                    S.mm(py[pyi][:], ct[:, j, :], hb[j][c][:], ["s_cTr", "s_cTi", ("s_hb", j, c)], [("s_py", pyi)], start=(n_ == 0), stop=(n_ == 3))
                    n_ += 1
            yv, yvn = yo.get(); tmp, tmpn = yo.get()
            S.stt(yv[:], uf[ui][:], sd[:, 0:1], py[pyi][:], ALU.mult, ALU.add, [ufn, "s_sd", ("s_py", pyi)], [yvn])
            emit_gelu(S, yv[:], yvn, yv[:], yvn, tmp[:], tmpn, eng2="dve")
            S.dma("sp", y5[:, t0:t0 + T], yv[:], reads=[yvn], writes=[("y5", t0)])
    S.barrier()
    for cm in reversed(st):
        cm.__exit__(None, None, None)


def emit_gdn(nc, S, L, NB, din, dout):
    NTOK = NB * L
    NT_ = NTOK // 128
    TSG = min(L, 1024)
    NSEG = L // TSG
    TPS = TSG // 128
    gq = din("gq", [128, 3, NTOK])
    gcw = din("gcw", [128, 3, 4])
    gz = din("gz", [NT_, 128, 128])
    gab = din("gab", [128, 2, NT_])
    gpar = din("gpar", [128, 2])
    gng = din("gng", [128, 128])
    gconst = din("gconst", [128, 8, 128])
    gy = dout("gy", [NT_, 128, 128])
    st = []
    A_ = lambda name, shape, dt=F32: _alloc(nc, st, "sb", name, shape, dt)
    cst = A_("g_cst", [128, 8, 128])
    ident, Ubd, Cbd, sel0, sel1, mSU, mIU, ones = [cst[:, i, :] for i in range(8)]
    cw = A_("g_cw", [128, 3, 4])
    ab = A_("g_ab", [128, 2, NT_])
    par = A_("g_par", [128, 2])
    ngb = A_("g_ngb", [128, 128])
    S.dma("sp", cst[:], gconst, writes=["g_cst"])
    S.dma("sp", cw[:], gcw, writes=["g_cw"])
    S.dma("sp", ab[:], gab, writes=["g_ab"])
    S.dma("sp", par[:], gpar, writes=["g_par"])
    S.dma("sp", ngb[:], gng, writes=["g_ngb"])
    pbig = _alloc(nc, st, "ps", "g_pbig", [128, 512], F32)
    pbanks = [_alloc(nc, st, "ps", "g_pb%d" % i, [128, 4, 128], F32) for i in range(5)]
    pslot = [pbanks[i // 4][:, i % 4, :] for i in range(20)]
    (ptk, ptv, pKK, pQK, pR1, pR2, pM, pP, pQ, pR, pwT, pu) = pslot[:12]
    rslots = [pslot[12:16], pslot[16:20]]
    B0, B1, B2 = "g_pb0", "g_pb1", "g_pb2"

    gp = Pool_(nc, st, "g_gp", [128, NT_], 12)
    sl = lambda t: t[:]
    beta = A_("g_beta", [128, NT_]); gc = A_("g_gc", [128, NT_]); kdsc = A_("g_kdsc", [128, NT_])
    egc = A_("g_egc", [128, NT_]); gcb = A_("g_gcb", [128, NT_]); begc = A_("g_begc", [128, NT_])
    gl = [A_("g_gl%d" % c, [128, NT_]) for c in range(2)]
    nea = A_("g_nea", [128, 1])
    S.act(beta[:], ab[:, 1, :], AF.Sigmoid, ["g_ab"], ["g_beta"])
    x_, xn_ = gp.get()
    S.ts("dve", x_[:], ab[:, 0, :], par[:, 1:2], ALU.add, ["g_ab", "g_par"], [xn_])
    sp_, spn_ = gp.get()
    emit_softplus(S, gp, x_[:], xn_, sp_[:], spn_, sl)
    S.act(nea[:], par[:, 0:1], AF.Exp, ["g_par"], ["g_nea"])
    NTP = max(NT_, 128)
    gpad = A_("g_gpad", [128, NTP])
    S.op("dve", lambda g: g.memset(gpad[:], 0.0), writes=["g_g"])
    g_ = gpad[:, 0:NT_]
    S.ts("dve", g_, sp_[:], nea[:, 0:1], ALU.mult, [spn_, "g_nea", "g_g"], ["g_g"], s2=-1.0, op1=ALU.mult)
    S.mm(pbig[:, 0:NTP], Ubd, gpad[:], ["g_cst", "g_g"], ["g_pbig"])
    S.act(gc[:], pbig[:, 0:NT_], AF.Copy, ["g_pbig"], ["g_gc"])
    S.mm(pbig[:, 0:NTP], Cbd, gpad[:], ["g_cst", "g_g"], ["g_pbig"])
    t_, tn_ = gp.get()
    S.tt("dve", t_[:], pbig[:, 0:NT_], gc[:], ALU.subtract, ["g_pbig", "g_gc"], [tn_])
    S.act(kdsc[:], t_[:], AF.Exp, [tn_], ["g_kdsc"])
    S.act(egc[:], gc[:], AF.Exp, ["g_gc"], ["g_egc"])
    S.act(t_[:], beta[:], AF.Ln, ["g_beta"], [tn_])
    S.tt("dve", gcb[:], gc[:], t_[:], ALU.add, ["g_gc", tn_], ["g_gcb"])
    S.tt("dve", begc[:], beta[:], egc[:], ALU.mult, ["g_beta", "g_egc"], ["g_begc"])
    for c, sel in ((0, sel0), (1, sel1)):
        S.mm(pbig[:, 0:NTP], sel, gpad[:], ["g_cst", "g_g"], ["g_pbig"])
        S.act(gl[c][:], pbig[:, 0:NT_], AF.Exp, ["g_pbig"], ["g_gl%d" % c])

    xin = [A_("g_xin%d" % w, [128, TSG + 3]) for w in range(3)]
    cv_ = [A_("g_cv%d" % w, [128, TSG]) for w in range(3)]
    sq = A_("g_sq", [128, TSG])
    rn = A_("g_rn", [128, 512])
    qTb = A_("g_qTb", [128, TSG], BF16)
    kTb = A_("g_kTb", [128, TSG], BF16)
    tp = Pool_(nc, st, "g_tp", [128, 128], 28)
    tpb = Pool_(nc, st, "g_tpb", [128, 128], 8, BF16)
    Sst = A_("g_S", [128, 128]); Sbf = A_("g_Sbf", [128, 128], BF16)
    col = Pool_(nc, st, "g_col", [128, 1], 6)
    zt = [A_("g_zt%d" % i, [128, 128]) for i in range(2)]

    import os
    GSTOP = float(os.environ.get("GSTOP", "9"))
    for b in range(NB if GSTOP > 1 else 0):
        S.op("dve", lambda g: g.memset(Sst[:], 0.0), writes=["g_S"])
        S.op("dve", lambda g: g.memset(Sbf[:], 0.0), writes=["g_Sbf"])
        for s in range(NSEG):
            tok0 = b * L + s * TSG
            for w in range(3):
                xn = "g_xin%d" % w
                cn = "g_cv%d" % w
                if s == 0:
                    S.op("dve", lambda g: g.memset(xin[w][:, 0:3], 0.0), writes=[xn])
                    S.dma("sp", xin[w][:, 3:3 + TSG], gq[:, w, tok0:tok0 + TSG], reads=[xn], writes=[xn])
                else:
                    S.dma("sp", xin[w][:, :], gq[:, w, tok0 - 3:tok0 + TSG], writes=[xn])
                S.ts("dve", cv_[w][:], xin[w][:, 3:3 + TSG], cw[:, w, 3:4], ALU.mult, [xn, "g_cw"], [cn])
                for k in (2, 1, 0):
                    S.stt(cv_[w][:], xin[w][:, k:k + TSG], cw[:, w, k:k + 1], cv_[w][:], ALU.mult, ALU.add, [xn, "g_cw", cn], [cn])
                S.act(cv_[w][:], cv_[w][:], AF.Silu, [cn], [cn])
            for w in range(2):
                cn = "g_cv%d" % w
                S.act(sq[:], cv_[w][:], AF.Square, [cn], ["g_sq"])
                QB = min(512, TSG)
                for q in range(TSG // QB):
                    qs = slice(q * QB, (q + 1) * QB)
                    S.mm(pbig[:, 0:QB], ones, sq[:, qs], ["g_cst", "g_sq"], ["g_pbig"])
                    if w == 0:
                        S.act(rn[:, 0:QB], pbig[:, 0:QB], AF.Sqrt, ["g_pbig"], ["g_rn"], scale=128.0, bias=128.0 * 1e-6)
                    else:
                        S.act(rn[:, 0:QB], pbig[:, 0:QB], AF.Sqrt, ["g_pbig"], ["g_rn"], scale=1.0, bias=1e-6)
                    S.op("dve", lambda g: g.reciprocal(out=rn[:, 0:QB], in_=rn[:, 0:QB]), reads=["g_rn"], writes=["g_rn"])
                    S.tt("dve", cv_[w][:, qs], cv_[w][:, qs], rn[:, 0:QB], ALU.mult, [cn, "g_rn"], [cn])
                S.act((qTb if w == 0 else kTb)[:], cv_[w][:], AF.Copy, [cn], ["g_qTb" if w == 0 else "g_kTb"])
            kn_, vn_ = cv_[1], cv_[2]
            for it in range(TPS if GSTOP > 2 else 0):
                ti = tok0 // 128 + it
                cs = slice(it * 128, (it + 1) * 128)
                tc_ = lambda a: a[:, ti:ti + 1]
                zi = ti % 2
                S.dma("sp", zt[zi][:], gz[ti], writes=["g_zt%d" % zi])
                S.op("pe", lambda g: g.transpose(out=ptk[:], in_=kn_[:, cs], identity=ident), reads=["g_cv1", "g_cst"], writes=[B0])
                kbg, kbgn = tp.get()
                S.ts("dve", kbg[:], ptk[:], tc_(begc), ALU.mult, [B0, "g_begc"], [kbgn])
                if GSTOP <= 2.05:
                    S.dma("sp", gy[ti], kbg[:], reads=[kbgn], writes=[("gy", ti)])
                    continue
                kdec, kdecn = tpb.get()
                S.ts("dve", kdec[:], ptk[:], tc_(kdsc), ALU.mult, [B0, "g_kdsc"], [kdecn])
                if GSTOP <= 2.1:
                    S.dma("sp", gy[ti], kbg[:], reads=[kbgn], writes=[("gy", ti)])
                    continue
                S.op("pe", lambda g: g.transpose(out=ptv[:], in_=vn_[:, cs], identity=ident), reads=["g_cv2", "g_cst"], writes=[B0])
                bv, bvn = tp.get()
                S.ts("dve", bv[:], ptv[:], tc_(beta), ALU.mult, [B0, "g_beta"], [bvn])
                if GSTOP <= 2.2:
                    S.dma("sp", gy[ti], bv[:], reads=[bvn], writes=[("gy", ti)])
                    continue
                S.mm(pKK[:], kTb[:, cs], kTb[:, cs], ["g_kTb"], [B0])
                S.mm(pQK[:], kTb[:, cs], qTb[:, cs], ["g_kTb", "g_qTb"], [B0])
                dg1, dg1n = tp.get(); dg2, dg2n = tp.get()
                S.ts("pool", dg1[:], ident, tc_(gcb), ALU.mult, ["g_cst", "g_gcb"], [dg1n])
                S.ts("pool", dg2[:], ident, tc_(gc), ALU.mult, ["g_cst", "g_gc"], [dg2n])
                S.mm(pR1[:], ones, dg1[:], ["g_cst", dg1n], [B1])
                S.mm(pR2[:], ones, dg2[:], ["g_cst", dg2n], [B1])
                E1, E1n = tp.get(); E2, E2n = tp.get()
                S.ts("dve", E1[:], pR1[:], tc_(gc), ALU.subtract, [B1, "g_gc"], [E1n], s2=0.0, op1=ALU.min)
                S.act(E1[:], E1[:], AF.Exp, [E1n], [E1n])
                S.ts("dve", E2[:], pR2[:], tc_(gc), ALU.subtract, [B1, "g_gc"], [E2n], s2=0.0, op1=ALU.min)
                S.act(E2[:], E2[:], AF.Exp, [E2n], [E2n])
                if GSTOP <= 2.4:
                    S.dma("sp", gy[ti], E1[:], reads=[E1n], writes=[("gy", ti)])
                    continue
                A0, A0n = tp.get()
                S.tt("dve", A0[:], pKK[:], E1[:], ALU.mult, [B0, E1n], [A0n])
                S.tt("pool", A0[:], A0[:], mSU, ALU.mult, [A0n, "g_cst"], [A0n])
                S.tt("dve", E2[:], pQK[:], E2[:], ALU.mult, [B0, E2n], [E2n])
                qkT, qkTn = tpb.get()
                S.tt("pool", qkT[:], E2[:], mIU, ALU.mult, [E2n, "g_cst"], [qkTn])
                S.op("pe", lambda g: g.transpose(out=pM[:], in_=A0[:], identity=ident), reads=[A0n, "g_cst"], writes=[B1])
                M0, M0n = tp.get()
                S.act(M0[:], pM[:], AF.Copy, [B1], [M0n])
                R, Rn = tp.get()
                S.tt("pool", R[:], ident, A0[:], ALU.subtract, ["g_cst", A0n], [Rn])
                if GSTOP <= 2.6:
                    S.dma("sp", gy[ti], R[:], reads=[Rn], writes=[("gy", ti)])
                    continue
                Pp, Ppn, Qp, Qpn = A0, A0n, M0, M0n
                for l in range(1, 6):
                    S.mm(pQ[:], Pp[:], Qp[:], [Ppn, Qpn], [B2])
                    Qn_, Qnn = tp.get()
                    S.act(Qn_[:], pQ[:], AF.Copy, [B2], [Qnn])
                    if l < 5:
                        S.mm(pP[:], Qp[:], Pp[:], [Ppn, Qpn], [B1])
                        Pn_, Pnn = tp.get()
                        S.op("dve", lambda g: g.tensor_copy(out=Pn_[:], in_=pP[:]), reads=[B1], writes=[Pnn])
                    S.mm(pR[:], Qn_[:], R[:], [Qnn, Rn], [B2])
                    R2, R2n = tp.get()
                    S.tt("dve", R2[:], R[:], pR[:], ALU.add, [Rn, B2], [R2n])
                    R, Rn = R2, R2n
                    Qp, Qpn = Qn_, Qnn
                    if l < 5:
                        Pp, Ppn = Pn_, Pnn
                if GSTOP <= 2.8:
                    S.dma("sp", gy[ti], R[:], reads=[Rn], writes=[("gy", ti)])
                    continue
                S.mm(pwT[:], kbg[:], R[:], [kbgn, Rn], [B2])
                wTb, wTbn = tpb.get()
                S.act(wTb[:], pwT[:], AF.Copy, [B2], [wTbn])
                S.mm(pu[:], R[:], bv[:], [Rn, bvn], [B2])
                u_, un_ = tp.get()
                S.op("dve", lambda g: g.tensor_copy(out=u_[:], in_=pu[:]), reads=[B2], writes=[un_])
                o_, on_ = tp.get()
                vnew, vnewn = tpb.get()
                tmp, tmpn = tp.get()
                if GSTOP <= 3:
                    S.dma("sp", gy[ti], u_[:], reads=[un_], writes=[("gy", ti)])
                    continue
                for c in range(2):
                    p = slice(64 * c, 64 * c + 64)
                    pwS, pqS, pqv, pdS = rslots[c]
                    RB = "g_pb%d" % (3 + c)
                    tk = slice(it * 128 + 64 * c, it * 128 + 64 * c + 64)
                    S.mm(pwS[p, :], wTb[:, p], Sbf[:], [wTbn, "g_Sbf"], [RB])
                    S.mm(pqS[p, :], qTb[:, tk], Sbf[:], ["g_qTb", "g_Sbf"], [RB])
                    S.tt("dve", vnew[p, :], u_[p, :], pwS[p, :], ALU.subtract, [un_, RB], [(vnewn, c)])
                    S.mm(pdS[:], kdec[p, :], vnew[p, :], [kdecn, (vnewn, c)], [RB])
                    S.mm(pqv[p, :], qkT[p, p], vnew[p, :], [qkTn, (vnewn, c)], [RB])
                    S.stt(Sst[:], Sst[:], gl[c][:, ti:ti + 1], pdS[:], ALU.mult, ALU.add, ["g_S", "g_gl%d" % c, RB], ["g_S"])
                    S.act(Sbf[:], Sst[:], AF.Copy, ["g_S"], ["g_Sbf"])
                    S.act(tmp[p, :], pqv[p, :], AF.Copy, [RB], [(tmpn, c)])
                    S.stt(o_[p, :], pqS[p, :], egc[p, ti:ti + 1], tmp[p, :], ALU.mult, ALU.add, [RB, "g_egc", (tmpn, c)], [(on_, c)])
                import os
                if os.environ.get("GDBG") and ti == 0:
                    def dbg(name, ap, rn_, shape=[128, 128]):
                        d = dout(name, shape)
                        S.dma("sp", d, ap, reads=rn_, writes=["dbg_" + name])
                    dbg("d_A", A0[:], [A0n]); dbg("d_R", R[:], [Rn]); dbg("d_u", u_[:], [un_]); dbg("d_o", o_[:], [(on_, 0), (on_, 1)])
                    dbg("d_kbg", kbg[:], [kbgn]); dbg("d_bv", bv[:], [bvn]); dbg("d_E1", E1[:], [E1n]); dbg("d_M", M0[:], [M0n])
                    dbg("d_gc", gc[:], ["g_gc"], [128, NT_]); dbg("d_beta", beta[:], ["g_beta"], [128, NT_]); dbg("d_g", g_, ["g_g"], [128, NT_])
                    dbg("d_kn", cv_[1][:, 0:128], ["g_cv1"]); dbg("d_qn", cv_[0][:, 0:128], ["g_cv0"]); dbg("d_S", Sst[:], ["g_S"])
                if GSTOP <= 4:
                    S.dma("sp", gy[ti], o_[:], reads=[(on_, 0), (on_, 1)], writes=[("gy", ti)])
                    continue
                ss, ssn = col.get()
                junk, junkn = tp.get()
                S.op("act", lambda g: g.activation(out=junk[:], in_=o_[:], func=AF.Square, accum_out=ss[:]),
                     reads=[(on_, 0), (on_, 1)], writes=[junkn, ssn])
                S.ts("dve", ss[:], ss[:], 1.0 / 128.0, ALU.mult, [ssn], [ssn], s2=1e-6, op1=ALU.add)
                S.act(ss[:], ss[:], AF.Sqrt, [ssn], [ssn])
                S.op("dve", lambda g: g.reciprocal(out=ss[:], in_=ss[:]), reads=[ssn], writes=[ssn])
                S.act(zt[zi][:], zt[zi][:], AF.Silu, ["g_zt%d" % zi], ["g_zt%d" % zi])
                y_, yn_ = tp.get()
                S.stt(y_[:], o_[:], ss[:, 0:1], ngb[:], ALU.mult, ALU.mult, [(on_, 0), (on_, 1), ssn, "g_ngb"], [yn_])
                S.tt("pool", y_[:], y_[:], zt[zi][:], ALU.mult, [yn_, "g_zt%d" % zi], [yn_])
                S.dma("sp", gy[ti], y_[:], reads=[yn_], writes=[("gy", ti)])
    S.barrier()
    for cm in reversed(st):
        cm.__exit__(None, None, None)


def gdn_consts():
    i = np.arange(128)
    same = (i[:, None] // 64) == (i[None, :] // 64)
    c = np.zeros((128, 8, 128), np.float32)
    c[:, 0] = np.eye(128)
    c[:, 1] = (i[:, None] <= i[None, :]) & same
    c[:, 2] = same
    c[:, 3] = (i[:, None] < 64) * np.ones((1, 128))
    c[:, 4] = (i[:, None] >= 64) * np.ones((1, 128))
    c[:, 5] = (i[:, None] < i[None, :]) & same
    c[:, 6] = (i[:, None] <= i[None, :]) & same
    c[:, 7] = 1.0
    return c


_PROGS = {}


def _prog(key, fn):
    if key not in _PROGS:
        _PROGS[key] = fn()
    return _PROGS[key]


def _gbpack(g, b):
    return np.ascontiguousarray(np.concatenate([g.reshape(-1, 128).T, b.reshape(-1, 128).T], axis=1), dtype=np.float32)


def _tile_gu(w, FW=256):
    w = np.asarray(w, dtype=np.float32)
    D, F = w.shape
    nfb = (F + FW - 1) // FW
    if nfb * FW != F:
        w = np.concatenate([w, np.zeros((D, nfb * FW - F), np.float32)], axis=1)
    return np.ascontiguousarray(w.reshape(D // 128, 128, nfb, FW).transpose(2, 1, 0, 3))


def _tile_d(w):
    w = np.asarray(w, dtype=np.float32)
    F, D = w.shape
    FC = F // 128
    WDG = [k for k in (11, 4, 2, 1) if FC % k == 0][0]
    return np.ascontiguousarray(w.reshape(FC // WDG, WDG, 128, D // 128, 128).transpose(3, 0, 2, 1, 4))


def _run(nc, in_maps):
    res = run_bass_kernel_spmd(nc, in_maps, core_ids=list(range(len(in_maps))))
    return res.results


def _c(a):
    return np.ascontiguousarray(a, dtype=np.float32)


def kernel(x, ffn1_w_gate, ffn1_w_up, ffn1_w_down, ln1_g, ln1_b, w_in,
           s5_lambda_re, s5_lambda_im, s5_b_re, s5_b_im, s5_c_re, s5_c_im, s5_d, s5_log_step,
           s5_w_glu, s5_b_glu, gdn_conv_w, gdn_a_log, gdn_dt_bias, gdn_norm_g,
           lru_conv_w, lru_conv_b, lru_w_a, lru_b_a, lru_w_x, lru_b_x, lru_lambda,
           w_out, ln2_g, ln2_b, ffn2_w_gate, ffn2_w_up, ffn2_w_down, ln3_g, ln3_b, depth=None):
    x = np.asarray(x)
    B, L, D = x.shape
    depth = DEPTH if depth is None else depth
    NC = N_CORES
    NTOK = B * L
    NTc = NTOK // NC
    NT_ = NTOK // 128
    NPJ = w_in.shape[-1]
    A = lambda v: np.asarray(v)
    XT = _c(x.reshape(NTOK, D).T)
    tsl = lambda i: slice(i * NTc, (i + 1) * NTc)
    p_ffn1 = _prog(("ffn", NTc, NPJ), lambda: build_ffn(NTc, NPJ=NPJ))
    p_ffn2 = _prog(("ffn", NTc, 0), lambda: build_ffn(NTc))
    p_out = _prog(("out", NTc), lambda: build_ffn(NTc, F=D_MODEL, mode="mix"))
    p_mix = _prog(("mix", L, B), lambda: build_mix(L, B))
    gconst = gdn_consts()
    s5tau = _c(np.broadcast_to(np.arange(512, dtype=np.float32), (128, 512)))
    z64 = np.zeros((64, 64), np.float32)
    for l in range(depth):
        wg, wu, wd, wi = _tile_gu(A(ffn1_w_gate[l])), _tile_gu(A(ffn1_w_up[l])), _tile_d(A(ffn1_w_down[l])), _tile_gu(A(w_in[l]))
        gb = _gbpack(A(ln1_g[l]), A(ln1_b[l]))
        r = _run(p_ffn1, [dict(xT=_c(XT[:, tsl(i)]), wg=wg, wu=wu, wd=wd, gb=gb, win=wi) for i in range(NC)])
        X1T = np.concatenate([r[i]["yT"] for i in range(NC)], axis=1)
        PJ = np.concatenate([r[i]["pjT"] for i in range(NC)], axis=1)
        del r, wg, wu, wd, wi
        lre, lim, lst = A(s5_lambda_re[l]), A(s5_lambda_im[l]), A(s5_log_step[l])
        bre, bim, cre, cim = A(s5_b_re[l]), A(s5_b_im[l]), A(s5_c_re[l]), A(s5_c_im[l])
        gcwl = A(gdn_conv_w[l])
        lcw = A(lru_conv_w[l])
        ims = []
        for c in range(NC):
            r0, r1 = 4624 + c * 64, 5136 + c * 64
            im = {}
            im["lx"] = _c(np.concatenate([PJ[r0:r0 + 64, b * L:(b + 1) * L] for b in range(B)], axis=0))
            im["lg"] = _c(np.concatenate([PJ[r1:r1 + 64, b * L:(b + 1) * L] for b in range(B)], axis=0))
            cs = slice(c * 64, (c + 1) * 64)
            cols = [lcw[0, cs], lcw[1, cs], lcw[2, cs], lcw[3, cs], A(lru_conv_b[l])[cs], A(lru_lambda[l])[cs],
                    A(lru_b_a[l])[cs], A(lru_b_x[l])[cs]]
            im["lpar"] = _c(np.stack([np.tile(v, B) for v in cols], axis=1))
            wa, wx = A(lru_w_a[l][c]), A(lru_w_x[l][c])
            im["lwa"] = _c(np.block([[wa, z64], [z64, wa]]))
            im["lwx"] = _c(np.block([[wx, z64], [z64, wx]]))
            gs = slice(4 * c, 4 * c + 4)
            im["su"] = _c(PJ[c * 64:(c + 1) * 64, :])
            row = np.stack([lre[gs].reshape(-1), lim[gs].reshape(-1), np.repeat(lst[gs], 64)])
            im["s5row"] = _c(np.broadcast_to(row[None], (64, 3, 256)))
            col = np.zeros((128, 2, 3), np.float32)
            bT = np.zeros((64, 2, 256), np.float32)
            cT = np.zeros((128, 2, 2, 64), np.float32)
            for j in range(2):
                g2 = slice(4 * c + 2 * j, 4 * c + 2 * j + 2)
                col[:, j, 0] = lre[g2].reshape(-1)
                col[:, j, 1] = lim[g2].reshape(-1)
                col[:, j, 2] = np.repeat(lst[g2], 64)
            for g in range(4):
                gg = 4 * c + g
                bT[16 * g:16 * g + 16, 0, 64 * g:64 * g + 64] = bre[gg].T
                bT[16 * g:16 * g + 16, 1, 64 * g:64 * g + 64] = bim[gg].T
                j, rr = divmod(g, 2)
                cT[64 * rr:64 * rr + 64, j, 0, 16 * g:16 * g + 16] = cre[gg].T
                cT[64 * rr:64 * rr + 64, j, 1, 16 * g:16 * g + 16] = cim[gg].T
            im["s5col"], im["s5bT"], im["s5cT"] = col, bT, cT
            im["s5d"] = _c(A(s5_d[l])[c * 64:(c + 1) * 64].reshape(64, 1))
            im["s5tau"] = s5tau
            im["gq"] = _c(np.stack([PJ[512 + w * 1024 + c * 128:512 + w * 1024 + (c + 1) * 128, :] for w in range(3)], axis=1))
            im["gcw"] = _c(np.stack([gcwl[:, w * 1024 + c * 128:w * 1024 + (c + 1) * 128].T for w in range(3)], axis=1))
            im["gz"] = _c(PJ[3584 + c * 128:3584 + (c + 1) * 128, :].T.reshape(NT_, 128, 128))
            im["gab"] = _c(np.stack([PJ[4608 + c, :].reshape(NT_, 128).T, PJ[4616 + c, :].reshape(NT_, 128).T], axis=1))
            im["gpar"] = _c(np.tile(np.array([[A(gdn_a_log[l])[c], A(gdn_dt_bias[l])[c]]], np.float32), (128, 1)))
            im["gng"] = _c(np.tile(A(gdn_norm_g[l])[None, :], (128, 1)))
            im["gconst"] = gconst
            ims.append(im)
        r = _run(p_mix, ims)
        del ims, PJ
        YMT = np.empty((D_MODEL, NTOK), np.float32)
        for c in range(NC):
            YMT[c * 64:(c + 1) * 64, :] = r[c]["y5"]
            YMT[512 + c * 128:512 + (c + 1) * 128, :] = r[c]["gy"].reshape(NTOK, 128).T
            for b in range(B):
                YMT[1536 + c * 64:1536 + (c + 1) * 64, b * L:(b + 1) * L] = r[c]["ly"][b * 64:(b + 1) * 64, :]
        del r
        wgl, wo = _c(A(s5_w_glu[l])), _tile_d(A(w_out[l]))
        bgl = _c(A(s5_b_glu[l]).reshape(-1, 128).T)
        gb = _gbpack(A(ln2_g[l]), A(ln2_b[l]))
        r = _run(p_out, [dict(xT=_c(X1T[:, tsl(i)]), ymT=_c(YMT[:, tsl(i)]), wglu=wgl, bglu=bgl, wd=wo, gb=gb) for i in range(NC)])
        X2T = np.concatenate([r[i]["yT"] for i in range(NC)], axis=1)
        del r, YMT, X1T
        wg, wu, wd = _tile_gu(A(ffn2_w_gate[l])), _tile_gu(A(ffn2_w_up[l])), _tile_d(A(ffn2_w_down[l]))
        gb = _gbpack(A(ln3_g[l]), A(ln3_b[l]))
        r = _run(p_ffn2, [dict(xT=_c(X2T[:, tsl(i)]), wg=wg, wu=wu, wd=wd, gb=gb) for i in range(NC)])
        XT = np.concatenate([r[i]["yT"] for i in range(NC)], axis=1)
        del r, wg, wu, wd, X2T
    return np.ascontiguousarray(XT.T.reshape(B, L, D)).astype(np.float32)
```

```python
import numpy as np
import concourse.bass as bass
import concourse.mybir as mybir
from concourse.bass_utils import run_bass_kernel_spmd

F32 = mybir.dt.float32
BF16 = mybir.dt.bfloat16
AF = mybir.ActivationFunctionType
ALU = mybir.AluOpType

D_MODEL = 2048
D_FF = 5632
DEPTH = 4
N_CORES = 8
ALPHA = (2.0 * DEPTH) ** 0.25
LN_EPS = 1e-5


class Sched:
    def __init__(self, nc, n_dma_sems=24):
        self.nc = nc
        self.eng = {"pe": nc.tensor, "dve": nc.vector, "act": nc.scalar, "pool": nc.gpsimd, "sp": nc.sync}
        self.sem = {}
        self.cnt = {}
        self.seen = {k: {} for k in self.eng}
        self._cms = []
        for k in self.eng:
            cm = nc.semaphore("s_" + k)
            self.sem[k] = cm.__enter__()
            self._cms.append(cm)
            self.cnt[k] = 0
        self.dma_sems = []
        for i in range(n_dma_sems):
            cm = nc.semaphore("s_dma%d" % i)
            self.dma_sems.append([cm.__enter__(), 0])
            self._cms.append(cm)
        self.dma_rr = 0
        self.last_w = {}
        self.readers = {}
        self.semobj = {k: self.sem[k] for k in self.eng}
        for i, (s, _) in enumerate(self.dma_sems):
            self.semobj["dma%d" % i] = s

    def close(self):
        for cm in reversed(self._cms):
            cm.__exit__(None, None, None)

    def _wait(self, e, semkey, val):
        if self.seen[e].get(semkey, 0) >= val:
            return
        if semkey == e and val > self.cnt[e]:
            return
        self.eng[e].wait_ge(self.semobj[semkey], val)
        self.seen[e][semkey] = val

    def _deps(self, e, reads, writes):
        for r in reads:
            w = self.last_w.get(r)
            if w is not None:
                self._wait(e, *w)
        for r in writes:
            w = self.last_w.get(r)
            if w is not None:
                self._wait(e, *w)
            for sk, v in self.readers.get(r, {}).items():
                self._wait(e, sk, v)

    def _commit(self, semkey, val, reads, writes):
        for r in reads:
            self.readers.setdefault(r, {})[semkey] = val
        for r in writes:
            self.last_w[r] = (semkey, val)
            self.readers[r] = {}

    def op(self, e, fn, reads=(), writes=(), inc=True):
        self._deps(e, reads, writes)
        ins = fn(self.eng[e])
        if inc:
            self.cnt[e] += 1
            ins.then_inc(self.sem[e], 1)
            self._commit(e, self.cnt[e], reads, writes)
        else:
            self._commit(e, self.cnt[e] + 1, reads, writes)

    def dma(self, q, out, in_, reads=(), writes=()):
        i = self.dma_rr
        self.dma_rr = (self.dma_rr + 1) % len(self.dma_sems)
        sk = "dma%d" % i
        self._wait(q, sk, self.dma_sems[i][1])
        self._deps(q, reads, writes)
        ins = self.eng[q].dma_start(out=out, in_=in_)
        self.dma_sems[i][1] += 16
        ins.then_inc(self.dma_sems[i][0], 16)
        self._commit(sk, self.dma_sems[i][1], reads, writes)

    def finish(self, e="sp"):
        for r, (sk, v) in list(self.last_w.items()):
            self._wait(e, sk, v)

    def barrier(self):
        tot = {k: self.cnt[k] for k in self.eng}
        for i, (s, v) in enumerate(self.dma_sems):
            tot["dma%d" % i] = v
        for e in self.eng:
            for sk, v in tot.items():
                if v > 0:
                    self._wait(e, sk, v)
        self.last_w = {}
        self.readers = {}

    def tt(self, e, out, a, b, op, r, w):
        self.op(e, lambda g: g.tensor_tensor(out=out, in0=a, in1=b, op=op), reads=r, writes=w)

    def ts(self, e, out, a, s1, op0, r, w, s2=None, op1=None):
        if op1 is None:
            self.op(e, lambda g: g.tensor_scalar(out=out, in0=a, scalar1=s1, scalar2=None, op0=op0), reads=r, writes=w)
        else:
            self.op(e, lambda g: g.tensor_scalar(out=out, in0=a, scalar1=s1, scalar2=s2, op0=op0, op1=op1), reads=r, writes=w)

    def stt(self, out, a, s, b, op0, op1, r, w):
        self.op("dve", lambda g: g.scalar_tensor_tensor(out=out, in0=a, scalar=s, in1=b, op0=op0, op1=op1), reads=r, writes=w)

    def act(self, out, a, func, r, w, scale=None, bias=None):
        kw = {}
        if scale is not None:
            kw["scale"] = scale
        if bias is not None:
            kw["bias"] = bias
        self.op("act", lambda g: g.activation(out=out, in_=a, func=func, **kw), reads=r, writes=w)

    def mm(self, out, lhsT, rhs, r, w, start=True, stop=True, inc=None):
        self.op("pe", lambda g: g.matmul(out, lhsT=lhsT, rhs=rhs, start=start, stop=stop), reads=r, writes=w,
                inc=(stop if inc is None else inc))


def _alloc(nc, stack, kind, name, shape, dt):
    cm = (nc.sbuf_tensor if kind == "sb" else nc.psum_tensor)(name, shape, dt)
    t = cm.__enter__()
    stack.append(cm)
    return t


def build_ffn(NT, D=D_MODEL, F=D_FF, TB=None, mode="ffn", NPJ=0, GW=512):
    nc = bass.Bass("TRN2", target_bir_lowering=False)
    TB = TB or min(512, NT)
    QW = min(512, TB)
    rscale = 0.5 if mode == "ffn" else 1.0
    DC, FC = D // 128, F // 128
    FW = 256
    NFB = F // FW
    FPB = FW // 128
    xT = nc.dram_tensor("xT", [D, NT], F32, kind="ExternalInput").ap()
    if mode == "ffn":
        wg = nc.dram_tensor("wg", [NFB, 128, DC, FW], F32, kind="ExternalInput").ap()
        wu = nc.dram_tensor("wu", [NFB, 128, DC, FW], F32, kind="ExternalInput").ap()
    else:
        ymT = nc.dram_tensor("ymT", [F, NT], F32, kind="ExternalInput").ap()
        wglu = nc.dram_tensor("wglu", [GW, GW], F32, kind="ExternalInput").ap()
        bglu = nc.dram_tensor("bglu", [128, GW // 128], F32, kind="ExternalInput").ap()
        ymT_v = ymT.rearrange("(c p) t -> p c t", p=128)
        wglu_v = wglu.rearrange("(c p) f -> p c f", p=128)
    if NPJ:
        NPB = (NPJ + FW - 1) // FW
        win = nc.dram_tensor("win", [NPB, 128, DC, FW], F32, kind="ExternalInput").ap()
        pjT = nc.dram_tensor("pjT", [NPJ, NT], F32, kind="ExternalOutput").ap()
    WDG = [w for w in (11, 4, 2, 1) if FC % w == 0][0]
    NDG = FC // WDG
    wd = nc.dram_tensor("wd", [DC, NDG, 128, WDG, 128], F32, kind="ExternalInput").ap()
    gb = nc.dram_tensor("gb", [128, 2 * DC], F32, kind="ExternalInput").ap()
    yT = nc.dram_tensor("yT", [D, NT], F32, kind="ExternalOutput").ap()
    xT_v = xT.rearrange("(c p) t -> p c t", p=128)
    yT_v = yT.rearrange("(c p) t -> p c t", p=128)

    st = []
    S = Sched(nc)
    xb = _alloc(nc, st, "sb", "xb", [128, DC, TB], BF16)
    aT = _alloc(nc, st, "sb", "aT", [128, FC, TB], BF16)
    rT = _alloc(nc, st, "sb", "rT", [128, DC, TB], F32)
    wgs = [_alloc(nc, st, "sb", "wgs%d" % i, [128, DC, FW], BF16) for i in range(2)]
    wus = [_alloc(nc, st, "sb", "wus%d" % i, [128, DC, FW], BF16) for i in range(2)]
    wds = [_alloc(nc, st, "sb", "wds%d" % i, [128, WDG, 128], BF16) for i in range(3)]
    gbs = _alloc(nc, st, "sb", "gbs", [128, 2 * DC], F32)
    ones = _alloc(nc, st, "sb", "ones", [128, 128], F32)
    sg = [_alloc(nc, st, "sb", "sg%d" % i, [128, 512], F32) for i in range(2)]
    GC = GW // 128
    if mode == "mix":
        y5b = _alloc(nc, st, "sb", "y5b", [128, GC, TB], BF16)
        y5f = _alloc(nc, st, "sb", "y5f", [128, GC, TB], F32)
        wgl = _alloc(nc, st, "sb", "wgl", [128, GC, GW], BF16)
        bgl = _alloc(nc, st, "sb", "bgl", [128, GC], F32)
    if NPJ:
        ybf = _alloc(nc, st, "sb", "ybf", [128, DC, TB], BF16)
        po = [_alloc(nc, st, "sb", "po%d" % i, [128, TB], F32) for i in range(2)]
    xf = [_alloc(nc, st, "sb", "xf%d" % i, [128, TB], F32) for i in range(2)]
    sq = [_alloc(nc, st, "sb", "sq%d" % i, [128, TB], F32) for i in range(2)]
    mean = _alloc(nc, st, "sb", "mean", [128, TB], F32)
    rstd = _alloc(nc, st, "sb", "rstd", [128, TB], F32)
    yo = [_alloc(nc, st, "sb", "yo%d" % i, [128, TB], F32) for i in range(2)]
    pg = [_alloc(nc, st, "ps", "pg%d" % i, [128, 512], F32) for i in range(2)]
    pu = [_alloc(nc, st, "ps", "pu%d" % i, [128, 512], F32) for i in range(2)]
    pd = [_alloc(nc, st, "ps", "pd%d" % i, [128, 512], F32) for i in range(2)]
    pm = _alloc(nc, st, "ps", "pm", [128, 512], F32)
    pv = _alloc(nc, st, "ps", "pv", [128, 512], F32)
    NQ = TB // QW

    S.dma("sp", gbs[:], gb, writes=["gbs"])
    S.op("dve", lambda e: e.memset(ones[:], 1.0 / D), writes=["ones"])

    kgu = 0
    kd = 0
    kx = 0
    ko = 0
    for tb in range(NT // TB):
        t0 = tb * TB
        S.dma("pool", xb[:], xT_v[:, :, t0:t0 + TB], writes=["xb"])
        if mode == "mix":
            if tb == 0:
                S.dma("pool", wgl[:], wglu_v, writes=["wgl"])
                S.dma("sp", bgl[:], bglu, writes=["bgl"])
            S.dma("pool", y5b[:], ymT_v[:, 0:GC, t0:t0 + TB], writes=["y5b"])
            S.dma("sp", y5f[:], ymT_v[:, 0:GC, t0:t0 + TB], writes=["y5f"])
            for q in range(NQ):
                S.dma("pool", aT[:, GC:FC, q * QW:(q + 1) * QW], ymT_v[:, GC:FC, t0 + q * QW:t0 + (q + 1) * QW],
                      writes=[("aT", fc, q) for fc in range(GC, FC)])
            for jc in range(GC):
                for q in range(NQ):
                    pb = kgu % 2
                    kgu += 1
                    for ic in range(GC):
                        S.mm(pg[pb][:, 0:QW], wgl[:, ic, jc * 128:(jc + 1) * 128], y5b[:, ic, q * QW:(q + 1) * QW],
                             ["wgl", "y5b"], [("pg", pb)], start=(ic == 0), stop=(ic == GC - 1))
                    S.act(sg[pb][:, 0:QW], pg[pb][:, 0:QW], AF.Sigmoid, [("pg", pb), "bgl"], [("sg", pb)], bias=bgl[:, jc:jc + 1])
                    S.tt("dve", aT[:, jc, q * QW:(q + 1) * QW], sg[pb][:, 0:QW], y5f[:, jc, q * QW:(q + 1) * QW], ALU.mult,
                         [("sg", pb), "y5f"], [("aT", jc, q)])
        if mode == "ffn":
            for fb in range(NFB):
                b = fb % 2
                S.dma("pool", wgs[b][:], wg[fb], writes=[("wg", b)])
                S.dma("pool", wus[b][:], wu[fb], writes=[("wu", b)])
                for fi in range(FPB):
                    fc = fb * FPB + fi
                    for q in range(NQ):
                        pb = kgu % 2
                        kgu += 1
                        for c in range(DC):
                            S.op("pe", lambda e, c=c: e.matmul(pg[pb][:, 0:QW], lhsT=wgs[b][:, c, fi * 128:(fi + 1) * 128],
                                                             rhs=xb[:, c, q * QW:(q + 1) * QW], start=(c == 0), stop=(c == DC - 1)),
                                 reads=[("wg", b), "xb"], writes=[("pg", pb)], inc=(c == DC - 1))
                        for c in range(DC):
                            S.op("pe", lambda e, c=c: e.matmul(pu[pb][:, 0:QW], lhsT=wus[b][:, c, fi * 128:(fi + 1) * 128],
                                                             rhs=xb[:, c, q * QW:(q + 1) * QW], start=(c == 0), stop=(c == DC - 1)),
                                 reads=[("wu", b), "xb"], writes=[("pu", pb)], inc=(c == DC - 1))
                        S.op("act", lambda e: e.activation(out=sg[pb][:, 0:QW], in_=pg[pb][:, 0:QW], func=AF.Silu),
                             reads=[("pg", pb)], writes=[("sg", pb)])
                        S.op("dve", lambda e: e.tensor_tensor(out=aT[:, fc, q * QW:(q + 1) * QW], in0=sg[pb][:, 0:QW], in1=pu[pb][:, 0:QW], op=ALU.mult),
                             reads=[("sg", pb), ("pu", pb)], writes=[("aT", fc, q)])
        for dco in range(DC):
            xb_ = kx % 2
            kx += 1
            S.dma("sp", xf[xb_][:], xT_v[:, dco, t0:t0 + TB], writes=[("xf", xb_)])
            for q in range(NQ):
                pb = kd % 2
                for g in range(NDG):
                    wb = kd % 3 if False else (kd * NDG + g) % 3
                    S.dma("pool", wds[wb][:], wd[dco, g], writes=[("wd", wb)])
                    for j in range(WDG):
                        fc = g * WDG + j
                        S.op("pe", lambda e, fc=fc, j=j: e.matmul(pd[pb][:, 0:QW], lhsT=wds[wb][:, j, :], rhs=aT[:, fc, q * QW:(q + 1) * QW],
                                                                 start=(fc == 0), stop=(fc == FC - 1)),
                             reads=[("wd", wb), ("aT", fc, q)], writes=[("pd", pb)], inc=(j == WDG - 1))
                kd += 1
                S.op("act", lambda e: e.activation(out=rT[:, dco, q * QW:(q + 1) * QW], in_=pd[pb][:, 0:QW], func=AF.Copy, scale=rscale),
                     reads=[("pd", pb)], writes=[("rT", dco, q)])
                S.op("dve", lambda e: e.scalar_tensor_tensor(out=rT[:, dco, q * QW:(q + 1) * QW], in0=xf[xb_][:, q * QW:(q + 1) * QW],
                                                           scalar=ALPHA, in1=rT[:, dco, q * QW:(q + 1) * QW], op0=ALU.mult, op1=ALU.add),
                     reads=[("xf", xb_), ("rT", dco, q)], writes=[("rT", dco, q)])
        for q in range(NQ):
            qs = slice(q * QW, (q + 1) * QW)
            for c in range(DC):
                S.op("pe", lambda e, c=c: e.matmul(pm[:, 0:QW], lhsT=ones[:], rhs=rT[:, c, qs], start=(c == 0), stop=(c == DC - 1)),
                     reads=["ones", ("rT", c, q)], writes=["pm"], inc=(c == DC - 1))
            S.op("act", lambda e: e.activation(out=mean[:, qs], in_=pm[:, 0:QW], func=AF.Copy), reads=["pm"], writes=[("mean", q)])
            for c in range(DC):
                S.op("dve", lambda e, c=c: e.tensor_tensor(out=rT[:, c, qs], in0=rT[:, c, qs], in1=mean[:, qs], op=ALU.subtract),
                     reads=[("rT", c, q), ("mean", q)], writes=[("rT", c, q)])
                sb_ = c % 2
                S.op("act", lambda e, c=c: e.activation(out=sq[sb_][:, qs], in_=rT[:, c, qs], func=AF.Square),
                     reads=[("rT", c, q)], writes=[("sq", sb_, q)])
                S.op("pe", lambda e, c=c: e.matmul(pv[:, 0:QW], lhsT=ones[:], rhs=sq[sb_][:, qs], start=(c == 0), stop=(c == DC - 1)),
                     reads=["ones", ("sq", sb_, q)], writes=["pv"])
            S.op("act", lambda e: e.activation(out=rstd[:, qs], in_=pv[:, 0:QW], func=AF.Sqrt, bias=LN_EPS, scale=1.0), reads=["pv"], writes=[("rstd", q)])
            S.op("dve", lambda e: e.reciprocal(out=rstd[:, qs], in_=rstd[:, qs]), reads=[("rstd", q)], writes=[("rstd", q)])
        for c in range(DC):
            ob = ko % 2
            ko += 1
            for q in range(NQ):
                qs = slice(q * QW, (q + 1) * QW)
                S.op("dve", lambda e: e.tensor_tensor(out=yo[ob][:, qs], in0=rT[:, c, qs], in1=rstd[:, qs], op=ALU.mult),
                     reads=[("rT", c, q), ("rstd", q)], writes=[("yo", ob)])
            S.op("pool", lambda e: e.tensor_scalar(out=yo[ob][:], in0=yo[ob][:], scalar1=gbs[:, c:c + 1], scalar2=gbs[:, DC + c:DC + c + 1],
                                                   op0=ALU.mult, op1=ALU.add),
                 reads=[("yo", ob), "gbs"], writes=[("yo", ob)])
            S.dma("sp", yT_v[:, c, t0:t0 + TB], yo[ob][:], reads=[("yo", ob)], writes=[("yT", c, tb)])
            if NPJ:
                S.act(ybf[:, c, :], yo[ob][:], AF.Copy, [("yo", ob)], [("ybf", c)])
        if NPJ:
            kp = 0
            for pb0 in range(0, NPJ, FW):
                pw = min(FW, NPJ - pb0)
                b = (pb0 // FW) % 2
                S.dma("pool", wgs[b][:], win[pb0 // FW], writes=[("wg", b)])
                for pc in range(0, pw, 128):
                    m = min(128, pw - pc)
                    for q in range(NQ):
                        pb = kp % 2
                        kp += 1
                        for c in range(DC):
                            S.mm(pg[pb][0:m, 0:QW], wgs[b][:, c, pc:pc + m], ybf[:, c, q * QW:(q + 1) * QW],
                                 [("wg", b)] + ([("ybf", cc) for cc in range(DC)] if c == 0 else []), [("pg", pb)], start=(c == 0), stop=(c == DC - 1))
                        S.act(po[pb][0:m, q * QW:(q + 1) * QW], pg[pb][0:m, 0:QW], AF.Copy, [("pg", pb)], [("po", pb)])
                        S.dma("sp", pjT[pb0 + pc:pb0 + pc + m, t0 + q * QW:t0 + (q + 1) * QW], po[pb][0:m, q * QW:(q + 1) * QW],
                              reads=[("po", pb)], writes=[("pjT", pb0 + pc, tb, q)])
    S.finish("sp")
    S.close()
    for cm in reversed(st):
        cm.__exit__(None, None, None)
    return nc


class Pool_:
    def __init__(self, nc, st, prefix, shape, n, dt=F32):
        self.tiles = [_alloc(nc, st, "sb", "%s%d" % (prefix, i), shape, dt) for i in range(n)]
        self.names = ["%s%d" % (prefix, i) for i in range(n)]
        self.k = 0

    def get(self):
        i = self.k % len(self.tiles)
        self.k += 1
        return self.tiles[i], self.names[i]


def emit_softplus(S, P, x, xn, out, outn, sl):
    ax, axn = P.get()
    S.act(sl(ax), x, AF.Abs, [xn], [axn])
    y, yn = P.get()
    S.act(sl(y), sl(ax), AF.Exp, [axn], [yn], scale=-1.0)
    t, tn = P.get()
    S.ts("dve", sl(t), sl(y), 2.0, ALU.add, [yn], [tn])
    S.op("dve", lambda g: g.reciprocal(out=sl(t), in_=sl(t)), reads=[tn], writes=[tn])
    s, sn = P.get()
    S.tt("dve", sl(s), sl(y), sl(t), ALU.mult, [yn, tn], [sn])
    s2, s2n = P.get()
    S.tt("dve", sl(s2), sl(s), sl(s), ALU.mult, [sn], [s2n])
    p, pn = P.get()
    S.ts("dve", sl(p), sl(s2), 1.0 / 11.0, ALU.mult, [s2n], [pn], s2=1.0 / 9.0, op1=ALU.add)
    for cst in (1.0 / 7.0, 1.0 / 5.0, 1.0 / 3.0, 1.0):
        S.tt("dve", sl(p), sl(p), sl(s2), ALU.mult, [pn, s2n], [pn])
        S.ts("dve", sl(p), sl(p), cst, ALU.add, [pn], [pn])
    S.tt("dve", sl(p), sl(p), sl(s), ALU.mult, [pn, sn], [pn])
    S.ts("dve", sl(ax), x, 0.0, ALU.max, [xn], [axn])
    S.stt(out, sl(p), 2.0, sl(ax), ALU.mult, ALU.add, [pn, axn], [outn])


def emit_gelu(S, out, outn, x, xn, tmp, tmpn, eng2="pool"):
    S.act(tmp, x, AF.Square, [xn], [tmpn])
    S.ts("dve", tmp, tmp, 0.044715, ALU.mult, [tmpn], [tmpn], s2=1.0, op1=ALU.add)
    S.tt("dve", tmp, tmp, x, ALU.mult, [tmpn, xn], [tmpn])
    S.act(tmp, tmp, AF.Sigmoid, [tmpn], [tmpn], scale=1.5957691216057308)
    S.tt(eng2, out, tmp, x, ALU.mult, [tmpn, xn], [outn])


def build_mix(L, NB=2, do_lru=True, do_s5=True, do_gdn=True):
    nc = bass.Bass("TRN2", target_bir_lowering=False)
    NTOK = NB * L
    S = Sched(nc)
    dr = {}

    def din(name, shape):
        dr[name] = nc.dram_tensor(name, shape, F32, kind="ExternalInput").ap()
        return dr[name]

    def dout(name, shape):
        dr[name] = nc.dram_tensor(name, shape, F32, kind="ExternalOutput").ap()
        return dr[name]

    if do_lru:
        emit_lru(nc, S, L, NB, din, dout)
        S.barrier()
    if do_s5:
        emit_s5(nc, S, L, NB, din, dout)
        S.barrier()
    if do_gdn:
        emit_gdn(nc, S, L, NB, din, dout)
    S.finish("sp")
    S.close()
    return nc


def emit_lru(nc, S, L, NB, din, dout):
    assert NB == 2
    TS = min(L, 2048)
    NSEG = L // TS
    lx = din("lx", [128, L])
    lg = din("lg", [128, L])
    lpar = din("lpar", [128, 8])
    lwa = din("lwa", [128, 128])
    lwx = din("lwx", [128, 128])
    ly = dout("ly", [128, L])
    st = []
    par = _alloc(nc, st, "sb", "l_par", [128, 8], F32)
    wa = _alloc(nc, st, "sb", "l_wa", [128, 128], BF16)
    wx = _alloc(nc, st, "sb", "l_wx", [128, 128], BF16)
    c12 = _alloc(nc, st, "sb", "l_c12", [128, 2], F32)
    carry = _alloc(nc, st, "sb", "l_carry", [128, 1], F32)
    tiny = Pool_(nc, st, "l_tiny", [128, 1], 8)
    big = Pool_(nc, st, "l_big", [128, TS], 9)
    xin = [_alloc(nc, st, "sb", "l_xin%d" % i, [128, TS + 3], F32) for i in range(2)]
    xcb = _alloc(nc, st, "sb", "l_xcb", [128, TS], BF16)
    pr = [_alloc(nc, st, "ps", "l_pr%d" % i, [128, 512], F32) for i in range(2)]
    pi = [_alloc(nc, st, "ps", "l_pi%d" % i, [128, 512], F32) for i in range(2)]

    S.dma("sp", par[:], lpar, writes=["l_par"])
    S.dma("pool", wa[:], lwa, writes=["l_wa"])
    S.dma("pool", wx[:], lwx, writes=["l_wx"])
    nl, nln = tiny.get()
    S.ts("dve", nl[:], par[:, 5:6], -1.0, ALU.mult, ["l_par"], [nln])
    sp_, spn = tiny.get()
    emit_softplus(S, tiny, nl[:], nln, sp_[:], spn, lambda t: t[:])
    S.ts("dve", c12[:, 0:1], sp_[:], -8.0, ALU.mult, [spn], ["l_c12"])
    S.ts("dve", c12[:, 1:2], sp_[:], -16.0, ALU.mult, [spn, "l_c12"], ["l_c12"])
    S.op("dve", lambda g: g.memset(carry[:], 0.0), writes=["l_carry"])

    for s in range(NSEG):
        xi = xin[s % 2]
        xn = "l_xin%d" % (s % 2)
        if s == 0:
            S.op("dve", lambda g: g.memset(xi[:, 0:3], 0.0), writes=[xn])
            S.dma("sp", xi[:, 3:3 + TS], lx[:, 0:TS], reads=[xn], writes=[xn])
        else:
            S.dma("sp", xi[:, :], lx[:, s * TS - 3:(s + 1) * TS], writes=[xn])
        xc, xcn = big.get()
        S.ts("dve", xc[:], xi[:, 3:3 + TS], par[:, 3:4], ALU.mult, [xn, "l_par"], [xcn], s2=par[:, 4:5], op1=ALU.add)
        for k in (2, 1, 0):
            S.stt(xc[:], xi[:, k:k + TS], par[:, k:k + 1], xc[:], ALU.mult, ALU.add, [xn, "l_par", xcn], [xcn])
        S.act(xcb[:], xc[:], AF.Copy, [xcn], ["l_xcb"])
        r, rn = big.get()
        ig, ign = big.get()
        for q in range(TS // 512):
            qs = slice(q * 512, (q + 1) * 512)
            b = q % 2
            S.mm(pr[b][:], wa[:], xcb[:, qs], ["l_wa", "l_xcb"], [("l_pr", b)])
            S.mm(pi[b][:], wx[:], xcb[:, qs], ["l_wx", "l_xcb"], [("l_pi", b)])
            S.act(r[:, qs], pr[b][:], AF.Sigmoid, [("l_pr", b), "l_par"], [rn], bias=par[:, 6:7])
            S.act(ig[:, qs], pi[b][:], AF.Sigmoid, [("l_pi", b), "l_par"], [ign], bias=par[:, 7:8])
        a, an = big.get()
        S.act(a[:], r[:], AF.Exp, [rn, "l_c12"], [an], scale=c12[:, 0:1])
        s2, s2n = big.get()
        S.act(s2[:], r[:], AF.Exp, [rn, "l_c12"], [s2n], scale=c12[:, 1:2])
        S.act(s2[:], s2[:], AF.Sqrt, [s2n], [s2n], scale=-1.0, bias=1.0)
        S.tt("dve", s2[:], s2[:], ig[:], ALU.mult, [s2n, ign], [s2n])
        S.tt("dve", s2[:], s2[:], xc[:], ALU.mult, [s2n, xcn], [s2n])
        h, hn = big.get()
        S.op("dve", lambda g: g.tensor_tensor_scan(out=h[:], data0=a[:], data1=s2[:], initial=carry[:, 0:1], op0=ALU.mult, op1=ALU.add),
             reads=[an, s2n, "l_carry"], writes=[hn])
        S.op("dve", lambda g: g.tensor_copy(out=carry[:], in_=h[:, TS - 1:TS]), reads=[hn], writes=["l_carry"])
        gt, gtn = big.get()
        S.dma("sp", gt[:], lg[:, s * TS:(s + 1) * TS], writes=[gtn])
        tmp, tmpn = big.get()
        ge, gen = big.get()
        emit_gelu(S, ge[:], gen, gt[:], gtn, tmp[:], tmpn)
        S.tt("dve", ge[:], ge[:], h[:], ALU.mult, [gen, hn], [gen])
        S.dma("sp", ly[:, s * TS:(s + 1) * TS], ge[:], reads=[gen], writes=[("ly", s)])
    S.barrier()
    for cm in reversed(st):
        cm.__exit__(None, None, None)


class Ex:
    def __init__(self, S, pool, ipool, sl):
        self.S, self.pool, self.ipool, self.sl = S, pool, ipool, sl

    def new(self):
        t, n = self.pool.get()
        return self.sl(t), n

    def bin(self, a, b, op, eng="dve"):
        o = self.new()
        self.S.tt(eng, o[0], a[0], b[0], op, [a[1], b[1]], [o[1]])
        return o

    def mul(self, a, b): return self.bin(a, b, ALU.mult)
    def add(self, a, b): return self.bin(a, b, ALU.add)
    def sub(self, a, b): return self.bin(a, b, ALU.subtract)

    def sc(self, a, c1, op0, c2=None, op1=None):
        o = self.new()
        self.S.ts("dve", o[0], a[0], c1, op0, [a[1]], [o[1]], s2=c2, op1=op1)
        return o

    def act(self, a, func, scale=None, bias=None):
        o = self.new()
        self.S.act(o[0], a[0], func, [a[1]], [o[1]], scale=scale, bias=bias)
        return o

    def recip(self, a):
        o = self.new()
        self.S.op("dve", lambda g: g.reciprocal(out=o[0], in_=a[0]), reads=[a[1]], writes=[o[1]])
        return o

    def frac(self, k):
        it, itn = self.ipool.get()
        it = self.sl(it)
        self.S.op("dve", lambda g: g.tensor_copy(out=it, in_=k[0]), reads=[k[1]], writes=[itn])
        kf = self.new()
        self.S.op("dve", lambda g: g.tensor_copy(out=kf[0], in_=it), reads=[itn], writes=[kf[1]])
        f = self.sub(k, kf)
        m = self.sc(f, 0.5, ALU.is_gt)
        f = self.sub(f, m)
        m = self.sc(f, -0.5, ALU.is_lt)
        f = self.add(f, m)
        return self.sc(f, 0.4999999, ALU.min, -0.4999999, ALU.max)

    def sincos(self, th):
        k = self.sc(th, 1.0 / (2.0 * np.pi), ALU.mult)
        return self.sincos_k(k)

    def sincos_k(self, k):
        s = self.act(self.frac(k), AF.Sin, scale=2.0 * np.pi)
        c = self.act(self.frac(self.sc(k, 0.25, ALU.add)), AF.Sin, scale=2.0 * np.pi)
        return s, c

    def s5_params(self, lre_in, lim, lstep):
        lre = self.sc(lre_in, -1e-4, ALU.min)
        dt = self.act(lstep, AF.Exp)
        rmag = self.act(self.mul(lre, dt), AF.Exp)
        sn, cs = self.sincos(self.mul(lim, dt))
        ar = self.mul(rmag, cs)
        ai = self.mul(rmag, sn)
        am1 = self.sc(ar, -1.0, ALU.add)
        num_r = self.add(self.mul(am1, lre), self.mul(ai, lim))
        num_i = self.sub(self.mul(ai, lre), self.mul(am1, lim))
        den = self.add(self.mul(lre, lre), self.mul(lim, lim))
        rden = self.recip(den)
        return rmag, cs, sn, self.mul(num_r, rden), self.mul(num_i, rden)


def emit_s5(nc, S, L, NB, din, dout):
    NTOK = NB * L
    T = 512
    NBLK = L // T
    su = din("su", [64, NTOK])
    s5row = din("s5row", [64, 3, 256])
    s5col = din("s5col", [128, 2, 3])
    s5bT = din("s5bT", [64, 2, 256])
    s5cT = din("s5cT", [128, 2, 2, 64])
    s5d = din("s5d", [64, 1])
    y5 = dout("y5", [64, NTOK])
    I32 = mybir.dt.int32
    st = []
    rowp = _alloc(nc, st, "sb", "s_rowp", [64, 3, 256], F32)
    colp = _alloc(nc, st, "sb", "s_colp", [128, 2, 3], F32)
    bT = _alloc(nc, st, "sb", "s_bT", [64, 2, 256], F32)
    cT = _alloc(nc, st, "sb", "s_cT", [128, 2, 2, 64], F32)
    sd = _alloc(nc, st, "sb", "s_sd", [64, 1], F32)
    BTr = _alloc(nc, st, "sb", "s_BTr", [64, 256], BF16)
    BTi = _alloc(nc, st, "sb", "s_BTi", [64, 256], BF16)
    cTr = _alloc(nc, st, "sb", "s_cTr", [128, 2, 64], BF16)
    cTi = _alloc(nc, st, "sb", "s_cTi", [128, 2, 64], BF16)
    rpool = Pool_(nc, st, "s_rp", [64, 256], 40)
    ripool = Pool_(nc, st, "s_rpi", [64, 256], 2, I32)
    cpool = Pool_(nc, st, "s_cp", [128, 2], 40)
    cipool = Pool_(nc, st, "s_cpi", [128, 2], 2, I32)
    Er = [_alloc(nc, st, "sb", "s_Er%d" % j, [128, T], F32) for j in range(2)]
    Ei = [_alloc(nc, st, "sb", "s_Ei%d" % j, [128, T], F32) for j in range(2)]
    Rt = [_alloc(nc, st, "sb", "s_Rt%d" % j, [128, T], F32) for j in range(2)]
    ttmp = _alloc(nc, st, "sb", "s_ttmp", [128, T], F32)
    wpow = [_alloc(nc, st, "sb", "s_wpow%d" % i, [128, 2, 2], F32) for i in range(2)]
    W512 = _alloc(nc, st, "sb", "s_W512", [128, 2, 2], F32)
    car = _alloc(nc, st, "sb", "s_car", [128, 2, 2 * NB], F32)
    ctmp = Pool_(nc, st, "s_ct", [128, 1], 6)
    uf = [_alloc(nc, st, "sb", "s_uf%d" % i, [64, T], F32) for i in range(2)]
    ub = [_alloc(nc, st, "sb", "s_ub%d" % i, [64, T], BF16) for i in range(2)]
    wk = Pool_(nc, st, "s_wk", [128, T], 12)
    hb = [[_alloc(nc, st, "sb", "s_hb%d%d" % (j, k), [128, T], BF16) for k in range(2)] for j in range(2)]
    yo = Pool_(nc, st, "s_yo", [64, T], 4)
    pbr = [_alloc(nc, st, "ps", "s_pbr%d" % j, [128, 512], F32) for j in range(2)]
    pbi = [_alloc(nc, st, "ps", "s_pbi%d" % j, [128, 512], F32) for j in range(2)]
    py = [_alloc(nc, st, "ps", "s_py%d" % j, [64, 512], F32) for j in range(2)]

    S.dma("sp", rowp[:], s5row, writes=["s_rowp"])
    S.dma("sp", colp[:], s5col, writes=["s_colp"])
    S.dma("sp", bT[:], s5bT, writes=["s_bT"])
    S.dma("sp", cT[:], s5cT, writes=["s_cT"])
    S.dma("sp", sd[:], s5d, writes=["s_sd"])
    ex = Ex(S, rpool, ripool, lambda t: t[:])
    _, _, _, kr, ki = ex.s5_params((rowp[:, 0, :], "s_rowp"), (rowp[:, 1, :], "s_rowp"), (rowp[:, 2, :], "s_rowp"))
    br, bi = (bT[:, 0, :], "s_bT"), (bT[:, 1, :], "s_bT")
    t1 = ex.sub(ex.mul(kr, br), ex.mul(ki, bi))
    t2 = ex.add(ex.mul(kr, bi), ex.mul(ki, br))
    S.act(BTr[:], t1[0], AF.Copy, [t1[1]], ["s_BTr"])
    S.act(BTi[:], t2[0], AF.Copy, [t2[1]], ["s_BTi"])
    S.act(cTr[:], cT[:, :, 0, :], AF.Copy, ["s_cT"], ["s_cTr"])
    S.act(cTi[:], cT[:, :, 1, :], AF.Copy, ["s_cT"], ["s_cTi"], scale=-1.0)
    ex = Ex(S, cpool, cipool, lambda t: t[:])
    rmag, cs, sn, _, _ = ex.s5_params((colp[:, :, 0], "s_colp"), (colp[:, :, 1], "s_colp"), (colp[:, :, 2], "s_colp"))
    k0 = _alloc(nc, st, "sb", "s_k0", [128, 2], F32)
    rmg = _alloc(nc, st, "sb", "s_rmg", [128, 2], F32)
    S.op("dve", lambda g: g.tensor_copy(out=rmg[:], in_=rmag[0]), reads=[rmag[1]], writes=["s_rmg"])
    dtc = ex.act((colp[:, :, 2], "s_colp"), AF.Exp)
    kk = ex.sc(ex.mul((colp[:, :, 1], "s_colp"), dtc), 1.0 / (2.0 * np.pi), ALU.mult)
    kf = ex.frac(kk)
    S.op("dve", lambda g: g.tensor_copy(out=k0[:], in_=kf[0]), reads=[kf[1]], writes=["s_k0"])
    sT, cT_ = ex.sincos_k(ex.sc((k0[:], "s_k0"), float(T), ALU.mult))
    S.op("dve", lambda g: g.tensor_copy(out=W512[:, 0, :], in_=cT_[0]), reads=[cT_[1]], writes=["s_W512"])
    S.op("dve", lambda g: g.tensor_copy(out=W512[:, 1, :], in_=sT[0]), reads=[sT[1], "s_W512"], writes=["s_W512"])
    tau = _alloc(nc, st, "sb", "s_tau", [128, T], F32)
    S.dma("sp", tau[:], din("s5tau", [128, T]), writes=["s_tau"])
    tpool = Pool_(nc, st, "s_tp", [128, T], 10)
    tipool = Pool_(nc, st, "s_tpi", [128, T], 2, I32)
    ext = Ex(S, tpool, tipool, lambda t: t[:])
    for j in range(2):
        ang = ext.new()
        S.ts("dve", ang[0], tau[:], k0[:, j:j + 1], ALU.mult, ["s_tau", "s_k0"], [ang[1]])
        sj, cj = ext.sincos_k(ang)
        S.op("dve", lambda g: g.tensor_copy(out=Er[j][:], in_=cj[0]), reads=[cj[1]], writes=["s_Er%d" % j])
        S.ts("dve", Ei[j][:], sj[0], -1.0, ALU.mult, [sj[1]], ["s_Ei%d" % j])
        S.ts("dve", Rt[j][:], tau[:], 0.0, ALU.mult, ["s_tau", "s_rmg"], ["s_Rt%d" % j], s2=rmg[:, j:j + 1], op1=ALU.add)
    S.op("dve", lambda g: g.memset(car[:], 0.0), writes=["s_car"])
    import os
    if os.environ.get("S5DBG"):
        d_er = dout("d_er", [128, T]); d_ei = dout("d_ei", [128, T]); d_rt = dout("d_rt", [128, T])
        d_col = dout("d_col", [128, 6]); d_bt = dout("d_bt", [64, 512])
        S.dma("sp", d_er, Er[0][:], reads=["s_Er0"], writes=["d_er"])
        S.dma("sp", d_ei, Ei[0][:], reads=["s_Ei0"], writes=["d_ei"])
        S.dma("sp", d_rt, Rt[0][:], reads=["s_Rt0"], writes=["d_rt"])
        S.dma("sp", d_col[:, 0:2], rmag[0], reads=[rmag[1]], writes=["d_col"])
        S.dma("sp", d_col[:, 2:4], cs[0], reads=[cs[1]], writes=["d_col2"])
        S.dma("sp", d_col[:, 4:6], sn[0], reads=[sn[1]], writes=["d_col3"])
        S.dma("sp", d_bt[:, 0:256], t1[0], reads=[t1[1]], writes=["d_bt"])
        S.dma("sp", d_bt[:, 256:512], t2[0], reads=[t2[1]], writes=["d_bt2"])

    k = 0
    for blk in range(NBLK):
        for b in range(NB):
            t0 = b * L + blk * T
            ui = k % 2
            k += 1
            ufn, ubn = "s_uf%d" % ui, "s_ub%d" % ui
            S.dma("sp", uf[ui][:], su[:, t0:t0 + T], writes=[ufn])
            S.act(ub[ui][:], uf[ui][:], AF.Copy, [ufn], [ubn])
            for j in range(2):
                js = slice(j * 128, (j + 1) * 128)
                S.mm(pbr[j][:], BTr[:, js], ub[ui][:], ["s_BTr", ubn], [("s_pbr", j)])
                S.mm(pbi[j][:], BTi[:, js], ub[ui][:], ["s_BTi", ubn], [("s_pbi", j)])
                ern, ein, rtn = "s_Er%d" % j, "s_Ei%d" % j, "s_Rt%d" % j
                a1, a1n = wk.get(); a2, a2n = wk.get(); a3, a3n = wk.get(); a4, a4n = wk.get()
                S.tt("dve", a1[:], pbr[j][:], Er[j][:], ALU.mult, [("s_pbr", j), ern], [a1n])
                S.tt("dve", a2[:], pbi[j][:], Ei[j][:], ALU.mult, [("s_pbi", j), ein], [a2n])
                S.tt("dve", a3[:], pbr[j][:], Ei[j][:], ALU.mult, [("s_pbr", j), ein], [a3n])
                S.tt("dve", a4[:], pbi[j][:], Er[j][:], ALU.mult, [("s_pbi", j), ern], [a4n])
                S.tt("pool", a1[:], a1[:], a2[:], ALU.subtract, [a1n, a2n], [a1n])
                S.tt("pool", a3[:], a3[:], a4[:], ALU.add, [a3n, a4n], [a3n])
                ci = j * NB + b
                Gr, Grn = wk.get(); Gi, Gin = wk.get()
                S.op("dve", lambda g: g.tensor_tensor_scan(out=Gr[:], data0=Rt[j][:], data1=a1[:], initial=car[:, 0, ci:ci + 1], op0=ALU.mult, op1=ALU.add),
                     reads=[rtn, a1n, "s_car"], writes=[Grn])
                S.op("dve", lambda g: g.tensor_tensor_scan(out=Gi[:], data0=Rt[j][:], data1=a3[:], initial=car[:, 1, ci:ci + 1], op0=ALU.mult, op1=ALU.add),
                     reads=[rtn, a3n, "s_car"], writes=[Gin])
                c1, c1n = ctmp.get(); c2, c2n = ctmp.get()
                Wr, Wi = W512[:, 0, j:j + 1], W512[:, 1, j:j + 1]
                S.ts("dve", c1[:], Gi[:, T - 1:T], Wi, ALU.mult, [Gin, "s_W512"], [c1n])
                S.ts("dve", c2[:], Gr[:, T - 1:T], Wi, ALU.mult, [Grn, "s_W512"], [c2n])
                S.stt(car[:, 0, ci:ci + 1], Gr[:, T - 1:T], Wr, c1[:], ALU.mult, ALU.subtract, [Grn, "s_W512", c1n, "s_car"], ["s_car"])
                S.stt(car[:, 1, ci:ci + 1], Gi[:, T - 1:T], Wr, c2[:], ALU.mult, ALU.add, [Gin, "s_W512", c2n, "s_car"], ["s_car"])
                S.tt("dve", a1[:], Gr[:], Er[j][:], ALU.mult, [Grn, ern], [a1n])
                S.tt("dve", a2[:], Gi[:], Ei[j][:], ALU.mult, [Gin, ein], [a2n])
                S.tt("dve", a3[:], Gi[:], Er[j][:], ALU.mult, [Gin, ern], [a3n])
                S.tt("dve", a4[:], Gr[:], Ei[j][:], ALU.mult, [Grn, ein], [a4n])
                S.tt("pool", hb[j][0][:], a1[:], a2[:], ALU.add, [a1n, a2n], [("s_hb", j, 0)])
                S.tt("pool", hb[j][1][:], a3[:], a4[:], ALU.subtract, [a3n, a4n], [("s_hb", j, 1)])
            pyi = k % 2
            n_ = 0
            for j in range(2):
                for c, ct in ((0, cTr), (1, cTi)):
                    S.mm(py[pyi][:], ct[:, j, :], hb[j][c][:], ["s_cTr", "s_cTi", ("s_hb", j, c)], [("s_py", pyi)], start=(n_ == 0), stop=(n_ == 3))
                    n_ += 1
            yv, yvn = yo.get(); tmp, tmpn = yo.get()
            S.stt(yv[:], uf[ui][:], sd[:, 0:1], py[pyi][:], ALU.mult, ALU.add, [ufn, "s_sd", ("s_py", pyi)], [yvn])
            emit_gelu(S, yv[:], yvn, yv[:], yvn, tmp[:], tmpn, eng2="dve")
            S.dma("sp", y5[:, t0:t0 + T], yv[:], reads=[yvn], writes=[("y5", t0)])
    S.barrier()
    for cm in reversed(st):
        cm.__exit__(None, None, None)


def emit_gdn(nc, S, L, NB, din, dout):
    NTOK = NB * L
    NT_ = NTOK // 128
    TSG = min(L, 1024)
    NSEG = L // TSG
    TPS = TSG // 128
    gq = din("gq", [128, 3, NTOK])
    gcw = din("gcw", [128, 3, 4])
    gz = din("gz", [NT_, 128, 128])
    gab = din("gab", [128, 2, NT_])
    gpar = din("gpar", [128, 2])
    gng = din("gng", [128, 128])
    gconst = din("gconst", [128, 8, 128])
    gy = dout("gy", [NT_, 128, 128])
    st = []
    A_ = lambda name, shape, dt=F32: _alloc(nc, st, "sb", name, shape, dt)
    cst = A_("g_cst", [128, 8, 128])
    ident, Ubd, Cbd, sel0, sel1, mSU, mIU, ones = [cst[:, i, :] for i in range(8)]
    cw = A_("g_cw", [128, 3, 4])
    ab = A_("g_ab", [128, 2, NT_])
    par = A_("g_par", [128, 2])
    ngb = A_("g_ngb", [128, 128])
    S.dma("sp", cst[:], gconst, writes=["g_cst"])
    S.dma("sp", cw[:], gcw, writes=["g_cw"])
    S.dma("sp", ab[:], gab, writes=["g_ab"])
    S.dma("sp", par[:], gpar, writes=["g_par"])
    S.dma("sp", ngb[:], gng, writes=["g_ngb"])
    pbig = _alloc(nc, st, "ps", "g_pbig", [128, 512], F32)
    pbanks = [_alloc(nc, st, "ps", "g_pb%d" % i, [128, 4, 128], F32) for i in range(6)]
    rslots = [[pbanks[4 + c][:, i, :] for i in range(4)] for c in range(2)]

    gp = Pool_(nc, st, "g_gp", [128, NT_], 12)
    sl = lambda t: t[:]
    beta = A_("g_beta", [128, NT_]); gc = A_("g_gc", [128, NT_]); kdsc = A_("g_kdsc", [128, NT_])
    egc = A_("g_egc", [128, NT_]); gcb = A_("g_gcb", [128, NT_]); begc = A_("g_begc", [128, NT_])
    gl = [A_("g_gl%d" % c, [128, NT_]) for c in range(2)]
    nea = A_("g_nea", [128, 1])
    S.act(beta[:], ab[:, 1, :], AF.Sigmoid, ["g_ab"], ["g_beta"])
    x_, xn_ = gp.get()
    S.ts("dve", x_[:], ab[:, 0, :], par[:, 1:2], ALU.add, ["g_ab", "g_par"], [xn_])
    sp_, spn_ = gp.get()
    emit_softplus(S, gp, x_[:], xn_, sp_[:], spn_, sl)
    S.act(nea[:], par[:, 0:1], AF.Exp, ["g_par"], ["g_nea"])
    NTP = max(NT_, 128)
    gpad = A_("g_gpad", [128, NTP])
    S.op("dve", lambda g: g.memset(gpad[:], 0.0), writes=["g_g"])
    g_ = gpad[:, 0:NT_]
    S.ts("dve", g_, sp_[:], nea[:, 0:1], ALU.mult, [spn_, "g_nea", "g_g"], ["g_g"], s2=-1.0, op1=ALU.mult)
    S.mm(pbig[:, 0:NTP], Ubd, gpad[:], ["g_cst", "g_g"], ["g_pbig"])
    S.act(gc[:], pbig[:, 0:NT_], AF.Copy, ["g_pbig"], ["g_gc"])
    S.mm(pbig[:, 0:NTP], Cbd, gpad[:], ["g_cst", "g_g"], ["g_pbig"])
    t_, tn_ = gp.get()
    S.tt("dve", t_[:], pbig[:, 0:NT_], gc[:], ALU.subtract, ["g_pbig", "g_gc"], [tn_])
    S.act(kdsc[:], t_[:], AF.Exp, [tn_], ["g_kdsc"])
    S.act(egc[:], gc[:], AF.Exp, ["g_gc"], ["g_egc"])
    S.act(t_[:], beta[:], AF.Ln, ["g_beta"], [tn_])
    S.tt("dve", gcb[:], gc[:], t_[:], ALU.add, ["g_gc", tn_], ["g_gcb"])
    S.tt("dve", begc[:], beta[:], egc[:], ALU.mult, ["g_beta", "g_egc"], ["g_begc"])
    for c, sel in ((0, sel0), (1, sel1)):
        S.mm(pbig[:, 0:NTP], sel, gpad[:], ["g_cst", "g_g"], ["g_pbig"])
        S.act(gl[c][:], pbig[:, 0:NT_], AF.Exp, ["g_pbig"], ["g_gl%d" % c])

    xin = [A_("g_xin%d" % w, [128, TSG + 3]) for w in range(3)]
    cv_ = [A_("g_cv%d" % w, [128, TSG]) for w in range(3)]
    sq = A_("g_sq", [128, TSG])
    rn = A_("g_rn", [128, 512])
    qTb = A_("g_qTb", [128, TSG], BF16)
    kTb = A_("g_kTb", [128, TSG], BF16)
    Sst = A_("g_S", [128, 128]); Sbf = A_("g_Sbf", [128, 128], BF16)
    col = Pool_(nc, st, "g_col", [128, 1], 6)
    zt = [A_("g_zt%d" % i, [128, 128]) for i in range(2)]

    import os
    G = int(os.environ.get('GDN_G', '4'))
    NOINT = bool(os.environ.get('GDN_NOINT'))
    res = [None] * G
    pending_rec = [None]
    tps = [Pool_(nc, st, "g_tp%d_" % k, [128, 128], 26) for k in range(G)]
    tpbs = [Pool_(nc, st, "g_tpb%d_" % k, [128, 128], 6, BF16) for k in range(G)]
    rtp = Pool_(nc, st, "g_rtp", [128, 128], 8)
    rtpb = Pool_(nc, st, "g_rtpb", [128, 128], 4, BF16)
    cur = {}

    def tile_setup(ti, it, k):
        tp, tpb = tps[k], tpbs[k]
        (ptk, ptv, pKK, pQK) = [pbanks[k][:, i, :] for i in range(4)]
        Bk = "g_pb%d" % k
        OLDS = bool(os.environ.get("GDN_OLDSLOTS"))
        kn_, vn_ = cv_[1], cv_[2]
        cs = slice(it * 128, (it + 1) * 128)
        tc_ = lambda a: a[:, ti:ti + 1]
        S.op("pe", lambda g: g.transpose(out=ptk[:], in_=kn_[:, cs], identity=ident), reads=["g_cv1", "g_cst"], writes=[Bk])
        kbg, kbgn = tp.get()
        S.ts("dve", kbg[:], ptk[:], tc_(begc), ALU.mult, [Bk, "g_begc"], [kbgn])
        kdec, kdecn = tpb.get()
        S.ts("dve", kdec[:], ptk[:], tc_(kdsc), ALU.mult, [Bk, "g_kdsc"], [kdecn])
        yield
        S.op("pe", lambda g: g.transpose(out=ptv[:], in_=vn_[:, cs], identity=ident), reads=["g_cv2", "g_cst"], writes=[Bk])
        bv, bvn = tp.get()
        S.ts("dve", bv[:], ptv[:], tc_(beta), ALU.mult, [Bk, "g_beta"], [bvn])
        dg1, dg1n = tp.get(); dg2, dg2n = tp.get()
        S.ts("pool", dg1[:], ident, tc_(gcb), ALU.mult, ["g_cst", "g_gcb"], [dg1n])
        S.ts("pool", dg2[:], ident, tc_(gc), ALU.mult, ["g_cst", "g_gc"], [dg2n])
        yield
        if OLDS:
            Bk = "g_pb1"
            pKK, pQK, pR1, pR2 = [pbanks[1][:, i, :] for i in range(4)]
        else:
            pKK, pQK, pR1, pR2 = ptk, ptv, pKK, pQK
        S.mm(pKK[:], kTb[:, cs], kTb[:, cs], ["g_kTb"], [Bk])
        S.mm(pQK[:], kTb[:, cs], qTb[:, cs], ["g_kTb", "g_qTb"], [Bk])
        S.mm(pR1[:], ones, dg1[:], ["g_cst", dg1n], [Bk])
        S.mm(pR2[:], ones, dg2[:], ["g_cst", dg2n], [Bk])
        E1, E1n = tp.get(); E2, E2n = tp.get()
        S.ts("dve", E1[:], pR1[:], tc_(gc), ALU.subtract, [Bk, "g_gc"], [E1n], s2=0.0, op1=ALU.min)
        S.ts("dve", E2[:], pR2[:], tc_(gc), ALU.subtract, [Bk, "g_gc"], [E2n], s2=0.0, op1=ALU.min)
        yield
        S.act(E1[:], E1[:], AF.Exp, [E1n], [E1n])
        S.act(E2[:], E2[:], AF.Exp, [E2n], [E2n])
        A0, A0n = tp.get()
        S.tt("dve", A0[:], pKK[:], E1[:], ALU.mult, [Bk, E1n], [A0n])
        S.tt("dve", E2[:], pQK[:], E2[:], ALU.mult, [Bk, E2n], [E2n])
        yield
        S.tt("pool", A0[:], A0[:], mSU, ALU.mult, [A0n, "g_cst"], [A0n])
        qkT, qkTn = tpb.get()
        S.tt("pool", qkT[:], E2[:], mIU, ALU.mult, [E2n, "g_cst"], [qkTn])
        yield
        pM, pP, pQ, pR = [pbanks[k][:, i, :] for i in range(4)]
        if OLDS:
            Bk = "g_pb2"
            pM, pP, pQ, pR = [pbanks[2][:, i, :] for i in range(4)]
        S.op("pe", lambda g: g.transpose(out=pM[:], in_=A0[:], identity=ident), reads=[A0n, "g_cst"], writes=[Bk])
        R, Rn = tp.get()
        S.tt("pool", R[:], ident, A0[:], ALU.subtract, ["g_cst", A0n], [Rn])
        M0, M0n = tp.get()
        S.act(M0[:], pM[:], AF.Copy, [Bk], [M0n])
        yield
        Pp, Ppn, Qp, Qpn = A0, A0n, M0, M0n
        for l in range(1, 6):
            S.mm(pQ[:], Pp[:], Qp[:], [Ppn, Qpn], [Bk])
            Qn_, Qnn = tp.get()
            S.act(Qn_[:], pQ[:], AF.Copy, [Bk], [Qnn])
            if l < 5:
                S.mm(pP[:], Qp[:], Pp[:], [Ppn, Qpn], [Bk])
                Pn_, Pnn = tp.get()
                S.op("dve", lambda g: g.tensor_copy(out=Pn_[:], in_=pP[:]), reads=[Bk], writes=[Pnn])
            yield
            S.mm(pR[:], Qn_[:], R[:], [Qnn, Rn], [Bk])
            R2, R2n = tp.get()
            S.tt("dve", R2[:], R[:], pR[:], ALU.add, [Rn, Bk], [R2n])
            R, Rn = R2, R2n
            Qp, Qpn = Qn_, Qnn
            if l < 5:
                Pp, Ppn = Pn_, Pnn
            yield
        pwT, pu = pM, pP
        if OLDS:
            pwT, pu = pM, pP
        S.mm(pwT[:], kbg[:], R[:], [kbgn, Rn], [Bk])
        wTb, wTbn = tpb.get()
        S.act(wTb[:], pwT[:], AF.Copy, [Bk], [wTbn])
        S.mm(pu[:], R[:], bv[:], [Rn, bvn], [Bk])
        u_, un_ = tp.get()
        S.op("dve", lambda g: g.tensor_copy(out=u_[:], in_=pu[:]), reads=[Bk], writes=[un_])
        res[k] = (kdec, kdecn, qkT, qkTn, wTb, wTbn, u_, un_)
        yield

    def recur_group(tis, its, ress):
        for ti, it, (kdec, kdecn, qkT, qkTn, wTb, wTbn, u_, un_) in zip(tis, its, ress):
            zi = ti % 2
            S.dma("sp", zt[zi][:], gz[ti], writes=["g_zt%d" % zi])
            o_, on_ = rtp.get()
            vnew, vnewn = rtpb.get()
            tmp, tmpn = rtp.get()
            for c in range(2):
                p = slice(64 * c, 64 * c + 64)
                pwS, pqS, pqv, pdS = rslots[c]
                RB = "g_pb%d" % (4 + c)
                if os.environ.get("GDN_OLDSLOTS"):
                    pwS, pqS, pqv, pdS = [pbanks[3 + c][:, i, :] for i in range(4)]
                    RB = "g_pb%d" % (3 + c)
                tk = slice(it * 128 + 64 * c, it * 128 + 64 * c + 64)
                S.mm(pwS[p, :], wTb[:, p], Sbf[:], [wTbn, "g_Sbf"], [RB])
                S.mm(pqS[p, :], qTb[:, tk], Sbf[:], ["g_qTb", "g_Sbf"], [RB])
                S.tt("dve", vnew[p, :], u_[p, :], pwS[p, :], ALU.subtract, [un_, RB], [(vnewn, c)])
                yield
                S.mm(pdS[:], kdec[p, :], vnew[p, :], [kdecn, (vnewn, c)], [RB])
                S.mm(pqv[p, :], qkT[p, p], vnew[p, :], [qkTn, (vnewn, c)], [RB])
                S.stt(Sst[:], Sst[:], gl[c][:, ti:ti + 1], pdS[:], ALU.mult, ALU.add, ["g_S", "g_gl%d" % c, RB], ["g_S"])
                S.act(Sbf[:], Sst[:], AF.Copy, ["g_S"], ["g_Sbf"])
                yield
                S.act(tmp[p, :], pqv[p, :], AF.Copy, [RB], [(tmpn, c)])
                S.stt(o_[p, :], pqS[p, :], egc[p, ti:ti + 1], tmp[p, :], ALU.mult, ALU.add, [RB, "g_egc", (tmpn, c)], [(on_, c)])
                yield
            ss, ssn = col.get()
            junk, junkn = rtp.get()
            S.op("act", lambda g: g.activation(out=junk[:], in_=o_[:], func=AF.Square, accum_out=ss[:]),
                 reads=[(on_, 0), (on_, 1)], writes=[junkn, ssn])
            S.ts("dve", ss[:], ss[:], 1.0 / 128.0, ALU.mult, [ssn], [ssn], s2=1e-6, op1=ALU.add)
            S.act(ss[:], ss[:], AF.Sqrt, [ssn], [ssn])
            S.op("dve", lambda g: g.reciprocal(out=ss[:], in_=ss[:]), reads=[ssn], writes=[ssn])
            yield
            S.act(zt[zi][:], zt[zi][:], AF.Silu, ["g_zt%d" % zi], ["g_zt%d" % zi])
            y_, yn_ = rtp.get()
            S.stt(y_[:], o_[:], ss[:, 0:1], ngb[:], ALU.mult, ALU.mult, [(on_, 0), (on_, 1), ssn, "g_ngb"], [yn_])
            S.tt("pool", y_[:], y_[:], zt[zi][:], ALU.mult, [yn_, "g_zt%d" % zi], [yn_])
            S.dma("sp", gy[ti], y_[:], reads=[yn_], writes=[("gy", ti)])
            yield

    import os
    GSTOP = float(os.environ.get("GSTOP", "9"))
    for b in range(NB if GSTOP > 1 else 0):
        S.op("dve", lambda g: g.memset(Sst[:], 0.0), writes=["g_S"])
        S.op("dve", lambda g: g.memset(Sbf[:], 0.0), writes=["g_Sbf"])
        for s in range(NSEG):
            tok0 = b * L + s * TSG
            for w in range(3):
                xn = "g_xin%d" % w
                cn = "g_cv%d" % w
                if s == 0:
                    S.op("dve", lambda g: g.memset(xin[w][:, 0:3], 0.0), writes=[xn])
                    S.dma("sp", xin[w][:, 3:3 + TSG], gq[:, w, tok0:tok0 + TSG], reads=[xn], writes=[xn])
                else:
                    S.dma("sp", xin[w][:, :], gq[:, w, tok0 - 3:tok0 + TSG], writes=[xn])
                S.ts("dve", cv_[w][:], xin[w][:, 3:3 + TSG], cw[:, w, 3:4], ALU.mult, [xn, "g_cw"], [cn])
                for k in (2, 1, 0):
                    S.stt(cv_[w][:], xin[w][:, k:k + TSG], cw[:, w, k:k + 1], cv_[w][:], ALU.mult, ALU.add, [xn, "g_cw", cn], [cn])
                S.act(cv_[w][:], cv_[w][:], AF.Silu, [cn], [cn])
            for w in range(2):
                cn = "g_cv%d" % w
                S.act(sq[:], cv_[w][:], AF.Square, [cn], ["g_sq"])
                QB = min(512, TSG)
                for q in range(TSG // QB):
                    qs = slice(q * QB, (q + 1) * QB)
                    S.mm(pbig[:, 0:QB], ones, sq[:, qs], ["g_cst", "g_sq"], ["g_pbig"])
                    if w == 0:
                        S.act(rn[:, 0:QB], pbig[:, 0:QB], AF.Sqrt, ["g_pbig"], ["g_rn"], scale=128.0, bias=128.0 * 1e-6)
                    else:
                        S.act(rn[:, 0:QB], pbig[:, 0:QB], AF.Sqrt, ["g_pbig"], ["g_rn"], scale=1.0, bias=1e-6)
                    S.op("dve", lambda g: g.reciprocal(out=rn[:, 0:QB], in_=rn[:, 0:QB]), reads=["g_rn"], writes=["g_rn"])
                    S.tt("dve", cv_[w][:, qs], cv_[w][:, qs], rn[:, 0:QB], ALU.mult, [cn, "g_rn"], [cn])
                S.act((qTb if w == 0 else kTb)[:], cv_[w][:], AF.Copy, [cn], ["g_qTb" if w == 0 else "g_kTb"])
            kn_, vn_ = cv_[1], cv_[2]
            for g0 in range(0, TPS, G):
                gens = [tile_setup(tok0 // 128 + it, it, it % G) for it in range(g0, min(g0 + G, TPS))]
                NST = int(os.environ.get("GDN_STAGES", "0"))
                if NST:
                    for gen in gens:
                        for _ in range(NST):
                            next(gen)
                    continue
                prev = pending_rec[0]
                live = ([prev] if prev is not None else []) + gens
                while live:
                    nxt = []
                    for gen in live:
                        try:
                            next(gen)
                            nxt.append(gen)
                        except StopIteration:
                            pass
                    live = nxt
                pending_rec[0] = recur_group([tok0 // 128 + it for it in range(g0, min(g0 + G, TPS))],
                                             [it for it in range(g0, min(g0 + G, TPS))], [res[it % G] for it in range(g0, min(g0 + G, TPS))])
                if NOINT:
                    for _ in pending_rec[0]:
                        pass
                    pending_rec[0] = None
            if pending_rec[0] is not None:
                for _ in pending_rec[0]:
                    pass
            pending_rec[0] = None
    S.barrier()
    for cm in reversed(st):
        cm.__exit__(None, None, None)


def gdn_consts():
    i = np.arange(128)
    same = (i[:, None] // 64) == (i[None, :] // 64)
    c = np.zeros((128, 8, 128), np.float32)
    c[:, 0] = np.eye(128)
    c[:, 1] = (i[:, None] <= i[None, :]) & same
    c[:, 2] = same
    c[:, 3] = (i[:, None] < 64) * np.ones((1, 128))
    c[:, 4] = (i[:, None] >= 64) * np.ones((1, 128))
    c[:, 5] = (i[:, None] < i[None, :]) & same
    c[:, 6] = (i[:, None] <= i[None, :]) & same
    c[:, 7] = 1.0
    return c


_PROGS = {}


def _prog(key, fn):
    if key not in _PROGS:
        _PROGS[key] = fn()
    return _PROGS[key]


def _gbpack(g, b):
    return np.ascontiguousarray(np.concatenate([g.reshape(-1, 128).T, b.reshape(-1, 128).T], axis=1), dtype=np.float32)


def _tile_gu(w, FW=256):
    w = np.asarray(w, dtype=np.float32)
    D, F = w.shape
    nfb = (F + FW - 1) // FW
    if nfb * FW != F:
        w = np.concatenate([w, np.zeros((D, nfb * FW - F), np.float32)], axis=1)
    return np.ascontiguousarray(w.reshape(D // 128, 128, nfb, FW).transpose(2, 1, 0, 3))


def _tile_d(w):
    w = np.asarray(w, dtype=np.float32)
    F, D = w.shape
    FC = F // 128
    WDG = [k for k in (11, 4, 2, 1) if FC % k == 0][0]
    return np.ascontiguousarray(w.reshape(FC // WDG, WDG, 128, D // 128, 128).transpose(3, 0, 2, 1, 4))


def _run(nc, in_maps):
    res = run_bass_kernel_spmd(nc, in_maps, core_ids=list(range(len(in_maps))))
    return res.results


def _c(a):
    return np.ascontiguousarray(a, dtype=np.float32)


def kernel(x, ffn1_w_gate, ffn1_w_up, ffn1_w_down, ln1_g, ln1_b, w_in,
           s5_lambda_re, s5_lambda_im, s5_b_re, s5_b_im, s5_c_re, s5_c_im, s5_d, s5_log_step,
           s5_w_glu, s5_b_glu, gdn_conv_w, gdn_a_log, gdn_dt_bias, gdn_norm_g,
           lru_conv_w, lru_conv_b, lru_w_a, lru_b_a, lru_w_x, lru_b_x, lru_lambda,
           w_out, ln2_g, ln2_b, ffn2_w_gate, ffn2_w_up, ffn2_w_down, ln3_g, ln3_b, depth=None):
    x = np.asarray(x)
    B, L, D = x.shape
    depth = DEPTH if depth is None else depth
    NC = N_CORES
    NTOK = B * L
    NTc = NTOK // NC
    NT_ = NTOK // 128
    NPJ = w_in.shape[-1]
    A = lambda v: np.asarray(v)
    XT = _c(x.reshape(NTOK, D).T)
    tsl = lambda i: slice(i * NTc, (i + 1) * NTc)
    p_ffn1 = _prog(("ffn", NTc, NPJ), lambda: build_ffn(NTc, NPJ=NPJ))
    p_ffn2 = _prog(("ffn", NTc, 0), lambda: build_ffn(NTc))
    p_out = _prog(("out", NTc), lambda: build_ffn(NTc, F=D_MODEL, mode="mix"))
    p_mix = _prog(("mix", L, B), lambda: build_mix(L, B))
    gconst = gdn_consts()
    s5tau = _c(np.broadcast_to(np.arange(512, dtype=np.float32), (128, 512)))
    z64 = np.zeros((64, 64), np.float32)
    for l in range(depth):
        wg, wu, wd, wi = _tile_gu(A(ffn1_w_gate[l])), _tile_gu(A(ffn1_w_up[l])), _tile_d(A(ffn1_w_down[l])), _tile_gu(A(w_in[l]))
        gb = _gbpack(A(ln1_g[l]), A(ln1_b[l]))
        r = _run(p_ffn1, [dict(xT=_c(XT[:, tsl(i)]), wg=wg, wu=wu, wd=wd, gb=gb, win=wi) for i in range(NC)])
        X1T = np.concatenate([r[i]["yT"] for i in range(NC)], axis=1)
        PJ = np.concatenate([r[i]["pjT"] for i in range(NC)], axis=1)
        del r, wg, wu, wd, wi
        lre, lim, lst = A(s5_lambda_re[l]), A(s5_lambda_im[l]), A(s5_log_step[l])
        bre, bim, cre, cim = A(s5_b_re[l]), A(s5_b_im[l]), A(s5_c_re[l]), A(s5_c_im[l])
        gcwl = A(gdn_conv_w[l])
        lcw = A(lru_conv_w[l])
        ims = []
        for c in range(NC):
            r0, r1 = 4624 + c * 64, 5136 + c * 64
            im = {}
            im["lx"] = _c(np.concatenate([PJ[r0:r0 + 64, b * L:(b + 1) * L] for b in range(B)], axis=0))
            im["lg"] = _c(np.concatenate([PJ[r1:r1 + 64, b * L:(b + 1) * L] for b in range(B)], axis=0))
            cs = slice(c * 64, (c + 1) * 64)
            cols = [lcw[0, cs], lcw[1, cs], lcw[2, cs], lcw[3, cs], A(lru_conv_b[l])[cs], A(lru_lambda[l])[cs],
                    A(lru_b_a[l])[cs], A(lru_b_x[l])[cs]]
            im["lpar"] = _c(np.stack([np.tile(v, B) for v in cols], axis=1))
            wa, wx = A(lru_w_a[l][c]), A(lru_w_x[l][c])
            im["lwa"] = _c(np.block([[wa, z64], [z64, wa]]))
            im["lwx"] = _c(np.block([[wx, z64], [z64, wx]]))
            gs = slice(4 * c, 4 * c + 4)
            im["su"] = _c(PJ[c * 64:(c + 1) * 64, :])
            row = np.stack([lre[gs].reshape(-1), lim[gs].reshape(-1), np.repeat(lst[gs], 64)])
            im["s5row"] = _c(np.broadcast_to(row[None], (64, 3, 256)))
            col = np.zeros((128, 2, 3), np.float32)
            bT = np.zeros((64, 2, 256), np.float32)
            cT = np.zeros((128, 2, 2, 64), np.float32)
            for j in range(2):
                g2 = slice(4 * c + 2 * j, 4 * c + 2 * j + 2)
                col[:, j, 0] = lre[g2].reshape(-1)
                col[:, j, 1] = lim[g2].reshape(-1)
                col[:, j, 2] = np.repeat(lst[g2], 64)
            for g in range(4):
                gg = 4 * c + g
                bT[16 * g:16 * g + 16, 0, 64 * g:64 * g + 64] = bre[gg].T
                bT[16 * g:16 * g + 16, 1, 64 * g:64 * g + 64] = bim[gg].T
                j, rr = divmod(g, 2)
                cT[64 * rr:64 * rr + 64, j, 0, 16 * g:16 * g + 16] = cre[gg].T
                cT[64 * rr:64 * rr + 64, j, 1, 16 * g:16 * g + 16] = cim[gg].T
            im["s5col"], im["s5bT"], im["s5cT"] = col, bT, cT
            im["s5d"] = _c(A(s5_d[l])[c * 64:(c + 1) * 64].reshape(64, 1))
            im["s5tau"] = s5tau
            im["gq"] = _c(np.stack([PJ[512 + w * 1024 + c * 128:512 + w * 1024 + (c + 1) * 128, :] for w in range(3)], axis=1))
            im["gcw"] = _c(np.stack([gcwl[:, w * 1024 + c * 128:w * 1024 + (c + 1) * 128].T for w in range(3)], axis=1))
            im["gz"] = _c(PJ[3584 + c * 128:3584 + (c + 1) * 128, :].T.reshape(NT_, 128, 128))
            im["gab"] = _c(np.stack([PJ[4608 + c, :].reshape(NT_, 128).T, PJ[4616 + c, :].reshape(NT_, 128).T], axis=1))
            im["gpar"] = _c(np.tile(np.array([[A(gdn_a_log[l])[c], A(gdn_dt_bias[l])[c]]], np.float32), (128, 1)))
            im["gng"] = _c(np.tile(A(gdn_norm_g[l])[None, :], (128, 1)))
            im["gconst"] = gconst
            ims.append(im)
        r = _run(p_mix, ims)
        del ims, PJ
        YMT = np.empty((D_MODEL, NTOK), np.float32)
        for c in range(NC):
            YMT[c * 64:(c + 1) * 64, :] = r[c]["y5"]
            YMT[512 + c * 128:512 + (c + 1) * 128, :] = r[c]["gy"].reshape(NTOK, 128).T
            for b in range(B):
                YMT[1536 + c * 64:1536 + (c + 1) * 64, b * L:(b + 1) * L] = r[c]["ly"][b * 64:(b + 1) * 64, :]
        del r
        wgl, wo = _c(A(s5_w_glu[l])), _tile_d(A(w_out[l]))
        bgl = _c(A(s5_b_glu[l]).reshape(-1, 128).T)
        gb = _gbpack(A(ln2_g[l]), A(ln2_b[l]))
        r = _run(p_out, [dict(xT=_c(X1T[:, tsl(i)]), ymT=_c(YMT[:, tsl(i)]), wglu=wgl, bglu=bgl, wd=wo, gb=gb) for i in range(NC)])
        X2T = np.concatenate([r[i]["yT"] for i in range(NC)], axis=1)
        del r, YMT, X1T
        wg, wu, wd = _tile_gu(A(ffn2_w_gate[l])), _tile_gu(A(ffn2_w_up[l])), _tile_d(A(ffn2_w_down[l]))
        gb = _gbpack(A(ln3_g[l]), A(ln3_b[l]))
        r = _run(p_ffn2, [dict(xT=_c(X2T[:, tsl(i)]), wg=wg, wu=wu, wd=wd, gb=gb) for i in range(NC)])
        XT = np.concatenate([r[i]["yT"] for i in range(NC)], axis=1)
        del r, wg, wu, wd, X2T
    return np.ascontiguousarray(XT.T.reshape(B, L, D)).astype(np.float32)
```

```python
import numpy as np
import concourse.bass as bass
import concourse.mybir as mybir
from concourse.bass_utils import run_bass_kernel_spmd

F32 = mybir.dt.float32
BF16 = mybir.dt.bfloat16
AF = mybir.ActivationFunctionType
ALU = mybir.AluOpType

D_MODEL = 2048
D_FF = 5632
DEPTH = 4
N_CORES = 8
ALPHA = (2.0 * DEPTH) ** 0.25
LN_EPS = 1e-5


class Sched:
    def __init__(self, nc, n_dma_sems=24):
        self.nc = nc
        self.eng = {"pe": nc.tensor, "dve": nc.vector, "act": nc.scalar, "pool": nc.gpsimd, "sp": nc.sync}
        self.sem = {}
        self.cnt = {}
        self.seen = {k: {} for k in self.eng}
        self._cms = []
        for k in self.eng:
            cm = nc.semaphore("s_" + k)
            self.sem[k] = cm.__enter__()
            self._cms.append(cm)
            self.cnt[k] = 0
        self.dma_sems = []
        for i in range(n_dma_sems):
            cm = nc.semaphore("s_dma%d" % i)
            self.dma_sems.append([cm.__enter__(), 0])
            self._cms.append(cm)
        self.dma_rr = 0
        self.last_w = {}
        self.readers = {}
        self.semobj = {k: self.sem[k] for k in self.eng}
        for i, (s, _) in enumerate(self.dma_sems):
            self.semobj["dma%d" % i] = s

    def close(self):
        for cm in reversed(self._cms):
            cm.__exit__(None, None, None)

    def _wait(self, e, semkey, val):
        if self.seen[e].get(semkey, 0) >= val:
            return
        if semkey == e and val > self.cnt[e]:
            return
        self.eng[e].wait_ge(self.semobj[semkey], val)
        self.seen[e][semkey] = val

    def _deps(self, e, reads, writes):
        for r in reads:
            w = self.last_w.get(r)
            if w is not None:
                self._wait(e, *w)
        for r in writes:
            w = self.last_w.get(r)
            if w is not None:
                self._wait(e, *w)
            for sk, v in self.readers.get(r, {}).items():
                self._wait(e, sk, v)

    def _commit(self, semkey, val, reads, writes):
        for r in reads:
            self.readers.setdefault(r, {})[semkey] = val
        for r in writes:
            self.last_w[r] = (semkey, val)
            self.readers[r] = {}

    def op(self, e, fn, reads=(), writes=(), inc=True):
        self._deps(e, reads, writes)
        ins = fn(self.eng[e])
        if inc:
            self.cnt[e] += 1
            ins.then_inc(self.sem[e], 1)
            self._commit(e, self.cnt[e], reads, writes)
        else:
            self._commit(e, self.cnt[e] + 1, reads, writes)

    def dma(self, q, out, in_, reads=(), writes=()):
        i = self.dma_rr
        self.dma_rr = (self.dma_rr + 1) % len(self.dma_sems)
        sk = "dma%d" % i
        self._wait(q, sk, self.dma_sems[i][1])
        self._deps(q, reads, writes)
        ins = self.eng[q].dma_start(out=out, in_=in_)
        self.dma_sems[i][1] += 16
        ins.then_inc(self.dma_sems[i][0], 16)
        self._commit(sk, self.dma_sems[i][1], reads, writes)

    def finish(self, e="sp"):
        for r, (sk, v) in list(self.last_w.items()):
            self._wait(e, sk, v)

    def barrier(self):
        tot = {k: self.cnt[k] for k in self.eng}
        for i, (s, v) in enumerate(self.dma_sems):
            tot["dma%d" % i] = v
        for e in self.eng:
            for sk, v in tot.items():
                if v > 0:
                    self._wait(e, sk, v)
        self.last_w = {}
        self.readers = {}

    def tt(self, e, out, a, b, op, r, w):
        self.op(e, lambda g: g.tensor_tensor(out=out, in0=a, in1=b, op=op), reads=r, writes=w)

    def ts(self, e, out, a, s1, op0, r, w, s2=None, op1=None):
        if op1 is None:
            self.op(e, lambda g: g.tensor_scalar(out=out, in0=a, scalar1=s1, scalar2=None, op0=op0), reads=r, writes=w)
        else:
            self.op(e, lambda g: g.tensor_scalar(out=out, in0=a, scalar1=s1, scalar2=s2, op0=op0, op1=op1), reads=r, writes=w)

    def stt(self, out, a, s, b, op0, op1, r, w):
        self.op("dve", lambda g: g.scalar_tensor_tensor(out=out, in0=a, scalar=s, in1=b, op0=op0, op1=op1), reads=r, writes=w)

    def act(self, out, a, func, r, w, scale=None, bias=None):
        kw = {}
        if scale is not None:
            kw["scale"] = scale
        if bias is not None:
            kw["bias"] = bias
        self.op("act", lambda g: g.activation(out=out, in_=a, func=func, **kw), reads=r, writes=w)

    def mm(self, out, lhsT, rhs, r, w, start=True, stop=True, inc=None):
        self.op("pe", lambda g: g.matmul(out, lhsT=lhsT, rhs=rhs, start=start, stop=stop), reads=r, writes=w,
                inc=(stop if inc is None else inc))


def _alloc(nc, stack, kind, name, shape, dt):
    cm = (nc.sbuf_tensor if kind == "sb" else nc.psum_tensor)(name, shape, dt)
    t = cm.__enter__()
    stack.append(cm)
    return t


def build_ffn(NT, D=D_MODEL, F=D_FF, TB=None, mode="ffn", NPJ=0, GW=512):
    nc = bass.Bass("TRN2", target_bir_lowering=False)
    TB = TB or min(512, NT)
    QW = min(512, TB)
    rscale = 0.5 if mode == "ffn" else 1.0
    DC, FC = D // 128, F // 128
    FW = 256
    NFB = F // FW
    FPB = FW // 128
    xT = nc.dram_tensor("xT", [D, NT], F32, kind="ExternalInput").ap()
    if mode == "ffn":
        wg = nc.dram_tensor("wg", [NFB, 128, DC, FW], F32, kind="ExternalInput").ap()
        wu = nc.dram_tensor("wu", [NFB, 128, DC, FW], F32, kind="ExternalInput").ap()
    else:
        ymT = nc.dram_tensor("ymT", [F, NT], F32, kind="ExternalInput").ap()
        wglu = nc.dram_tensor("wglu", [GW, GW], F32, kind="ExternalInput").ap()
        bglu = nc.dram_tensor("bglu", [128, GW // 128], F32, kind="ExternalInput").ap()
        ymT_v = ymT.rearrange("(c p) t -> p c t", p=128)
        wglu_v = wglu.rearrange("(c p) f -> p c f", p=128)
    if NPJ:
        NPB = (NPJ + FW - 1) // FW
        win = nc.dram_tensor("win", [NPB, 128, DC, FW], F32, kind="ExternalInput").ap()
        pjT = nc.dram_tensor("pjT", [NPJ, NT], F32, kind="ExternalOutput").ap()
    WDG = [w for w in (11, 4, 2, 1) if FC % w == 0][0]
    NDG = FC // WDG
    wd = nc.dram_tensor("wd", [DC, NDG, 128, WDG, 128], F32, kind="ExternalInput").ap()
    NTB = NT // TB
    CACHE = False
    if CACHE:
        wd_c = nc.dram_tensor("wd_c", [DC, NDG, 128, WDG, 128], BF16, kind="Internal").ap()
        if mode == "ffn":
            wg_c = nc.dram_tensor("wg_c", [NFB, 128, DC, FW], BF16, kind="Internal").ap()
            wu_c = nc.dram_tensor("wu_c", [NFB, 128, DC, FW], BF16, kind="Internal").ap()
        if NPJ:
            win_c = nc.dram_tensor("win_c", [(NPJ + FW - 1) // FW, 128, DC, FW], BF16, kind="Internal").ap()

    def wload(dst, dstn, src32, cache, ckey, tb):
        if not CACHE:
            S.dma("pool", dst, src32, writes=[dstn])
        elif tb == 0:
            S.dma("pool", dst, src32, writes=[dstn])
            S.dma("sp", cache, dst, reads=[dstn], writes=[ckey])
        else:
            S.dma("sp", dst, cache, reads=[ckey], writes=[dstn])
    gb = nc.dram_tensor("gb", [128, 2 * DC], F32, kind="ExternalInput").ap()
    yT = nc.dram_tensor("yT", [D, NT], F32, kind="ExternalOutput").ap()
    xT_v = xT.rearrange("(c p) t -> p c t", p=128)
    yT_v = yT.rearrange("(c p) t -> p c t", p=128)

    st = []
    S = Sched(nc)
    xb = _alloc(nc, st, "sb", "xb", [128, DC, TB], BF16)
    aT = _alloc(nc, st, "sb", "aT", [128, FC, TB], BF16)
    rT = _alloc(nc, st, "sb", "rT", [128, DC, TB], F32)
    wgs = [_alloc(nc, st, "sb", "wgs%d" % i, [128, DC, FW], BF16) for i in range(2)]
    wus = [_alloc(nc, st, "sb", "wus%d" % i, [128, DC, FW], BF16) for i in range(2)]
    wds = [_alloc(nc, st, "sb", "wds%d" % i, [128, WDG, 128], BF16) for i in range(3)]
    gbs = _alloc(nc, st, "sb", "gbs", [128, 2 * DC], F32)
    ones = _alloc(nc, st, "sb", "ones", [128, 128], F32)
    sg = [_alloc(nc, st, "sb", "sg%d" % i, [128, 512], F32) for i in range(2)]
    GC = GW // 128
    if mode == "mix":
        y5b = _alloc(nc, st, "sb", "y5b", [128, GC, TB], BF16)
        y5f = _alloc(nc, st, "sb", "y5f", [128, GC, TB], F32)
        wgl = _alloc(nc, st, "sb", "wgl", [128, GC, GW], BF16)
        bgl = _alloc(nc, st, "sb", "bgl", [128, GC], F32)
    if NPJ:
        ybf = _alloc(nc, st, "sb", "ybf", [128, DC, TB], BF16)
        po = [_alloc(nc, st, "sb", "po%d" % i, [128, TB], F32) for i in range(2)]
    xf = [_alloc(nc, st, "sb", "xf%d" % i, [128, TB], F32) for i in range(2)]
    sq = [_alloc(nc, st, "sb", "sq%d" % i, [128, TB], F32) for i in range(2)]
    mean = _alloc(nc, st, "sb", "mean", [128, TB], F32)
    rstd = _alloc(nc, st, "sb", "rstd", [128, TB], F32)
    yo = [_alloc(nc, st, "sb", "yo%d" % i, [128, TB], F32) for i in range(2)]
    pg = [_alloc(nc, st, "ps", "pg%d" % i, [128, 512], F32) for i in range(2)]
    pu = [_alloc(nc, st, "ps", "pu%d" % i, [128, 512], F32) for i in range(2)]
    pd = [_alloc(nc, st, "ps", "pd%d" % i, [128, 512], F32) for i in range(2)]
    pm = _alloc(nc, st, "ps", "pm", [128, 512], F32)
    pv = _alloc(nc, st, "ps", "pv", [128, 512], F32)
    NQ = TB // QW

    S.dma("sp", gbs[:], gb, writes=["gbs"])
    S.op("dve", lambda e: e.memset(ones[:], 1.0 / D), writes=["ones"])

    kgu = 0
    kd = 0
    kx = 0
    ko = 0
    for tb in range(NT // TB):
        t0 = tb * TB
        S.dma("pool", xb[:], xT_v[:, :, t0:t0 + TB], writes=["xb"])
        if mode == "mix":
            if tb == 0:
                S.dma("pool", wgl[:], wglu_v, writes=["wgl"])
                S.dma("sp", bgl[:], bglu, writes=["bgl"])
            S.dma("pool", y5b[:], ymT_v[:, 0:GC, t0:t0 + TB], writes=["y5b"])
            S.dma("sp", y5f[:], ymT_v[:, 0:GC, t0:t0 + TB], writes=["y5f"])
            for q in range(NQ):
                S.dma("pool", aT[:, GC:FC, q * QW:(q + 1) * QW], ymT_v[:, GC:FC, t0 + q * QW:t0 + (q + 1) * QW],
                      writes=[("aT", fc, q) for fc in range(GC, FC)])
            for jc in range(GC):
                for q in range(NQ):
                    pb = kgu % 2
                    kgu += 1
                    for ic in range(GC):
                        S.mm(pg[pb][:, 0:QW], wgl[:, ic, jc * 128:(jc + 1) * 128], y5b[:, ic, q * QW:(q + 1) * QW],
                             ["wgl", "y5b"], [("pg", pb)], start=(ic == 0), stop=(ic == GC - 1))
                    S.act(sg[pb][:, 0:QW], pg[pb][:, 0:QW], AF.Sigmoid, [("pg", pb), "bgl"], [("sg", pb)], bias=bgl[:, jc:jc + 1])
                    S.tt("dve", aT[:, jc, q * QW:(q + 1) * QW], sg[pb][:, 0:QW], y5f[:, jc, q * QW:(q + 1) * QW], ALU.mult,
                         [("sg", pb), "y5f"], [("aT", jc, q)])
        if mode == "ffn":
            for fb in range(NFB):
                b = fb % 2
                wload(wgs[b][:], ("wg", b), wg[fb], wg_c[fb] if CACHE else None, ("wg_c", fb), tb)
                wload(wus[b][:], ("wu", b), wu[fb], wu_c[fb] if CACHE else None, ("wu_c", fb), tb)
                for fi in range(FPB):
                    fc = fb * FPB + fi
                    for q in range(NQ):
                        pb = kgu % 2
                        kgu += 1
                        for c in range(DC):
                            S.op("pe", lambda e, c=c: e.matmul(pg[pb][:, 0:QW], lhsT=wgs[b][:, c, fi * 128:(fi + 1) * 128],
                                                             rhs=xb[:, c, q * QW:(q + 1) * QW], start=(c == 0), stop=(c == DC - 1)),
                                 reads=[("wg", b), "xb"], writes=[("pg", pb)], inc=(c == DC - 1))
                        for c in range(DC):
                            S.op("pe", lambda e, c=c: e.matmul(pu[pb][:, 0:QW], lhsT=wus[b][:, c, fi * 128:(fi + 1) * 128],
                                                             rhs=xb[:, c, q * QW:(q + 1) * QW], start=(c == 0), stop=(c == DC - 1)),
                                 reads=[("wu", b), "xb"], writes=[("pu", pb)], inc=(c == DC - 1))
                        S.op("act", lambda e: e.activation(out=sg[pb][:, 0:QW], in_=pg[pb][:, 0:QW], func=AF.Silu),
                             reads=[("pg", pb)], writes=[("sg", pb)])
                        S.op("dve", lambda e: e.tensor_tensor(out=aT[:, fc, q * QW:(q + 1) * QW], in0=sg[pb][:, 0:QW], in1=pu[pb][:, 0:QW], op=ALU.mult),
                             reads=[("sg", pb), ("pu", pb)], writes=[("aT", fc, q)])
        for dco in range(DC):
            xb_ = kx % 2
            kx += 1
            S.dma("sp", xf[xb_][:], xT_v[:, dco, t0:t0 + TB], writes=[("xf", xb_)])
            for q in range(NQ):
                pb = kd % 2
                for g in range(NDG):
                    wb = kd % 3 if False else (kd * NDG + g) % 3
                    wload(wds[wb][:], ("wd", wb), wd[dco, g], wd_c[dco, g] if CACHE else None, ("wd_c", dco, g), tb)
                    for j in range(WDG):
                        fc = g * WDG + j
                        S.op("pe", lambda e, fc=fc, j=j: e.matmul(pd[pb][:, 0:QW], lhsT=wds[wb][:, j, :], rhs=aT[:, fc, q * QW:(q + 1) * QW],
                                                                 start=(fc == 0), stop=(fc == FC - 1)),
                             reads=[("wd", wb), ("aT", fc, q)], writes=[("pd", pb)], inc=(j == WDG - 1))
                kd += 1
                S.op("act", lambda e: e.activation(out=rT[:, dco, q * QW:(q + 1) * QW], in_=pd[pb][:, 0:QW], func=AF.Copy, scale=rscale),
                     reads=[("pd", pb)], writes=[("rT", dco, q)])
                S.op("dve", lambda e: e.scalar_tensor_tensor(out=rT[:, dco, q * QW:(q + 1) * QW], in0=xf[xb_][:, q * QW:(q + 1) * QW],
                                                           scalar=ALPHA, in1=rT[:, dco, q * QW:(q + 1) * QW], op0=ALU.mult, op1=ALU.add),
                     reads=[("xf", xb_), ("rT", dco, q)], writes=[("rT", dco, q)])
        for q in range(NQ):
            qs = slice(q * QW, (q + 1) * QW)
            for c in range(DC):
                S.op("pe", lambda e, c=c: e.matmul(pm[:, 0:QW], lhsT=ones[:], rhs=rT[:, c, qs], start=(c == 0), stop=(c == DC - 1)),
                     reads=["ones", ("rT", c, q)], writes=["pm"], inc=(c == DC - 1))
            S.op("act", lambda e: e.activation(out=mean[:, qs], in_=pm[:, 0:QW], func=AF.Copy), reads=["pm"], writes=[("mean", q)])
            for c in range(DC):
                S.op("dve", lambda e, c=c: e.tensor_tensor(out=rT[:, c, qs], in0=rT[:, c, qs], in1=mean[:, qs], op=ALU.subtract),
                     reads=[("rT", c, q), ("mean", q)], writes=[("rT", c, q)])
                sb_ = c % 2
                S.op("act", lambda e, c=c: e.activation(out=sq[sb_][:, qs], in_=rT[:, c, qs], func=AF.Square),
                     reads=[("rT", c, q)], writes=[("sq", sb_, q)])
                S.op("pe", lambda e, c=c: e.matmul(pv[:, 0:QW], lhsT=ones[:], rhs=sq[sb_][:, qs], start=(c == 0), stop=(c == DC - 1)),
                     reads=["ones", ("sq", sb_, q)], writes=["pv"])
            S.op("act", lambda e: e.activation(out=rstd[:, qs], in_=pv[:, 0:QW], func=AF.Sqrt, bias=LN_EPS, scale=1.0), reads=["pv"], writes=[("rstd", q)])
            S.op("dve", lambda e: e.reciprocal(out=rstd[:, qs], in_=rstd[:, qs]), reads=[("rstd", q)], writes=[("rstd", q)])
        for c in range(DC):
            ob = ko % 2
            ko += 1
            for q in range(NQ):
                qs = slice(q * QW, (q + 1) * QW)
                S.op("dve", lambda e: e.tensor_tensor(out=yo[ob][:, qs], in0=rT[:, c, qs], in1=rstd[:, qs], op=ALU.mult),
                     reads=[("rT", c, q), ("rstd", q)], writes=[("yo", ob)])
            S.op("pool", lambda e: e.tensor_scalar(out=yo[ob][:], in0=yo[ob][:], scalar1=gbs[:, c:c + 1], scalar2=gbs[:, DC + c:DC + c + 1],
                                                   op0=ALU.mult, op1=ALU.add),
                 reads=[("yo", ob), "gbs"], writes=[("yo", ob)])
            S.dma("sp", yT_v[:, c, t0:t0 + TB], yo[ob][:], reads=[("yo", ob)], writes=[("yT", c, tb)])
            if NPJ:
                S.act(ybf[:, c, :], yo[ob][:], AF.Copy, [("yo", ob)], [("ybf", c)])
        if NPJ:
            kp = 0
            for pb0 in range(0, NPJ, FW):
                pw = min(FW, NPJ - pb0)
                b = (pb0 // FW) % 2
                wload(wgs[b][:], ("wg", b), win[pb0 // FW], win_c[pb0 // FW] if CACHE else None, ("win_c", pb0 // FW), tb)
                for pc in range(0, pw, 128):
                    m = min(128, pw - pc)
                    for q in range(NQ):
                        pb = kp % 2
                        kp += 1
                        for c in range(DC):
                            S.mm(pg[pb][0:m, 0:QW], wgs[b][:, c, pc:pc + m], ybf[:, c, q * QW:(q + 1) * QW],
                                 [("wg", b)] + ([("ybf", cc) for cc in range(DC)] if c == 0 else []), [("pg", pb)], start=(c == 0), stop=(c == DC - 1))
                        S.act(po[pb][0:m, q * QW:(q + 1) * QW], pg[pb][0:m, 0:QW], AF.Copy, [("pg", pb)], [("po", pb)])
                        S.dma("sp", pjT[pb0 + pc:pb0 + pc + m, t0 + q * QW:t0 + (q + 1) * QW], po[pb][0:m, q * QW:(q + 1) * QW],
                              reads=[("po", pb)], writes=[("pjT", pb0 + pc, tb, q)])
    S.finish("sp")
    S.close()
    for cm in reversed(st):
        cm.__exit__(None, None, None)
    return nc


class Pool_:
    def __init__(self, nc, st, prefix, shape, n, dt=F32):
        self.tiles = [_alloc(nc, st, "sb", "%s%d" % (prefix, i), shape, dt) for i in range(n)]
        self.names = ["%s%d" % (prefix, i) for i in range(n)]
        self.k = 0

    def get(self):
        i = self.k % len(self.tiles)
        self.k += 1
        return self.tiles[i], self.names[i]


def emit_softplus(S, P, x, xn, out, outn, sl):
    ax, axn = P.get()
    S.act(sl(ax), x, AF.Abs, [xn], [axn])
    y, yn = P.get()
    S.act(sl(y), sl(ax), AF.Exp, [axn], [yn], scale=-1.0)
    t, tn = P.get()
    S.ts("dve", sl(t), sl(y), 2.0, ALU.add, [yn], [tn])
    S.op("dve", lambda g: g.reciprocal(out=sl(t), in_=sl(t)), reads=[tn], writes=[tn])
    s, sn = P.get()
    S.tt("dve", sl(s), sl(y), sl(t), ALU.mult, [yn, tn], [sn])
    s2, s2n = P.get()
    S.tt("dve", sl(s2), sl(s), sl(s), ALU.mult, [sn], [s2n])
    p, pn = P.get()
    S.ts("dve", sl(p), sl(s2), 1.0 / 11.0, ALU.mult, [s2n], [pn], s2=1.0 / 9.0, op1=ALU.add)
    for cst in (1.0 / 7.0, 1.0 / 5.0, 1.0 / 3.0, 1.0):
        S.tt("dve", sl(p), sl(p), sl(s2), ALU.mult, [pn, s2n], [pn])
        S.ts("dve", sl(p), sl(p), cst, ALU.add, [pn], [pn])
    S.tt("dve", sl(p), sl(p), sl(s), ALU.mult, [pn, sn], [pn])
    S.ts("dve", sl(ax), x, 0.0, ALU.max, [xn], [axn])
    S.stt(out, sl(p), 2.0, sl(ax), ALU.mult, ALU.add, [pn, axn], [outn])


def emit_gelu(S, out, outn, x, xn, tmp, tmpn, eng2="pool"):
    S.act(tmp, x, AF.Square, [xn], [tmpn])
    S.ts("dve", tmp, tmp, 0.044715, ALU.mult, [tmpn], [tmpn], s2=1.0, op1=ALU.add)
    S.tt("dve", tmp, tmp, x, ALU.mult, [tmpn, xn], [tmpn])
    S.act(tmp, tmp, AF.Sigmoid, [tmpn], [tmpn], scale=1.5957691216057308)
    S.tt(eng2, out, tmp, x, ALU.mult, [tmpn, xn], [outn])


def build_mix(L, NB=2, do_lru=True, do_s5=True, do_gdn=True):
    nc = bass.Bass("TRN2", target_bir_lowering=False)
    NTOK = NB * L
    S = Sched(nc)
    dr = {}

    def din(name, shape):
        dr[name] = nc.dram_tensor(name, shape, F32, kind="ExternalInput").ap()
        return dr[name]

    def dout(name, shape):
        dr[name] = nc.dram_tensor(name, shape, F32, kind="ExternalOutput").ap()
        return dr[name]

    if do_lru:
        emit_lru(nc, S, L, NB, din, dout)
        S.barrier()
    if do_s5:
        emit_s5(nc, S, L, NB, din, dout)
        S.barrier()
    if do_gdn:
        emit_gdn(nc, S, L, NB, din, dout)
    S.finish("sp")
    S.close()
    return nc


def emit_lru(nc, S, L, NB, din, dout):
    assert NB == 2
    TS = min(L, 2048)
    NSEG = L // TS
    lx = din("lx", [128, L])
    lg = din("lg", [128, L])
    lpar = din("lpar", [128, 8])
    lwa = din("lwa", [128, 128])
    lwx = din("lwx", [128, 128])
    ly = dout("ly", [128, L])
    st = []
    par = _alloc(nc, st, "sb", "l_par", [128, 8], F32)
    wa = _alloc(nc, st, "sb", "l_wa", [128, 128], BF16)
    wx = _alloc(nc, st, "sb", "l_wx", [128, 128], BF16)
    c12 = _alloc(nc, st, "sb", "l_c12", [128, 2], F32)
    carry = _alloc(nc, st, "sb", "l_carry", [128, 1], F32)
    tiny = Pool_(nc, st, "l_tiny", [128, 1], 8)
    big = Pool_(nc, st, "l_big", [128, TS], 9)
    xin = [_alloc(nc, st, "sb", "l_xin%d" % i, [128, TS + 3], F32) for i in range(2)]
    xcb = _alloc(nc, st, "sb", "l_xcb", [128, TS], BF16)
    pr = [_alloc(nc, st, "ps", "l_pr%d" % i, [128, 512], F32) for i in range(2)]
    pi = [_alloc(nc, st, "ps", "l_pi%d" % i, [128, 512], F32) for i in range(2)]

    S.dma("sp", par[:], lpar, writes=["l_par"])
    S.dma("pool", wa[:], lwa, writes=["l_wa"])
    S.dma("pool", wx[:], lwx, writes=["l_wx"])
    nl, nln = tiny.get()
    S.ts("dve", nl[:], par[:, 5:6], -1.0, ALU.mult, ["l_par"], [nln])
    sp_, spn = tiny.get()
    emit_softplus(S, tiny, nl[:], nln, sp_[:], spn, lambda t: t[:])
    S.ts("dve", c12[:, 0:1], sp_[:], -8.0, ALU.mult, [spn], ["l_c12"])
    S.ts("dve", c12[:, 1:2], sp_[:], -16.0, ALU.mult, [spn, "l_c12"], ["l_c12"])
    S.op("dve", lambda g: g.memset(carry[:], 0.0), writes=["l_carry"])

    for s in range(NSEG):
        xi = xin[s % 2]
        xn = "l_xin%d" % (s % 2)
        if s == 0:
            S.op("dve", lambda g: g.memset(xi[:, 0:3], 0.0), writes=[xn])
            S.dma("sp", xi[:, 3:3 + TS], lx[:, 0:TS], reads=[xn], writes=[xn])
        else:
            S.dma("sp", xi[:, :], lx[:, s * TS - 3:(s + 1) * TS], writes=[xn])
        xc, xcn = big.get()
        S.ts("dve", xc[:], xi[:, 3:3 + TS], par[:, 3:4], ALU.mult, [xn, "l_par"], [xcn], s2=par[:, 4:5], op1=ALU.add)
        for k in (2, 1, 0):
            S.stt(xc[:], xi[:, k:k + TS], par[:, k:k + 1], xc[:], ALU.mult, ALU.add, [xn, "l_par", xcn], [xcn])
        S.act(xcb[:], xc[:], AF.Copy, [xcn], ["l_xcb"])
        r, rn = big.get()
        ig, ign = big.get()
        for q in range(TS // 512):
            qs = slice(q * 512, (q + 1) * 512)
            b = q % 2
            S.mm(pr[b][:], wa[:], xcb[:, qs], ["l_wa", "l_xcb"], [("l_pr", b)])
            S.mm(pi[b][:], wx[:], xcb[:, qs], ["l_wx", "l_xcb"], [("l_pi", b)])
            S.act(r[:, qs], pr[b][:], AF.Sigmoid, [("l_pr", b), "l_par"], [rn], bias=par[:, 6:7])
            S.act(ig[:, qs], pi[b][:], AF.Sigmoid, [("l_pi", b), "l_par"], [ign], bias=par[:, 7:8])
        a, an = big.get()
        S.act(a[:], r[:], AF.Exp, [rn, "l_c12"], [an], scale=c12[:, 0:1])
        s2, s2n = big.get()
        S.act(s2[:], r[:], AF.Exp, [rn, "l_c12"], [s2n], scale=c12[:, 1:2])
        S.act(s2[:], s2[:], AF.Sqrt, [s2n], [s2n], scale=-1.0, bias=1.0)
        S.tt("dve", s2[:], s2[:], ig[:], ALU.mult, [s2n, ign], [s2n])
        S.tt("dve", s2[:], s2[:], xc[:], ALU.mult, [s2n, xcn], [s2n])
        h, hn = big.get()
        S.op("dve", lambda g: g.tensor_tensor_scan(out=h[:], data0=a[:], data1=s2[:], initial=carry[:, 0:1], op0=ALU.mult, op1=ALU.add),
             reads=[an, s2n, "l_carry"], writes=[hn])
        S.op("dve", lambda g: g.tensor_copy(out=carry[:], in_=h[:, TS - 1:TS]), reads=[hn], writes=["l_carry"])
        gt, gtn = big.get()
        S.dma("sp", gt[:], lg[:, s * TS:(s + 1) * TS], writes=[gtn])
        tmp, tmpn = big.get()
        ge, gen = big.get()
        emit_gelu(S, ge[:], gen, gt[:], gtn, tmp[:], tmpn)
        S.tt("dve", ge[:], ge[:], h[:], ALU.mult, [gen, hn], [gen])
        S.dma("sp", ly[:, s * TS:(s + 1) * TS], ge[:], reads=[gen], writes=[("ly", s)])
    S.barrier()
    for cm in reversed(st):
        cm.__exit__(None, None, None)


class Ex:
    def __init__(self, S, pool, ipool, sl):
        self.S, self.pool, self.ipool, self.sl = S, pool, ipool, sl

    def new(self):
        t, n = self.pool.get()
        return self.sl(t), n

    def bin(self, a, b, op, eng="dve"):
        o = self.new()
        self.S.tt(eng, o[0], a[0], b[0], op, [a[1], b[1]], [o[1]])
        return o

    def mul(self, a, b): return self.bin(a, b, ALU.mult)
    def add(self, a, b): return self.bin(a, b, ALU.add)
    def sub(self, a, b): return self.bin(a, b, ALU.subtract)

    def sc(self, a, c1, op0, c2=None, op1=None):
        o = self.new()
        self.S.ts("dve", o[0], a[0], c1, op0, [a[1]], [o[1]], s2=c2, op1=op1)
        return o

    def act(self, a, func, scale=None, bias=None):
        o = self.new()
        self.S.act(o[0], a[0], func, [a[1]], [o[1]], scale=scale, bias=bias)
        return o

    def recip(self, a):
        o = self.new()
        self.S.op("dve", lambda g: g.reciprocal(out=o[0], in_=a[0]), reads=[a[1]], writes=[o[1]])
        return o

    def frac(self, k):
        it, itn = self.ipool.get()
        it = self.sl(it)
        self.S.op("dve", lambda g: g.tensor_copy(out=it, in_=k[0]), reads=[k[1]], writes=[itn])
        kf = self.new()
        self.S.op("dve", lambda g: g.tensor_copy(out=kf[0], in_=it), reads=[itn], writes=[kf[1]])
        f = self.sub(k, kf)
        m = self.sc(f, 0.5, ALU.is_gt)
        f = self.sub(f, m)
        m = self.sc(f, -0.5, ALU.is_lt)
        f = self.add(f, m)
        return self.sc(f, 0.4999999, ALU.min, -0.4999999, ALU.max)

    def sincos(self, th):
        k = self.sc(th, 1.0 / (2.0 * np.pi), ALU.mult)
        return self.sincos_k(k)

    def sincos_k(self, k):
        s = self.act(self.frac(k), AF.Sin, scale=2.0 * np.pi)
        c = self.act(self.frac(self.sc(k, 0.25, ALU.add)), AF.Sin, scale=2.0 * np.pi)
        return s, c

    def s5_params(self, lre_in, lim, lstep):
        lre = self.sc(lre_in, -1e-4, ALU.min)
        dt = self.act(lstep, AF.Exp)
        rmag = self.act(self.mul(lre, dt), AF.Exp)
        sn, cs = self.sincos(self.mul(lim, dt))
        ar = self.mul(rmag, cs)
        ai = self.mul(rmag, sn)
        am1 = self.sc(ar, -1.0, ALU.add)
        num_r = self.add(self.mul(am1, lre), self.mul(ai, lim))
        num_i = self.sub(self.mul(ai, lre), self.mul(am1, lim))
        den = self.add(self.mul(lre, lre), self.mul(lim, lim))
        rden = self.recip(den)
        return rmag, cs, sn, self.mul(num_r, rden), self.mul(num_i, rden)


def emit_s5(nc, S, L, NB, din, dout):
    NTOK = NB * L
    T = 512
    NBLK = L // T
    su = din("su", [64, NTOK])
    s5row = din("s5row", [64, 3, 256])
    s5col = din("s5col", [128, 2, 3])
    s5bT = din("s5bT", [64, 2, 256])
    s5cT = din("s5cT", [128, 2, 2, 64])
    s5d = din("s5d", [64, 1])
    y5 = dout("y5", [64, NTOK])
    I32 = mybir.dt.int32
    st = []
    rowp = _alloc(nc, st, "sb", "s_rowp", [64, 3, 256], F32)
    colp = _alloc(nc, st, "sb", "s_colp", [128, 2, 3], F32)
    bT = _alloc(nc, st, "sb", "s_bT", [64, 2, 256], F32)
    cT = _alloc(nc, st, "sb", "s_cT", [128, 2, 2, 64], F32)
    sd = _alloc(nc, st, "sb", "s_sd", [64, 1], F32)
    BTr = _alloc(nc, st, "sb", "s_BTr", [64, 256], BF16)
    BTi = _alloc(nc, st, "sb", "s_BTi", [64, 256], BF16)
    cTr = _alloc(nc, st, "sb", "s_cTr", [128, 2, 64], BF16)
    cTi = _alloc(nc, st, "sb", "s_cTi", [128, 2, 64], BF16)
    rpool = Pool_(nc, st, "s_rp", [64, 256], 40)
    ripool = Pool_(nc, st, "s_rpi", [64, 256], 2, I32)
    cpool = Pool_(nc, st, "s_cp", [128, 2], 40)
    cipool = Pool_(nc, st, "s_cpi", [128, 2], 2, I32)
    Er = [_alloc(nc, st, "sb", "s_Er%d" % j, [128, T], F32) for j in range(2)]
    Ei = [_alloc(nc, st, "sb", "s_Ei%d" % j, [128, T], F32) for j in range(2)]
    Rt = [_alloc(nc, st, "sb", "s_Rt%d" % j, [128, T], F32) for j in range(2)]
    ttmp = _alloc(nc, st, "sb", "s_ttmp", [128, T], F32)
    wpow = [_alloc(nc, st, "sb", "s_wpow%d" % i, [128, 2, 2], F32) for i in range(2)]
    W512 = _alloc(nc, st, "sb", "s_W512", [128, 2, 2], F32)
    car = _alloc(nc, st, "sb", "s_car", [128, 2, 2 * NB], F32)
    ctmp = Pool_(nc, st, "s_ct", [128, 1], 6)
    uf = [_alloc(nc, st, "sb", "s_uf%d" % i, [64, T], F32) for i in range(2)]
    ub = [_alloc(nc, st, "sb", "s_ub%d" % i, [64, T], BF16) for i in range(2)]
    wk = Pool_(nc, st, "s_wk", [128, T], 12)
    hb = [[_alloc(nc, st, "sb", "s_hb%d%d" % (j, k), [128, T], BF16) for k in range(2)] for j in range(2)]
    yo = Pool_(nc, st, "s_yo", [64, T], 4)
    pbr = [_alloc(nc, st, "ps", "s_pbr%d" % j, [128, 512], F32) for j in range(2)]
    pbi = [_alloc(nc, st, "ps", "s_pbi%d" % j, [128, 512], F32) for j in range(2)]
    py = [_alloc(nc, st, "ps", "s_py%d" % j, [64, 512], F32) for j in range(2)]

    S.dma("sp", rowp[:], s5row, writes=["s_rowp"])
    S.dma("sp", colp[:], s5col, writes=["s_colp"])
    S.dma("sp", bT[:], s5bT, writes=["s_bT"])
    S.dma("sp", cT[:], s5cT, writes=["s_cT"])
    S.dma("sp", sd[:], s5d, writes=["s_sd"])
    ex = Ex(S, rpool, ripool, lambda t: t[:])
    _, _, _, kr, ki = ex.s5_params((rowp[:, 0, :], "s_rowp"), (rowp[:, 1, :], "s_rowp"), (rowp[:, 2, :], "s_rowp"))
    br, bi = (bT[:, 0, :], "s_bT"), (bT[:, 1, :], "s_bT")
    t1 = ex.sub(ex.mul(kr, br), ex.mul(ki, bi))
    t2 = ex.add(ex.mul(kr, bi), ex.mul(ki, br))
    S.act(BTr[:], t1[0], AF.Copy, [t1[1]], ["s_BTr"])
    S.act(BTi[:], t2[0], AF.Copy, [t2[1]], ["s_BTi"])
    S.act(cTr[:], cT[:, :, 0, :], AF.Copy, ["s_cT"], ["s_cTr"])
    S.act(cTi[:], cT[:, :, 1, :], AF.Copy, ["s_cT"], ["s_cTi"], scale=-1.0)
    ex = Ex(S, cpool, cipool, lambda t: t[:])
    rmag, cs, sn, _, _ = ex.s5_params((colp[:, :, 0], "s_colp"), (colp[:, :, 1], "s_colp"), (colp[:, :, 2], "s_colp"))
    k0 = _alloc(nc, st, "sb", "s_k0", [128, 2], F32)
    rmg = _alloc(nc, st, "sb", "s_rmg", [128, 2], F32)
    S.op("dve", lambda g: g.tensor_copy(out=rmg[:], in_=rmag[0]), reads=[rmag[1]], writes=["s_rmg"])
    dtc = ex.act((colp[:, :, 2], "s_colp"), AF.Exp)
    kk = ex.sc(ex.mul((colp[:, :, 1], "s_colp"), dtc), 1.0 / (2.0 * np.pi), ALU.mult)
    kf = ex.frac(kk)
    S.op("dve", lambda g: g.tensor_copy(out=k0[:], in_=kf[0]), reads=[kf[1]], writes=["s_k0"])
    sT, cT_ = ex.sincos_k(ex.sc((k0[:], "s_k0"), float(T), ALU.mult))
    S.op("dve", lambda g: g.tensor_copy(out=W512[:, 0, :], in_=cT_[0]), reads=[cT_[1]], writes=["s_W512"])
    S.op("dve", lambda g: g.tensor_copy(out=W512[:, 1, :], in_=sT[0]), reads=[sT[1], "s_W512"], writes=["s_W512"])
    tau = _alloc(nc, st, "sb", "s_tau", [128, T], F32)
    S.dma("sp", tau[:], din("s5tau", [128, T]), writes=["s_tau"])
    tpool = Pool_(nc, st, "s_tp", [128, T], 10)
    tipool = Pool_(nc, st, "s_tpi", [128, T], 2, I32)
    ext = Ex(S, tpool, tipool, lambda t: t[:])
    for j in range(2):
        ang = ext.new()
        S.ts("dve", ang[0], tau[:], k0[:, j:j + 1], ALU.mult, ["s_tau", "s_k0"], [ang[1]])
        sj, cj = ext.sincos_k(ang)
        S.op("dve", lambda g: g.tensor_copy(out=Er[j][:], in_=cj[0]), reads=[cj[1]], writes=["s_Er%d" % j])
        S.ts("dve", Ei[j][:], sj[0], -1.0, ALU.mult, [sj[1]], ["s_Ei%d" % j])
        S.ts("dve", Rt[j][:], tau[:], 0.0, ALU.mult, ["s_tau", "s_rmg"], ["s_Rt%d" % j], s2=rmg[:, j:j + 1], op1=ALU.add)
    S.op("dve", lambda g: g.memset(car[:], 0.0), writes=["s_car"])
    import os
    if os.environ.get("S5DBG"):
        d_er = dout("d_er", [128, T]); d_ei = dout("d_ei", [128, T]); d_rt = dout("d_rt", [128, T])
        d_col = dout("d_col", [128, 6]); d_bt = dout("d_bt", [64, 512])
        S.dma("sp", d_er, Er[0][:], reads=["s_Er0"], writes=["d_er"])
        S.dma("sp", d_ei, Ei[0][:], reads=["s_Ei0"], writes=["d_ei"])
        S.dma("sp", d_rt, Rt[0][:], reads=["s_Rt0"], writes=["d_rt"])
        S.dma("sp", d_col[:, 0:2], rmag[0], reads=[rmag[1]], writes=["d_col"])
        S.dma("sp", d_col[:, 2:4], cs[0], reads=[cs[1]], writes=["d_col2"])
        S.dma("sp", d_col[:, 4:6], sn[0], reads=[sn[1]], writes=["d_col3"])
        S.dma("sp", d_bt[:, 0:256], t1[0], reads=[t1[1]], writes=["d_bt"])
        S.dma("sp", d_bt[:, 256:512], t2[0], reads=[t2[1]], writes=["d_bt2"])

    k = 0
    for blk in range(NBLK):
        for b in range(NB):
            t0 = b * L + blk * T
            ui = k % 2
            k += 1
            ufn, ubn = "s_uf%d" % ui, "s_ub%d" % ui
            S.dma("sp", uf[ui][:], su[:, t0:t0 + T], writes=[ufn])
            S.act(ub[ui][:], uf[ui][:], AF.Copy, [ufn], [ubn])
            for j in range(2):
                js = slice(j * 128, (j + 1) * 128)
                S.mm(pbr[j][:], BTr[:, js], ub[ui][:], ["s_BTr", ubn], [("s_pbr", j)])
                S.mm(pbi[j][:], BTi[:, js], ub[ui][:], ["s_BTi", ubn], [("s_pbi", j)])
                ern, ein, rtn = "s_Er%d" % j, "s_Ei%d" % j, "s_Rt%d" % j
                a1, a1n = wk.get(); a2, a2n = wk.get(); a3, a3n = wk.get(); a4, a4n = wk.get()
                S.tt("dve", a1[:], pbr[j][:], Er[j][:], ALU.mult, [("s_pbr", j), ern], [a1n])
                S.tt("dve", a2[:], pbi[j][:], Ei[j][:], ALU.mult, [("s_pbi", j), ein], [a2n])
                S.tt("dve", a3[:], pbr[j][:], Ei[j][:], ALU.mult, [("s_pbr", j), ein], [a3n])
                S.tt("dve", a4[:], pbi[j][:], Er[j][:], ALU.mult, [("s_pbi", j), ern], [a4n])
                S.tt("pool", a1[:], a1[:], a2[:], ALU.subtract, [a1n, a2n], [a1n])
                S.tt("pool", a3[:], a3[:], a4[:], ALU.add, [a3n, a4n], [a3n])
                ci = j * NB + b
                Gr, Grn = wk.get(); Gi, Gin = wk.get()
                S.op("dve", lambda g: g.tensor_tensor_scan(out=Gr[:], data0=Rt[j][:], data1=a1[:], initial=car[:, 0, ci:ci + 1], op0=ALU.mult, op1=ALU.add),
                     reads=[rtn, a1n, "s_car"], writes=[Grn])
                S.op("dve", lambda g: g.tensor_tensor_scan(out=Gi[:], data0=Rt[j][:], data1=a3[:], initial=car[:, 1, ci:ci + 1], op0=ALU.mult, op1=ALU.add),
                     reads=[rtn, a3n, "s_car"], writes=[Gin])
                c1, c1n = ctmp.get(); c2, c2n = ctmp.get()
                Wr, Wi = W512[:, 0, j:j + 1], W512[:, 1, j:j + 1]
                S.ts("dve", c1[:], Gi[:, T - 1:T], Wi, ALU.mult, [Gin, "s_W512"], [c1n])
                S.ts("dve", c2[:], Gr[:, T - 1:T], Wi, ALU.mult, [Grn, "s_W512"], [c2n])
                S.stt(car[:, 0, ci:ci + 1], Gr[:, T - 1:T], Wr, c1[:], ALU.mult, ALU.subtract, [Grn, "s_W512", c1n, "s_car"], ["s_car"])
                S.stt(car[:, 1, ci:ci + 1], Gi[:, T - 1:T], Wr, c2[:], ALU.mult, ALU.add, [Gin, "s_W512", c2n, "s_car"], ["s_car"])
                S.tt("dve", a1[:], Gr[:], Er[j][:], ALU.mult, [Grn, ern], [a1n])
                S.tt("dve", a2[:], Gi[:], Ei[j][:], ALU.mult, [Gin, ein], [a2n])
                S.tt("dve", a3[:], Gi[:], Er[j][:], ALU.mult, [Gin, ern], [a3n])
                S.tt("dve", a4[:], Gr[:], Ei[j][:], ALU.mult, [Grn, ein], [a4n])
                S.tt("pool", hb[j][0][:], a1[:], a2[:], ALU.add, [a1n, a2n], [("s_hb", j, 0)])
                S.tt("pool", hb[j][1][:], a3[:], a4[:], ALU.subtract, [a3n, a4n], [("s_hb", j, 1)])
            pyi = k % 2
            n_ = 0
            for j in range(2):
                for c, ct in ((0, cTr), (1, cTi)):
                    S.mm(py[pyi][:], ct[:, j, :], hb[j][c][:], ["s_cTr", "s_cTi", ("s_hb", j, c)], [("s_py", pyi)], start=(n_ == 0), stop=(n_ == 3))
                    n_ += 1
            yv, yvn = yo.get(); tmp, tmpn = yo.get()
            S.stt(yv[:], uf[ui][:], sd[:, 0:1], py[pyi][:], ALU.mult, ALU.add, [ufn, "s_sd", ("s_py", pyi)], [yvn])
            emit_gelu(S, yv[:], yvn, yv[:], yvn, tmp[:], tmpn, eng2="dve")
            S.dma("sp", y5[:, t0:t0 + T], yv[:], reads=[yvn], writes=[("y5", t0)])
    S.barrier()
    for cm in reversed(st):
        cm.__exit__(None, None, None)


def emit_gdn(nc, S, L, NB, din, dout):
    NTOK = NB * L
    NT_ = NTOK // 128
    TSG = min(L, 1024)
    NSEG = L // TSG
    TPS = TSG // 128
    gq = din("gq", [128, 3, NTOK])
    gcw = din("gcw", [128, 3, 4])
    gz = din("gz", [NT_, 128, 128])
    gab = din("gab", [128, 2, NT_])
    gpar = din("gpar", [128, 2])
    gng = din("gng", [128, 128])
    gconst = din("gconst", [128, 8, 128])
    gy = dout("gy", [NT_, 128, 128])
    st = []
    A_ = lambda name, shape, dt=F32: _alloc(nc, st, "sb", name, shape, dt)
    cst = A_("g_cst", [128, 8, 128])
    ident, Ubd, Cbd, sel0, sel1, mSU, mIU, ones = [cst[:, i, :] for i in range(8)]
    cw = A_("g_cw", [128, 3, 4])
    ab = A_("g_ab", [128, 2, NT_])
    par = A_("g_par", [128, 2])
    ngb = A_("g_ngb", [128, 128])
    S.dma("sp", cst[:], gconst, writes=["g_cst"])
    S.dma("sp", cw[:], gcw, writes=["g_cw"])
    S.dma("sp", ab[:], gab, writes=["g_ab"])
    S.dma("sp", par[:], gpar, writes=["g_par"])
    S.dma("sp", ngb[:], gng, writes=["g_ngb"])
    pbig = _alloc(nc, st, "ps", "g_pbig", [128, 512], F32)
    pbanks = [_alloc(nc, st, "ps", "g_pb%d" % i, [128, 4, 128], F32) for i in range(6)]
    rslots = [[pbanks[4 + c][:, i, :] for i in range(4)] for c in range(2)]

    gp = Pool_(nc, st, "g_gp", [128, NT_], 12)
    sl = lambda t: t[:]
    beta = A_("g_beta", [128, NT_]); gc = A_("g_gc", [128, NT_]); kdsc = A_("g_kdsc", [128, NT_])
    egc = A_("g_egc", [128, NT_]); gcb = A_("g_gcb", [128, NT_]); begc = A_("g_begc", [128, NT_])
    gl = [A_("g_gl%d" % c, [128, NT_]) for c in range(2)]
    nea = A_("g_nea", [128, 1])
    S.act(beta[:], ab[:, 1, :], AF.Sigmoid, ["g_ab"], ["g_beta"])
    x_, xn_ = gp.get()
    S.ts("dve", x_[:], ab[:, 0, :], par[:, 1:2], ALU.add, ["g_ab", "g_par"], [xn_])
    sp_, spn_ = gp.get()
    emit_softplus(S, gp, x_[:], xn_, sp_[:], spn_, sl)
    S.act(nea[:], par[:, 0:1], AF.Exp, ["g_par"], ["g_nea"])
    NTP = max(NT_, 128)
    gpad = A_("g_gpad", [128, NTP])
    S.op("dve", lambda g: g.memset(gpad[:], 0.0), writes=["g_g"])
    g_ = gpad[:, 0:NT_]
    S.ts("dve", g_, sp_[:], nea[:, 0:1], ALU.mult, [spn_, "g_nea", "g_g"], ["g_g"], s2=-1.0, op1=ALU.mult)
    S.mm(pbig[:, 0:NTP], Ubd, gpad[:], ["g_cst", "g_g"], ["g_pbig"])
    S.act(gc[:], pbig[:, 0:NT_], AF.Copy, ["g_pbig"], ["g_gc"])
    S.mm(pbig[:, 0:NTP], Cbd, gpad[:], ["g_cst", "g_g"], ["g_pbig"])
    t_, tn_ = gp.get()
    S.tt("dve", t_[:], pbig[:, 0:NT_], gc[:], ALU.subtract, ["g_pbig", "g_gc"], [tn_])
    S.act(kdsc[:], t_[:], AF.Exp, [tn_], ["g_kdsc"])
    S.act(egc[:], gc[:], AF.Exp, ["g_gc"], ["g_egc"])
    S.act(t_[:], beta[:], AF.Ln, ["g_beta"], [tn_])
    S.tt("dve", gcb[:], gc[:], t_[:], ALU.add, ["g_gc", tn_], ["g_gcb"])
    S.tt("dve", begc[:], beta[:], egc[:], ALU.mult, ["g_beta", "g_egc"], ["g_begc"])
    for c, sel in ((0, sel0), (1, sel1)):
        S.mm(pbig[:, 0:NTP], sel, gpad[:], ["g_cst", "g_g"], ["g_pbig"])
        S.act(gl[c][:], pbig[:, 0:NT_], AF.Exp, ["g_pbig"], ["g_gl%d" % c])
    gcd = nc.dram_tensor("g_gcd", [2, NTP, 128], F32, kind="Internal").ap()
    gTt = [A_("g_gT%d" % i, [128, 128]) for i in range(2)]
    for w_, (src_, srcn_) in enumerate(((gc, "g_gc"), (gcb, "g_gcb"))):
        S.op("dve", lambda g: g.memset(gpad[:], 0.0), writes=["g_g"])
        S.op("dve", lambda g: g.tensor_copy(out=gpad[:, 0:NT_], in_=src_[:]), reads=[srcn_, "g_g"], writes=["g_g"])
        S.op("pe", lambda g: g.transpose(out=pbig[:, 0:128], in_=gpad[:, 0:128], identity=ident), reads=["g_g", "g_cst"], writes=["g_pbig"])
        gT, gTn = gTt[w_], "g_gT%d" % w_
        S.act(gT[:], pbig[:, 0:128], AF.Copy, ["g_pbig"], [gTn])
        S.dma("sp", gcd[w_, 0:128, :], gT[:], reads=[gTn], writes=[("g_gcd", w_)])

    xin = [A_("g_xin%d" % w, [128, TSG + 3]) for w in range(3)]
    cv_ = [A_("g_cv%d" % w, [128, TSG]) for w in range(3)]
    sq = A_("g_sq", [128, TSG])
    rn = A_("g_rn", [128, 512])
    qTb = A_("g_qTb", [128, TSG], BF16)
    kTb = A_("g_kTb", [128, TSG], BF16)
    Sst = A_("g_S", [128, 128]); Sbf = A_("g_Sbf", [128, 128], BF16)
    col = Pool_(nc, st, "g_col", [128, 1], 6)
    zt = [A_("g_zt%d" % i, [128, 128]) for i in range(2)]

    import os
    G = int(os.environ.get('GDN_G', '4'))
    NOINT = bool(os.environ.get('GDN_NOINT'))
    res = [None] * G
    pending_rec = [None]
    tps = [Pool_(nc, st, "g_tp%d_" % k, [128, 128], 26) for k in range(G)]
    tpbs = [Pool_(nc, st, "g_tpb%d_" % k, [128, 128], 6, BF16) for k in range(G)]
    rtp = Pool_(nc, st, "g_rtp", [128, 128], 8)
    rtpb = Pool_(nc, st, "g_rtpb", [128, 128], 4, BF16)
    cur = {}

    def tile_setup(ti, it, k):
        tp, tpb = tps[k], tpbs[k]
        (ptk, ptv, pKK, pQK) = [pbanks[k][:, i, :] for i in range(4)]
        Bk = "g_pb%d" % k
        OLDS = bool(os.environ.get("GDN_OLDSLOTS"))
        kn_, vn_ = cv_[1], cv_[2]
        cs = slice(it * 128, (it + 1) * 128)
        tc_ = lambda a: a[:, ti:ti + 1]
        S.op("pe", lambda g: g.transpose(out=ptk[:], in_=kn_[:, cs], identity=ident), reads=["g_cv1", "g_cst"], writes=[Bk])
        kbg, kbgn = tp.get()
        S.ts("dve", kbg[:], ptk[:], tc_(begc), ALU.mult, [Bk, "g_begc"], [kbgn])
        kdec, kdecn = tpb.get()
        S.ts("dve", kdec[:], ptk[:], tc_(kdsc), ALU.mult, [Bk, "g_kdsc"], [kdecn])
        yield
        S.op("pe", lambda g: g.transpose(out=ptv[:], in_=vn_[:, cs], identity=ident), reads=["g_cv2", "g_cst"], writes=[Bk])
        bv, bvn = tp.get()
        S.ts("dve", bv[:], ptv[:], tc_(beta), ALU.mult, [Bk, "g_beta"], [bvn])
        dg1, dg1n = tp.get(); dg2, dg2n = tp.get()
        S.dma("sp", dg1[:], gcd[1, ti, :].partition_broadcast(128), reads=[("g_gcd", 1)], writes=[dg1n])
        S.dma("sp", dg2[:], gcd[0, ti, :].partition_broadcast(128), reads=[("g_gcd", 0)], writes=[dg2n])
        yield
        if OLDS:
            Bk = "g_pb1"
            pKK, pQK, pR1, pR2 = [pbanks[1][:, i, :] for i in range(4)]
        else:
            pKK, pQK, pR1, pR2 = ptk, ptv, pKK, pQK
        S.mm(pKK[:], kTb[:, cs], kTb[:, cs], ["g_kTb"], [Bk])
        S.mm(pQK[:], kTb[:, cs], qTb[:, cs], ["g_kTb", "g_qTb"], [Bk])
        E1, E1n = tp.get(); E2, E2n = tp.get()
        S.ts("dve", E1[:], dg1[:], tc_(gc), ALU.subtract, [dg1n, "g_gc"], [E1n], s2=0.0, op1=ALU.min)
        S.ts("dve", E2[:], dg2[:], tc_(gc), ALU.subtract, [dg2n, "g_gc"], [E2n], s2=0.0, op1=ALU.min)
        yield
        S.act(E1[:], E1[:], AF.Exp, [E1n], [E1n])
        S.act(E2[:], E2[:], AF.Exp, [E2n], [E2n])
        A0, A0n = tp.get()
        S.tt("dve", A0[:], pKK[:], E1[:], ALU.mult, [Bk, E1n], [A0n])
        S.tt("dve", E2[:], pQK[:], E2[:], ALU.mult, [Bk, E2n], [E2n])
        yield
        S.tt("pool", A0[:], A0[:], mSU, ALU.mult, [A0n, "g_cst"], [A0n])
        qkT, qkTn = tpb.get()
        S.tt("pool", qkT[:], E2[:], mIU, ALU.mult, [E2n, "g_cst"], [qkTn])
        yield
        pM, pP, pQ, pR = [pbanks[k][:, i, :] for i in range(4)]
        if OLDS:
            Bk = "g_pb2"
            pM, pP, pQ, pR = [pbanks[2][:, i, :] for i in range(4)]
        S.op("pe", lambda g: g.transpose(out=pM[:], in_=A0[:], identity=ident), reads=[A0n, "g_cst"], writes=[Bk])
        R, Rn = tp.get()
        S.tt("pool", R[:], ident, A0[:], ALU.subtract, ["g_cst", A0n], [Rn])
        M0, M0n = tp.get()
        S.act(M0[:], pM[:], AF.Copy, [Bk], [M0n])
        yield
        Pp, Ppn, Qp, Qpn = A0, A0n, M0, M0n
        for l in range(1, 6):
            S.mm(pQ[:], Pp[:], Qp[:], [Ppn, Qpn], [Bk])
            Qn_, Qnn = tp.get()
            S.act(Qn_[:], pQ[:], AF.Copy, [Bk], [Qnn])
            if l < 5:
                S.mm(pP[:], Qp[:], Pp[:], [Ppn, Qpn], [Bk])
                Pn_, Pnn = tp.get()
                S.op("dve", lambda g: g.tensor_copy(out=Pn_[:], in_=pP[:]), reads=[Bk], writes=[Pnn])
            yield
            S.mm(pR[:], Qn_[:], R[:], [Qnn, Rn], [Bk])
            R2, R2n = tp.get()
            S.tt("dve", R2[:], R[:], pR[:], ALU.add, [Rn, Bk], [R2n])
            R, Rn = R2, R2n
            Qp, Qpn = Qn_, Qnn
            if l < 5:
                Pp, Ppn = Pn_, Pnn
            yield
        pwT, pu = pM, pP
        if OLDS:
            pwT, pu = pM, pP
        S.mm(pwT[:], kbg[:], R[:], [kbgn, Rn], [Bk])
        wTb, wTbn = tpb.get()
        S.act(wTb[:], pwT[:], AF.Copy, [Bk], [wTbn])
        S.mm(pu[:], R[:], bv[:], [Rn, bvn], [Bk])
        u_, un_ = tp.get()
        S.op("dve", lambda g: g.tensor_copy(out=u_[:], in_=pu[:]), reads=[Bk], writes=[un_])
        res[k] = (kdec, kdecn, qkT, qkTn, wTb, wTbn, u_, un_)
        yield

    def recur_group(tis, its, ress):
        for ti, it, (kdec, kdecn, qkT, qkTn, wTb, wTbn, u_, un_) in zip(tis, its, ress):
            zi = ti % 2
            S.dma("sp", zt[zi][:], gz[ti], writes=["g_zt%d" % zi])
            o_, on_ = rtp.get()
            vnew, vnewn = rtpb.get()
            tmp, tmpn = rtp.get()
            for c in range(2):
                p = slice(64 * c, 64 * c + 64)
                pwS, pqS, pqv, pdS = rslots[c]
                RB = "g_pb%d" % (4 + c)
                if os.environ.get("GDN_OLDSLOTS"):
                    pwS, pqS, pqv, pdS = [pbanks[3 + c][:, i, :] for i in range(4)]
                    RB = "g_pb%d" % (3 + c)
                tk = slice(it * 128 + 64 * c, it * 128 + 64 * c + 64)
                S.mm(pwS[p, :], wTb[:, p], Sbf[:], [wTbn, "g_Sbf"], [RB])
                S.mm(pqS[p, :], qTb[:, tk], Sbf[:], ["g_qTb", "g_Sbf"], [RB])
                S.tt("dve", vnew[p, :], u_[p, :], pwS[p, :], ALU.subtract, [un_, RB], [(vnewn, c)])
                yield
                S.mm(pdS[:], kdec[p, :], vnew[p, :], [kdecn, (vnewn, c)], [RB])
                S.mm(pqv[p, :], qkT[p, p], vnew[p, :], [qkTn, (vnewn, c)], [RB])
                S.stt(Sst[:], Sst[:], gl[c][:, ti:ti + 1], pdS[:], ALU.mult, ALU.add, ["g_S", "g_gl%d" % c, RB], ["g_S"])
                S.act(Sbf[:], Sst[:], AF.Copy, ["g_S"], ["g_Sbf"])
                yield
                S.act(tmp[p, :], pqv[p, :], AF.Copy, [RB], [(tmpn, c)])
                S.stt(o_[p, :], pqS[p, :], egc[p, ti:ti + 1], tmp[p, :], ALU.mult, ALU.add, [RB, "g_egc", (tmpn, c)], [(on_, c)])
                yield
            ss, ssn = col.get()
            junk, junkn = rtp.get()
            S.op("act", lambda g: g.activation(out=junk[:], in_=o_[:], func=AF.Square, accum_out=ss[:]),
                 reads=[(on_, 0), (on_, 1)], writes=[junkn, ssn])
            S.ts("dve", ss[:], ss[:], 1.0 / 128.0, ALU.mult, [ssn], [ssn], s2=1e-6, op1=ALU.add)
            S.act(ss[:], ss[:], AF.Sqrt, [ssn], [ssn])
            S.op("dve", lambda g: g.reciprocal(out=ss[:], in_=ss[:]), reads=[ssn], writes=[ssn])
            yield
            S.act(zt[zi][:], zt[zi][:], AF.Silu, ["g_zt%d" % zi], ["g_zt%d" % zi])
            y_, yn_ = rtp.get()
            S.stt(y_[:], o_[:], ss[:, 0:1], ngb[:], ALU.mult, ALU.mult, [(on_, 0), (on_, 1), ssn, "g_ngb"], [yn_])
            S.tt("pool", y_[:], y_[:], zt[zi][:], ALU.mult, [yn_, "g_zt%d" % zi], [yn_])
            S.dma("sp", gy[ti], y_[:], reads=[yn_], writes=[("gy", ti)])
            yield

    import os
    GSTOP = float(os.environ.get("GSTOP", "9"))
    for b in range(NB if GSTOP > 1 else 0):
        S.op("dve", lambda g: g.memset(Sst[:], 0.0), writes=["g_S"])
        S.op("dve", lambda g: g.memset(Sbf[:], 0.0), writes=["g_Sbf"])
        for s in range(NSEG):
            tok0 = b * L + s * TSG
            for w in range(3):
                xn = "g_xin%d" % w
                cn = "g_cv%d" % w
                if s == 0:
                    S.op("dve", lambda g: g.memset(xin[w][:, 0:3], 0.0), writes=[xn])
                    S.dma("sp", xin[w][:, 3:3 + TSG], gq[:, w, tok0:tok0 + TSG], reads=[xn], writes=[xn])
                else:
                    S.dma("sp", xin[w][:, :], gq[:, w, tok0 - 3:tok0 + TSG], writes=[xn])
                S.ts("dve", cv_[w][:], xin[w][:, 3:3 + TSG], cw[:, w, 3:4], ALU.mult, [xn, "g_cw"], [cn])
                for k in (2, 1, 0):
                    S.stt(cv_[w][:], xin[w][:, k:k + TSG], cw[:, w, k:k + 1], cv_[w][:], ALU.mult, ALU.add, [xn, "g_cw", cn], [cn])
                S.act(cv_[w][:], cv_[w][:], AF.Silu, [cn], [cn])
            for w in range(2):
                cn = "g_cv%d" % w
                S.act(sq[:], cv_[w][:], AF.Square, [cn], ["g_sq"])
                QB = min(512, TSG)
                for q in range(TSG // QB):
                    qs = slice(q * QB, (q + 1) * QB)
                    S.mm(pbig[:, 0:QB], ones, sq[:, qs], ["g_cst", "g_sq"], ["g_pbig"])
                    if w == 0:
                        S.act(rn[:, 0:QB], pbig[:, 0:QB], AF.Sqrt, ["g_pbig"], ["g_rn"], scale=128.0, bias=128.0 * 1e-6)
                    else:
                        S.act(rn[:, 0:QB], pbig[:, 0:QB], AF.Sqrt, ["g_pbig"], ["g_rn"], scale=1.0, bias=1e-6)
                    S.op("dve", lambda g: g.reciprocal(out=rn[:, 0:QB], in_=rn[:, 0:QB]), reads=["g_rn"], writes=["g_rn"])
                    S.tt("dve", cv_[w][:, qs], cv_[w][:, qs], rn[:, 0:QB], ALU.mult, [cn, "g_rn"], [cn])
                S.act((qTb if w == 0 else kTb)[:], cv_[w][:], AF.Copy, [cn], ["g_qTb" if w == 0 else "g_kTb"])
            kn_, vn_ = cv_[1], cv_[2]
            for g0 in range(0, TPS, G):
                gens = [tile_setup(tok0 // 128 + it, it, it % G) for it in range(g0, min(g0 + G, TPS))]
                NST = int(os.environ.get("GDN_STAGES", "0"))
                if NST:
                    for gen in gens:
                        for _ in range(NST):
                            next(gen)
                    continue
                prev = pending_rec[0]
                live = ([prev] if prev is not None else []) + gens
                while live:
                    nxt = []
                    for gen in live:
                        try:
                            next(gen)
                            nxt.append(gen)
                        except StopIteration:
                            pass
                    live = nxt
                pending_rec[0] = recur_group([tok0 // 128 + it for it in range(g0, min(g0 + G, TPS))],
                                             [it for it in range(g0, min(g0 + G, TPS))], [res[it % G] for it in range(g0, min(g0 + G, TPS))])
                if NOINT:
                    for _ in pending_rec[0]:
                        pass
                    pending_rec[0] = None
            if pending_rec[0] is not None:
                for _ in pending_rec[0]:
                    pass
            pending_rec[0] = None
    S.barrier()
    for cm in reversed(st):
        cm.__exit__(None, None, None)


def gdn_consts():
    i = np.arange(128)
    same = (i[:, None] // 64) == (i[None, :] // 64)
    c = np.zeros((128, 8, 128), np.float32)
    c[:, 0] = np.eye(128)
    c[:, 1] = (i[:, None] <= i[None, :]) & same
    c[:, 2] = same
    c[:, 3] = (i[:, None] < 64) * np.ones((1, 128))
    c[:, 4] = (i[:, None] >= 64) * np.ones((1, 128))
    c[:, 5] = (i[:, None] < i[None, :]) & same
    c[:, 6] = (i[:, None] <= i[None, :]) & same
    c[:, 7] = 1.0
    return c


_PROGS = {}


def _prog(key, fn):
    if key not in _PROGS:
        _PROGS[key] = fn()
    return _PROGS[key]


def _gbpack(g, b):
    return np.ascontiguousarray(np.concatenate([g.reshape(-1, 128).T, b.reshape(-1, 128).T], axis=1), dtype=np.float32)


def _tile_gu(w, FW=256):
    w = np.asarray(w, dtype=np.float32)
    D, F = w.shape
    nfb = (F + FW - 1) // FW
    if nfb * FW != F:
        w = np.concatenate([w, np.zeros((D, nfb * FW - F), np.float32)], axis=1)
    return np.ascontiguousarray(w.reshape(D // 128, 128, nfb, FW).transpose(2, 1, 0, 3))


def _tile_d(w):
    w = np.asarray(w, dtype=np.float32)
    F, D = w.shape
    FC = F // 128
    WDG = [k for k in (11, 4, 2, 1) if FC % k == 0][0]
    return np.ascontiguousarray(w.reshape(FC // WDG, WDG, 128, D // 128, 128).transpose(3, 0, 2, 1, 4))


def _run(nc, in_maps):
    res = run_bass_kernel_spmd(nc, in_maps, core_ids=list(range(len(in_maps))))
    return res.results


def _c(a):
    return np.ascontiguousarray(a, dtype=np.float32)


def kernel(x, ffn1_w_gate, ffn1_w_up, ffn1_w_down, ln1_g, ln1_b, w_in,
           s5_lambda_re, s5_lambda_im, s5_b_re, s5_b_im, s5_c_re, s5_c_im, s5_d, s5_log_step,
           s5_w_glu, s5_b_glu, gdn_conv_w, gdn_a_log, gdn_dt_bias, gdn_norm_g,
           lru_conv_w, lru_conv_b, lru_w_a, lru_b_a, lru_w_x, lru_b_x, lru_lambda,
           w_out, ln2_g, ln2_b, ffn2_w_gate, ffn2_w_up, ffn2_w_down, ln3_g, ln3_b, depth=None):
    x = np.asarray(x)
    B, L, D = x.shape
    depth = DEPTH if depth is None else depth
    NC = N_CORES
    NTOK = B * L
    NTc = NTOK // NC
    NT_ = NTOK // 128
    NPJ = w_in.shape[-1]
    A = lambda v: np.asarray(v)
    XT = _c(x.reshape(NTOK, D).T)
    tsl = lambda i: slice(i * NTc, (i + 1) * NTc)
    p_ffn1 = _prog(("ffn", NTc, NPJ), lambda: build_ffn(NTc, NPJ=NPJ))
    p_ffn2 = _prog(("ffn", NTc, 0), lambda: build_ffn(NTc))
    p_out = _prog(("out", NTc), lambda: build_ffn(NTc, F=D_MODEL, mode="mix"))
    p_mix = _prog(("mix", L, B), lambda: build_mix(L, B))
    gconst = gdn_consts()
    s5tau = _c(np.broadcast_to(np.arange(512, dtype=np.float32), (128, 512)))
    z64 = np.zeros((64, 64), np.float32)
    for l in range(depth):
        wg, wu, wd, wi = _tile_gu(A(ffn1_w_gate[l])), _tile_gu(A(ffn1_w_up[l])), _tile_d(A(ffn1_w_down[l])), _tile_gu(A(w_in[l]))
        gb = _gbpack(A(ln1_g[l]), A(ln1_b[l]))
        r = _run(p_ffn1, [dict(xT=_c(XT[:, tsl(i)]), wg=wg, wu=wu, wd=wd, gb=gb, win=wi) for i in range(NC)])
        X1T = np.concatenate([r[i]["yT"] for i in range(NC)], axis=1)
        PJ = np.concatenate([r[i]["pjT"] for i in range(NC)], axis=1)
        del r, wg, wu, wd, wi
        lre, lim, lst = A(s5_lambda_re[l]), A(s5_lambda_im[l]), A(s5_log_step[l])
        bre, bim, cre, cim = A(s5_b_re[l]), A(s5_b_im[l]), A(s5_c_re[l]), A(s5_c_im[l])
        gcwl = A(gdn_conv_w[l])
        lcw = A(lru_conv_w[l])
        ims = []
        for c in range(NC):
            r0, r1 = 4624 + c * 64, 5136 + c * 64
            im = {}
            im["lx"] = _c(np.concatenate([PJ[r0:r0 + 64, b * L:(b + 1) * L] for b in range(B)], axis=0))
            im["lg"] = _c(np.concatenate([PJ[r1:r1 + 64, b * L:(b + 1) * L] for b in range(B)], axis=0))
            cs = slice(c * 64, (c + 1) * 64)
            cols = [lcw[0, cs], lcw[1, cs], lcw[2, cs], lcw[3, cs], A(lru_conv_b[l])[cs], A(lru_lambda[l])[cs],
                    A(lru_b_a[l])[cs], A(lru_b_x[l])[cs]]
            im["lpar"] = _c(np.stack([np.tile(v, B) for v in cols], axis=1))
            wa, wx = A(lru_w_a[l][c]), A(lru_w_x[l][c])
            im["lwa"] = _c(np.block([[wa, z64], [z64, wa]]))
            im["lwx"] = _c(np.block([[wx, z64], [z64, wx]]))
            gs = slice(4 * c, 4 * c + 4)
            im["su"] = _c(PJ[c * 64:(c + 1) * 64, :])
            row = np.stack([lre[gs].reshape(-1), lim[gs].reshape(-1), np.repeat(lst[gs], 64)])
            im["s5row"] = _c(np.broadcast_to(row[None], (64, 3, 256)))
            col = np.zeros((128, 2, 3), np.float32)
            bT = np.zeros((64, 2, 256), np.float32)
            cT = np.zeros((128, 2, 2, 64), np.float32)
            for j in range(2):
                g2 = slice(4 * c + 2 * j, 4 * c + 2 * j + 2)
                col[:, j, 0] = lre[g2].reshape(-1)
                col[:, j, 1] = lim[g2].reshape(-1)
                col[:, j, 2] = np.repeat(lst[g2], 64)
            for g in range(4):
                gg = 4 * c + g
                bT[16 * g:16 * g + 16, 0, 64 * g:64 * g + 64] = bre[gg].T
                bT[16 * g:16 * g + 16, 1, 64 * g:64 * g + 64] = bim[gg].T
                j, rr = divmod(g, 2)
                cT[64 * rr:64 * rr + 64, j, 0, 16 * g:16 * g + 16] = cre[gg].T
                cT[64 * rr:64 * rr + 64, j, 1, 16 * g:16 * g + 16] = cim[gg].T
            im["s5col"], im["s5bT"], im["s5cT"] = col, bT, cT
            im["s5d"] = _c(A(s5_d[l])[c * 64:(c + 1) * 64].reshape(64, 1))
            im["s5tau"] = s5tau
            im["gq"] = _c(np.stack([PJ[512 + w * 1024 + c * 128:512 + w * 1024 + (c + 1) * 128, :] for w in range(3)], axis=1))
            im["gcw"] = _c(np.stack([gcwl[:, w * 1024 + c * 128:w * 1024 + (c + 1) * 128].T for w in range(3)], axis=1))
            im["gz"] = _c(PJ[3584 + c * 128:3584 + (c + 1) * 128, :].T.reshape(NT_, 128, 128))
            im["gab"] = _c(np.stack([PJ[4608 + c, :].reshape(NT_, 128).T, PJ[4616 + c, :].reshape(NT_, 128).T], axis=1))
            im["gpar"] = _c(np.tile(np.array([[A(gdn_a_log[l])[c], A(gdn_dt_bias[l])[c]]], np.float32), (128, 1)))
            im["gng"] = _c(np.tile(A(gdn_norm_g[l])[None, :], (128, 1)))
            im["gconst"] = gconst
            ims.append(im)
        r = _run(p_mix, ims)
        del ims, PJ
        YMT = np.empty((D_MODEL, NTOK), np.float32)
        for c in range(NC):
            YMT[c * 64:(c + 1) * 64, :] = r[c]["y5"]
            YMT[512 + c * 128:512 + (c + 1) * 128, :] = r[c]["gy"].reshape(NTOK, 128).T
            for b in range(B):
                YMT[1536 + c * 64:1536 + (c + 1) * 64, b * L:(b + 1) * L] = r[c]["ly"][b * 64:(b + 1) * 64, :]
        del r
        wgl, wo = _c(A(s5_w_glu[l])), _tile_d(A(w_out[l]))
        bgl = _c(A(s5_b_glu[l]).reshape(-1, 128).T)
        gb = _gbpack(A(ln2_g[l]), A(ln2_b[l]))
        r = _run(p_out, [dict(xT=_c(X1T[:, tsl(i)]), ymT=_c(YMT[:, tsl(i)]), wglu=wgl, bglu=bgl, wd=wo, gb=gb) for i in range(NC)])
        X2T = np.concatenate([r[i]["yT"] for i in range(NC)], axis=1)
        del r, YMT, X1T
        wg, wu, wd = _tile_gu(A(ffn2_w_gate[l])), _tile_gu(A(ffn2_w_up[l])), _tile_d(A(ffn2_w_down[l]))
        gb = _gbpack(A(ln3_g[l]), A(ln3_b[l]))
        r = _run(p_ffn2, [dict(xT=_c(X2T[:, tsl(i)]), wg=wg, wu=wu, wd=wd, gb=gb) for i in range(NC)])
        XT = np.concatenate([r[i]["yT"] for i in range(NC)], axis=1)
        del r, wg, wu, wd, X2T
    return np.ascontiguousarray(XT.T.reshape(B, L, D)).astype(np.float32)
```

```python
import numpy as np
import concourse.bass as bass
import concourse.mybir as mybir
from concourse.bass_utils import run_bass_kernel_spmd

F32 = mybir.dt.float32
BF16 = mybir.dt.bfloat16
AF = mybir.ActivationFunctionType
ALU = mybir.AluOpType

D_MODEL = 2048
D_FF = 5632
DEPTH = 4
N_CORES = 8
ALPHA = (2.0 * DEPTH) ** 0.25
LN_EPS = 1e-5


class Sched:
    def __init__(self, nc, n_dma_sems=24):
        self.nc = nc
        self.eng = {"pe": nc.tensor, "dve": nc.vector, "act": nc.scalar, "pool": nc.gpsimd, "sp": nc.sync}
        self.sem = {}
        self.cnt = {}
        self.seen = {k: {} for k in self.eng}
        self._cms = []
        for k in self.eng:
            cm = nc.semaphore("s_" + k)
            self.sem[k] = cm.__enter__()
            self._cms.append(cm)
            self.cnt[k] = 0
        self.dma_sems = []
        for i in range(n_dma_sems):
            cm = nc.semaphore("s_dma%d" % i)
            self.dma_sems.append([cm.__enter__(), 0])
            self._cms.append(cm)
        self.dma_rr = 0
        self.last_w = {}
        self.readers = {}
        self.semobj = {k: self.sem[k] for k in self.eng}
        for i, (s, _) in enumerate(self.dma_sems):
            self.semobj["dma%d" % i] = s

    def close(self):
        for cm in reversed(self._cms):
            cm.__exit__(None, None, None)

    def _wait(self, e, semkey, val):
        if self.seen[e].get(semkey, 0) >= val:
            return
        if semkey == e and val > self.cnt[e]:
            return
        self.eng[e].wait_ge(self.semobj[semkey], val)
        self.seen[e][semkey] = val

    def _deps(self, e, reads, writes):
        for r in reads:
            w = self.last_w.get(r)
            if w is not None:
                self._wait(e, *w)
        for r in writes:
            w = self.last_w.get(r)
            if w is not None:
                self._wait(e, *w)
            for sk, v in self.readers.get(r, {}).items():
                self._wait(e, sk, v)

    def _commit(self, semkey, val, reads, writes):
        for r in reads:
            self.readers.setdefault(r, {})[semkey] = val
        for r in writes:
            self.last_w[r] = (semkey, val)
            self.readers[r] = {}

    def op(self, e, fn, reads=(), writes=(), inc=True):
        self._deps(e, reads, writes)
        ins = fn(self.eng[e])
        if inc:
            self.cnt[e] += 1
            ins.then_inc(self.sem[e], 1)
            self._commit(e, self.cnt[e], reads, writes)
        else:
            self._commit(e, self.cnt[e] + 1, reads, writes)

    def dma(self, q, out, in_, reads=(), writes=()):
        i = self.dma_rr
        self.dma_rr = (self.dma_rr + 1) % len(self.dma_sems)
        sk = "dma%d" % i
        self._wait(q, sk, self.dma_sems[i][1])
        self._deps(q, reads, writes)
        ins = self.eng[q].dma_start(out=out, in_=in_)
        self.dma_sems[i][1] += 16
        ins.then_inc(self.dma_sems[i][0], 16)
        self._commit(sk, self.dma_sems[i][1], reads, writes)

    def finish(self, e="sp"):
        for r, (sk, v) in list(self.last_w.items()):
            self._wait(e, sk, v)

    def barrier(self):
        tot = {k: self.cnt[k] for k in self.eng}
        for i, (s, v) in enumerate(self.dma_sems):
            tot["dma%d" % i] = v
        for e in self.eng:
            for sk, v in tot.items():
                if v > 0:
                    self._wait(e, sk, v)
        self.last_w = {}
        self.readers = {}

    def tt(self, e, out, a, b, op, r, w):
        self.op(e, lambda g: g.tensor_tensor(out=out, in0=a, in1=b, op=op), reads=r, writes=w)

    def ts(self, e, out, a, s1, op0, r, w, s2=None, op1=None):
        if op1 is None:
            self.op(e, lambda g: g.tensor_scalar(out=out, in0=a, scalar1=s1, scalar2=None, op0=op0), reads=r, writes=w)
        else:
            self.op(e, lambda g: g.tensor_scalar(out=out, in0=a, scalar1=s1, scalar2=s2, op0=op0, op1=op1), reads=r, writes=w)

    def stt(self, out, a, s, b, op0, op1, r, w):
        self.op("dve", lambda g: g.scalar_tensor_tensor(out=out, in0=a, scalar=s, in1=b, op0=op0, op1=op1), reads=r, writes=w)

    def act(self, out, a, func, r, w, scale=None, bias=None):
        kw = {}
        if scale is not None:
            kw["scale"] = scale
        if bias is not None:
            kw["bias"] = bias
        self.op("act", lambda g: g.activation(out=out, in_=a, func=func, **kw), reads=r, writes=w)

    def mm(self, out, lhsT, rhs, r, w, start=True, stop=True, inc=None):
        self.op("pe", lambda g: g.matmul(out, lhsT=lhsT, rhs=rhs, start=start, stop=stop), reads=r, writes=w,
                inc=(stop if inc is None else inc))


def _alloc(nc, stack, kind, name, shape, dt):
    cm = (nc.sbuf_tensor if kind == "sb" else nc.psum_tensor)(name, shape, dt)
    t = cm.__enter__()
    stack.append(cm)
    return t


def build_ffn(NT, D=D_MODEL, F=D_FF, TB=None, mode="ffn", NPJ=0, GW=512):
    nc = bass.Bass("TRN2", target_bir_lowering=False)
    TB = TB or min(512, NT)
    QW = min(512, TB)
    rscale = 0.5 if mode == "ffn" else 1.0
    DC, FC = D // 128, F // 128
    FW = 256
    NFB = F // FW
    FPB = FW // 128
    xT = nc.dram_tensor("xT", [D, NT], F32, kind="ExternalInput").ap()
    if mode == "ffn":
        wg = nc.dram_tensor("wg", [NFB, 128, DC, FW], F32, kind="ExternalInput").ap()
        wu = nc.dram_tensor("wu", [NFB, 128, DC, FW], F32, kind="ExternalInput").ap()
    else:
        ymT = nc.dram_tensor("ymT", [F, NT], F32, kind="ExternalInput").ap()
        wglu = nc.dram_tensor("wglu", [GW, GW], F32, kind="ExternalInput").ap()
        bglu = nc.dram_tensor("bglu", [128, GW // 128], F32, kind="ExternalInput").ap()
        ymT_v = ymT.rearrange("(c p) t -> p c t", p=128)
        wglu_v = wglu.rearrange("(c p) f -> p c f", p=128)
    if NPJ:
        NPB = (NPJ + FW - 1) // FW
        win = nc.dram_tensor("win", [NPB, 128, DC, FW], F32, kind="ExternalInput").ap()
        pjT = nc.dram_tensor("pjT", [NPJ, NT], F32, kind="ExternalOutput").ap()
    WDG = [w for w in (11, 4, 2, 1) if FC % w == 0][0]
    NDG = FC // WDG
    wd = nc.dram_tensor("wd", [DC, NDG, 128, WDG, 128], F32, kind="ExternalInput").ap()
    NTB = NT // TB
    CACHE = False
    if CACHE:
        wd_c = nc.dram_tensor("wd_c", [DC, NDG, 128, WDG, 128], BF16, kind="Internal").ap()
        if mode == "ffn":
            wg_c = nc.dram_tensor("wg_c", [NFB, 128, DC, FW], BF16, kind="Internal").ap()
            wu_c = nc.dram_tensor("wu_c", [NFB, 128, DC, FW], BF16, kind="Internal").ap()
        if NPJ:
            win_c = nc.dram_tensor("win_c", [(NPJ + FW - 1) // FW, 128, DC, FW], BF16, kind="Internal").ap()

    def wload(dst, dstn, src32, cache, ckey, tb):
        if not CACHE:
            S.dma("pool", dst, src32, writes=[dstn])
        elif tb == 0:
            S.dma("pool", dst, src32, writes=[dstn])
            S.dma("sp", cache, dst, reads=[dstn], writes=[ckey])
        else:
            S.dma("sp", dst, cache, reads=[ckey], writes=[dstn])
    gb = nc.dram_tensor("gb", [128, 2 * DC], F32, kind="ExternalInput").ap()
    yT = nc.dram_tensor("yT", [D, NT], F32, kind="ExternalOutput").ap()
    xT_v = xT.rearrange("(c p) t -> p c t", p=128)
    yT_v = yT.rearrange("(c p) t -> p c t", p=128)

    st = []
    S = Sched(nc)
    xb = _alloc(nc, st, "sb", "xb", [128, DC, TB], BF16) if mode == "ffn" else None
    aT = _alloc(nc, st, "sb", "aT", [128, FC, TB], BF16)
    rT = _alloc(nc, st, "sb", "rT", [128, DC, TB], F32)
    wgs = [_alloc(nc, st, "sb", "wgs%d" % i, [128, DC, FW], BF16) for i in range(2)] if (mode == "ffn" or NPJ) else None
    wus = [_alloc(nc, st, "sb", "wus%d" % i, [128, DC, FW], BF16) for i in range(2)] if mode == "ffn" else None
    wds = [_alloc(nc, st, "sb", "wds%d" % i, [128, WDG, 128], BF16) for i in range(3)]
    gbs = _alloc(nc, st, "sb", "gbs", [128, 2 * DC], F32)
    ones = _alloc(nc, st, "sb", "ones", [128, 128], F32)
    sg = [_alloc(nc, st, "sb", "sg%d" % i, [128, 512], F32) for i in range(2)]
    GC = GW // 128
    if mode == "mix":
        y5b = _alloc(nc, st, "sb", "y5b", [128, GC, TB], BF16)
        y5f = _alloc(nc, st, "sb", "y5f", [128, GC, TB], F32)
        wgl = _alloc(nc, st, "sb", "wgl", [128, GC, GW], BF16)
        bgl = _alloc(nc, st, "sb", "bgl", [128, GC], F32)
    if NPJ:
        ybf = _alloc(nc, st, "sb", "ybf", [128, DC, TB], BF16)
        po = [_alloc(nc, st, "sb", "po%d" % i, [128, TB], F32) for i in range(2)]
    xf = [_alloc(nc, st, "sb", "xf%d" % i, [128, TB], F32) for i in range(2)]
    sq = [_alloc(nc, st, "sb", "sq%d" % i, [128, TB], F32) for i in range(2)]
    mean = _alloc(nc, st, "sb", "mean", [128, TB], F32)
    rstd = _alloc(nc, st, "sb", "rstd", [128, TB], F32)
    yo = [_alloc(nc, st, "sb", "yo%d" % i, [128, TB], F32) for i in range(2)]
    pg = [_alloc(nc, st, "ps", "pg%d" % i, [128, 512], F32) for i in range(2)]
    pu = [_alloc(nc, st, "ps", "pu%d" % i, [128, 512], F32) for i in range(2)]
    pd = [_alloc(nc, st, "ps", "pd%d" % i, [128, 512], F32) for i in range(2)]
    pm = _alloc(nc, st, "ps", "pm", [128, 512], F32)
    pv = _alloc(nc, st, "ps", "pv", [128, 512], F32)
    NQ = TB // QW

    S.dma("sp", gbs[:], gb, writes=["gbs"])
    S.op("dve", lambda e: e.memset(ones[:], 1.0 / D), writes=["ones"])

    kgu = 0
    kd = 0
    kx = 0
    ko = 0
    for tb in range(NT // TB):
        t0 = tb * TB
        if mode == "ffn":
            S.dma("pool", xb[:], xT_v[:, :, t0:t0 + TB], writes=["xb"])
        if mode == "mix":
            if tb == 0:
                S.dma("pool", wgl[:], wglu_v, writes=["wgl"])
                S.dma("sp", bgl[:], bglu, writes=["bgl"])
            S.dma("pool", y5b[:], ymT_v[:, 0:GC, t0:t0 + TB], writes=["y5b"])
            S.dma("sp", y5f[:], ymT_v[:, 0:GC, t0:t0 + TB], writes=["y5f"])
            for q in range(NQ):
                S.dma("pool", aT[:, GC:FC, q * QW:(q + 1) * QW], ymT_v[:, GC:FC, t0 + q * QW:t0 + (q + 1) * QW],
                      writes=[("aT", fc, q) for fc in range(GC, FC)])
            for jc in range(GC):
                for q in range(NQ):
                    pb = kgu % 2
                    kgu += 1
                    for ic in range(GC):
                        S.mm(pg[pb][:, 0:QW], wgl[:, ic, jc * 128:(jc + 1) * 128], y5b[:, ic, q * QW:(q + 1) * QW],
                             ["wgl", "y5b"], [("pg", pb)], start=(ic == 0), stop=(ic == GC - 1))
                    S.act(sg[pb][:, 0:QW], pg[pb][:, 0:QW], AF.Sigmoid, [("pg", pb), "bgl"], [("sg", pb)], bias=bgl[:, jc:jc + 1])
                    S.tt("dve", aT[:, jc, q * QW:(q + 1) * QW], sg[pb][:, 0:QW], y5f[:, jc, q * QW:(q + 1) * QW], ALU.mult,
                         [("sg", pb), "y5f"], [("aT", jc, q)])
        if mode == "ffn":
            for fb in range(NFB):
                b = fb % 2
                wload(wgs[b][:], ("wg", b), wg[fb], wg_c[fb] if CACHE else None, ("wg_c", fb), tb)
                wload(wus[b][:], ("wu", b), wu[fb], wu_c[fb] if CACHE else None, ("wu_c", fb), tb)
                for fi in range(FPB):
                    fc = fb * FPB + fi
                    for q in range(NQ):
                        pb = kgu % 2
                        kgu += 1
                        for c in range(DC):
                            S.op("pe", lambda e, c=c: e.matmul(pg[pb][:, 0:QW], lhsT=wgs[b][:, c, fi * 128:(fi + 1) * 128],
                                                             rhs=xb[:, c, q * QW:(q + 1) * QW], start=(c == 0), stop=(c == DC - 1)),
                                 reads=[("wg", b), "xb"], writes=[("pg", pb)], inc=(c == DC - 1))
                        for c in range(DC):
                            S.op("pe", lambda e, c=c: e.matmul(pu[pb][:, 0:QW], lhsT=wus[b][:, c, fi * 128:(fi + 1) * 128],
                                                             rhs=xb[:, c, q * QW:(q + 1) * QW], start=(c == 0), stop=(c == DC - 1)),
                                 reads=[("wu", b), "xb"], writes=[("pu", pb)], inc=(c == DC - 1))
                        S.op("act", lambda e: e.activation(out=sg[pb][:, 0:QW], in_=pg[pb][:, 0:QW], func=AF.Silu),
                             reads=[("pg", pb)], writes=[("sg", pb)])
                        S.op("dve", lambda e: e.tensor_tensor(out=aT[:, fc, q * QW:(q + 1) * QW], in0=sg[pb][:, 0:QW], in1=pu[pb][:, 0:QW], op=ALU.mult),
                             reads=[("sg", pb), ("pu", pb)], writes=[("aT", fc, q)])
        for dco in range(DC):
            xb_ = kx % 2
            kx += 1
            S.dma("sp", xf[xb_][:], xT_v[:, dco, t0:t0 + TB], writes=[("xf", xb_)])
            for q in range(NQ):
                pb = kd % 2
                for g in range(NDG):
                    wb = kd % 3 if False else (kd * NDG + g) % 3
                    wload(wds[wb][:], ("wd", wb), wd[dco, g], wd_c[dco, g] if CACHE else None, ("wd_c", dco, g), tb)
                    for j in range(WDG):
                        fc = g * WDG + j
                        S.op("pe", lambda e, fc=fc, j=j: e.matmul(pd[pb][:, 0:QW], lhsT=wds[wb][:, j, :], rhs=aT[:, fc, q * QW:(q + 1) * QW],
                                                                 start=(fc == 0), stop=(fc == FC - 1)),
                             reads=[("wd", wb), ("aT", fc, q)], writes=[("pd", pb)], inc=(j == WDG - 1))
                kd += 1
                S.op("act", lambda e: e.activation(out=rT[:, dco, q * QW:(q + 1) * QW], in_=pd[pb][:, 0:QW], func=AF.Copy, scale=rscale),
                     reads=[("pd", pb)], writes=[("rT", dco, q)])
                S.op("dve", lambda e: e.scalar_tensor_tensor(out=rT[:, dco, q * QW:(q + 1) * QW], in0=xf[xb_][:, q * QW:(q + 1) * QW],
                                                           scalar=ALPHA, in1=rT[:, dco, q * QW:(q + 1) * QW], op0=ALU.mult, op1=ALU.add),
                     reads=[("xf", xb_), ("rT", dco, q)], writes=[("rT", dco, q)])
        for q in range(NQ):
            qs = slice(q * QW, (q + 1) * QW)
            for c in range(DC):
                S.op("pe", lambda e, c=c: e.matmul(pm[:, 0:QW], lhsT=ones[:], rhs=rT[:, c, qs], start=(c == 0), stop=(c == DC - 1)),
                     reads=["ones", ("rT", c, q)], writes=["pm"], inc=(c == DC - 1))
            S.op("act", lambda e: e.activation(out=mean[:, qs], in_=pm[:, 0:QW], func=AF.Copy), reads=["pm"], writes=[("mean", q)])
            for c in range(DC):
                S.op("dve", lambda e, c=c: e.tensor_tensor(out=rT[:, c, qs], in0=rT[:, c, qs], in1=mean[:, qs], op=ALU.subtract),
                     reads=[("rT", c, q), ("mean", q)], writes=[("rT", c, q)])
                sb_ = c % 2
                S.op("act", lambda e, c=c: e.activation(out=sq[sb_][:, qs], in_=rT[:, c, qs], func=AF.Square),
                     reads=[("rT", c, q)], writes=[("sq", sb_, q)])
                S.op("pe", lambda e, c=c: e.matmul(pv[:, 0:QW], lhsT=ones[:], rhs=sq[sb_][:, qs], start=(c == 0), stop=(c == DC - 1)),
                     reads=["ones", ("sq", sb_, q)], writes=["pv"])
            S.op("act", lambda e: e.activation(out=rstd[:, qs], in_=pv[:, 0:QW], func=AF.Sqrt, bias=LN_EPS, scale=1.0), reads=["pv"], writes=[("rstd", q)])
            S.op("dve", lambda e: e.reciprocal(out=rstd[:, qs], in_=rstd[:, qs]), reads=[("rstd", q)], writes=[("rstd", q)])
        for c in range(DC):
            ob = ko % 2
            ko += 1
            for q in range(NQ):
                qs = slice(q * QW, (q + 1) * QW)
                S.op("dve", lambda e: e.tensor_tensor(out=yo[ob][:, qs], in0=rT[:, c, qs], in1=rstd[:, qs], op=ALU.mult),
                     reads=[("rT", c, q), ("rstd", q)], writes=[("yo", ob)])
            S.op("pool", lambda e: e.tensor_scalar(out=yo[ob][:], in0=yo[ob][:], scalar1=gbs[:, c:c + 1], scalar2=gbs[:, DC + c:DC + c + 1],
                                                   op0=ALU.mult, op1=ALU.add),
                 reads=[("yo", ob), "gbs"], writes=[("yo", ob)])
            S.dma("sp", yT_v[:, c, t0:t0 + TB], yo[ob][:], reads=[("yo", ob)], writes=[("yT", c, tb)])
            if NPJ:
                S.act(ybf[:, c, :], yo[ob][:], AF.Copy, [("yo", ob)], [("ybf", c)])
        if NPJ:
            kp = 0
            for pb0 in range(0, NPJ, FW):
                pw = min(FW, NPJ - pb0)
                b = (pb0 // FW) % 2
                wload(wgs[b][:], ("wg", b), win[pb0 // FW], win_c[pb0 // FW] if CACHE else None, ("win_c", pb0 // FW), tb)
                for pc in range(0, pw, 128):
                    m = min(128, pw - pc)
                    for q in range(NQ):
                        pb = kp % 2
                        kp += 1
                        for c in range(DC):
                            S.mm(pg[pb][0:m, 0:QW], wgs[b][:, c, pc:pc + m], ybf[:, c, q * QW:(q + 1) * QW],
                                 [("wg", b)] + ([("ybf", cc) for cc in range(DC)] if c == 0 else []), [("pg", pb)], start=(c == 0), stop=(c == DC - 1))
                        S.act(po[pb][0:m, q * QW:(q + 1) * QW], pg[pb][0:m, 0:QW], AF.Copy, [("pg", pb)], [("po", pb)])
                        S.dma("sp", pjT[pb0 + pc:pb0 + pc + m, t0 + q * QW:t0 + (q + 1) * QW], po[pb][0:m, q * QW:(q + 1) * QW],
                              reads=[("po", pb)], writes=[("pjT", pb0 + pc, tb, q)])
    S.finish("sp")
    S.close()
    for cm in reversed(st):
        cm.__exit__(None, None, None)
    return nc


class Pool_:
    def __init__(self, nc, st, prefix, shape, n, dt=F32):
        self.tiles = [_alloc(nc, st, "sb", "%s%d" % (prefix, i), shape, dt) for i in range(n)]
        self.names = ["%s%d" % (prefix, i) for i in range(n)]
        self.k = 0

    def get(self):
        i = self.k % len(self.tiles)
        self.k += 1
        return self.tiles[i], self.names[i]


def emit_softplus(S, P, x, xn, out, outn, sl):
    ax, axn = P.get()
    S.act(sl(ax), x, AF.Abs, [xn], [axn])
    y, yn = P.get()
    S.act(sl(y), sl(ax), AF.Exp, [axn], [yn], scale=-1.0)
    t, tn = P.get()
    S.ts("dve", sl(t), sl(y), 2.0, ALU.add, [yn], [tn])
    S.op("dve", lambda g: g.reciprocal(out=sl(t), in_=sl(t)), reads=[tn], writes=[tn])
    s, sn = P.get()
    S.tt("dve", sl(s), sl(y), sl(t), ALU.mult, [yn, tn], [sn])
    s2, s2n = P.get()
    S.tt("dve", sl(s2), sl(s), sl(s), ALU.mult, [sn], [s2n])
    p, pn = P.get()
    S.ts("dve", sl(p), sl(s2), 1.0 / 11.0, ALU.mult, [s2n], [pn], s2=1.0 / 9.0, op1=ALU.add)
    for cst in (1.0 / 7.0, 1.0 / 5.0, 1.0 / 3.0, 1.0):
        S.tt("dve", sl(p), sl(p), sl(s2), ALU.mult, [pn, s2n], [pn])
        S.ts("dve", sl(p), sl(p), cst, ALU.add, [pn], [pn])
    S.tt("dve", sl(p), sl(p), sl(s), ALU.mult, [pn, sn], [pn])
    S.ts("dve", sl(ax), x, 0.0, ALU.max, [xn], [axn])
    S.stt(out, sl(p), 2.0, sl(ax), ALU.mult, ALU.add, [pn, axn], [outn])


def emit_gelu(S, out, outn, x, xn, tmp, tmpn, eng2="pool"):
    S.act(tmp, x, AF.Square, [xn], [tmpn])
    S.ts("dve", tmp, tmp, 0.044715, ALU.mult, [tmpn], [tmpn], s2=1.0, op1=ALU.add)
    S.tt("dve", tmp, tmp, x, ALU.mult, [tmpn, xn], [tmpn])
    S.act(tmp, tmp, AF.Sigmoid, [tmpn], [tmpn], scale=1.5957691216057308)
    S.tt(eng2, out, tmp, x, ALU.mult, [tmpn, xn], [outn])


def build_mix(L, NB=2, do_lru=True, do_s5=True, do_gdn=True):
    nc = bass.Bass("TRN2", target_bir_lowering=False)
    NTOK = NB * L
    S = Sched(nc)
    dr = {}

    def din(name, shape):
        dr[name] = nc.dram_tensor(name, shape, F32, kind="ExternalInput").ap()
        return dr[name]

    def dout(name, shape):
        dr[name] = nc.dram_tensor(name, shape, F32, kind="ExternalOutput").ap()
        return dr[name]

    if do_lru:
        emit_lru(nc, S, L, NB, din, dout)
        S.barrier()
    if do_s5:
        emit_s5(nc, S, L, NB, din, dout)
        S.barrier()
    if do_gdn:
        emit_gdn(nc, S, L, NB, din, dout)
    S.finish("sp")
    S.close()
    return nc


def emit_lru(nc, S, L, NB, din, dout):
    assert NB == 2
    TS = min(L, 2048)
    NSEG = L // TS
    lx = din("lx", [128, L])
    lg = din("lg", [128, L])
    lpar = din("lpar", [128, 8])
    lwa = din("lwa", [128, 128])
    lwx = din("lwx", [128, 128])
    ly = dout("ly", [128, L])
    st = []
    par = _alloc(nc, st, "sb", "l_par", [128, 8], F32)
    wa = _alloc(nc, st, "sb", "l_wa", [128, 128], BF16)
    wx = _alloc(nc, st, "sb", "l_wx", [128, 128], BF16)
    c12 = _alloc(nc, st, "sb", "l_c12", [128, 2], F32)
    carry = _alloc(nc, st, "sb", "l_carry", [128, 1], F32)
    tiny = Pool_(nc, st, "l_tiny", [128, 1], 8)
    big = Pool_(nc, st, "l_big", [128, TS], 9)
    xin = [_alloc(nc, st, "sb", "l_xin%d" % i, [128, TS + 3], F32) for i in range(2)]
    xcb = _alloc(nc, st, "sb", "l_xcb", [128, TS], BF16)
    pr = [_alloc(nc, st, "ps", "l_pr%d" % i, [128, 512], F32) for i in range(2)]
    pi = [_alloc(nc, st, "ps", "l_pi%d" % i, [128, 512], F32) for i in range(2)]

    S.dma("sp", par[:], lpar, writes=["l_par"])
    S.dma("pool", wa[:], lwa, writes=["l_wa"])
    S.dma("pool", wx[:], lwx, writes=["l_wx"])
    nl, nln = tiny.get()
    S.ts("dve", nl[:], par[:, 5:6], -1.0, ALU.mult, ["l_par"], [nln])
    sp_, spn = tiny.get()
    emit_softplus(S, tiny, nl[:], nln, sp_[:], spn, lambda t: t[:])
    S.ts("dve", c12[:, 0:1], sp_[:], -8.0, ALU.mult, [spn], ["l_c12"])
    S.ts("dve", c12[:, 1:2], sp_[:], -16.0, ALU.mult, [spn, "l_c12"], ["l_c12"])
    S.op("dve", lambda g: g.memset(carry[:], 0.0), writes=["l_carry"])

    for s in range(NSEG):
        xi = xin[s % 2]
        xn = "l_xin%d" % (s % 2)
        if s == 0:
            S.op("dve", lambda g: g.memset(xi[:, 0:3], 0.0), writes=[xn])
            S.dma("sp", xi[:, 3:3 + TS], lx[:, 0:TS], reads=[xn], writes=[xn])
        else:
            S.dma("sp", xi[:, :], lx[:, s * TS - 3:(s + 1) * TS], writes=[xn])
        xc, xcn = big.get()
        S.ts("dve", xc[:], xi[:, 3:3 + TS], par[:, 3:4], ALU.mult, [xn, "l_par"], [xcn], s2=par[:, 4:5], op1=ALU.add)
        for k in (2, 1, 0):
            S.stt(xc[:], xi[:, k:k + TS], par[:, k:k + 1], xc[:], ALU.mult, ALU.add, [xn, "l_par", xcn], [xcn])
        S.act(xcb[:], xc[:], AF.Copy, [xcn], ["l_xcb"])
        r, rn = big.get()
        ig, ign = big.get()
        for q in range(TS // 512):
            qs = slice(q * 512, (q + 1) * 512)
            b = q % 2
            S.mm(pr[b][:], wa[:], xcb[:, qs], ["l_wa", "l_xcb"], [("l_pr", b)])
            S.mm(pi[b][:], wx[:], xcb[:, qs], ["l_wx", "l_xcb"], [("l_pi", b)])
            S.act(r[:, qs], pr[b][:], AF.Sigmoid, [("l_pr", b), "l_par"], [rn], bias=par[:, 6:7])
            S.act(ig[:, qs], pi[b][:], AF.Sigmoid, [("l_pi", b), "l_par"], [ign], bias=par[:, 7:8])
        a, an = big.get()
        S.act(a[:], r[:], AF.Exp, [rn, "l_c12"], [an], scale=c12[:, 0:1])
        s2, s2n = big.get()
        S.act(s2[:], r[:], AF.Exp, [rn, "l_c12"], [s2n], scale=c12[:, 1:2])
        S.act(s2[:], s2[:], AF.Sqrt, [s2n], [s2n], scale=-1.0, bias=1.0)
        S.tt("dve", s2[:], s2[:], ig[:], ALU.mult, [s2n, ign], [s2n])
        S.tt("dve", s2[:], s2[:], xc[:], ALU.mult, [s2n, xcn], [s2n])
        h, hn = big.get()
        S.op("dve", lambda g: g.tensor_tensor_scan(out=h[:], data0=a[:], data1=s2[:], initial=carry[:, 0:1], op0=ALU.mult, op1=ALU.add),
             reads=[an, s2n, "l_carry"], writes=[hn])
        S.op("dve", lambda g: g.tensor_copy(out=carry[:], in_=h[:, TS - 1:TS]), reads=[hn], writes=["l_carry"])
        gt, gtn = big.get()
        S.dma("sp", gt[:], lg[:, s * TS:(s + 1) * TS], writes=[gtn])
        tmp, tmpn = big.get()
        ge, gen = big.get()
        emit_gelu(S, ge[:], gen, gt[:], gtn, tmp[:], tmpn)
        S.tt("dve", ge[:], ge[:], h[:], ALU.mult, [gen, hn], [gen])
        S.dma("sp", ly[:, s * TS:(s + 1) * TS], ge[:], reads=[gen], writes=[("ly", s)])
    S.barrier()
    for cm in reversed(st):
        cm.__exit__(None, None, None)


class Ex:
    def __init__(self, S, pool, ipool, sl):
        self.S, self.pool, self.ipool, self.sl = S, pool, ipool, sl

    def new(self):
        t, n = self.pool.get()
        return self.sl(t), n

    def bin(self, a, b, op, eng="dve"):
        o = self.new()
        self.S.tt(eng, o[0], a[0], b[0], op, [a[1], b[1]], [o[1]])
        return o

    def mul(self, a, b): return self.bin(a, b, ALU.mult)
    def add(self, a, b): return self.bin(a, b, ALU.add)
    def sub(self, a, b): return self.bin(a, b, ALU.subtract)

    def sc(self, a, c1, op0, c2=None, op1=None):
        o = self.new()
        self.S.ts("dve", o[0], a[0], c1, op0, [a[1]], [o[1]], s2=c2, op1=op1)
        return o

    def act(self, a, func, scale=None, bias=None):
        o = self.new()
        self.S.act(o[0], a[0], func, [a[1]], [o[1]], scale=scale, bias=bias)
        return o

    def recip(self, a):
        o = self.new()
        self.S.op("dve", lambda g: g.reciprocal(out=o[0], in_=a[0]), reads=[a[1]], writes=[o[1]])
        return o

    def frac(self, k):
        it, itn = self.ipool.get()
        it = self.sl(it)
        self.S.op("dve", lambda g: g.tensor_copy(out=it, in_=k[0]), reads=[k[1]], writes=[itn])
        kf = self.new()
        self.S.op("dve", lambda g: g.tensor_copy(out=kf[0], in_=it), reads=[itn], writes=[kf[1]])
        f = self.sub(k, kf)
        m = self.sc(f, 0.5, ALU.is_gt)
        f = self.sub(f, m)
        m = self.sc(f, -0.5, ALU.is_lt)
        f = self.add(f, m)
        return self.sc(f, 0.4999999, ALU.min, -0.4999999, ALU.max)

    def sincos(self, th):
        k = self.sc(th, 1.0 / (2.0 * np.pi), ALU.mult)
        return self.sincos_k(k)

    def sincos_k(self, k):
        s = self.act(self.frac(k), AF.Sin, scale=2.0 * np.pi)
        c = self.act(self.frac(self.sc(k, 0.25, ALU.add)), AF.Sin, scale=2.0 * np.pi)
        return s, c

    def s5_params(self, lre_in, lim, lstep):
        lre = self.sc(lre_in, -1e-4, ALU.min)
        dt = self.act(lstep, AF.Exp)
        rmag = self.act(self.mul(lre, dt), AF.Exp)
        sn, cs = self.sincos(self.mul(lim, dt))
        ar = self.mul(rmag, cs)
        ai = self.mul(rmag, sn)
        am1 = self.sc(ar, -1.0, ALU.add)
        num_r = self.add(self.mul(am1, lre), self.mul(ai, lim))
        num_i = self.sub(self.mul(ai, lre), self.mul(am1, lim))
        den = self.add(self.mul(lre, lre), self.mul(lim, lim))
        rden = self.recip(den)
        return rmag, cs, sn, self.mul(num_r, rden), self.mul(num_i, rden)


def emit_s5(nc, S, L, NB, din, dout):
    NTOK = NB * L
    T = 512
    NBLK = L // T
    su = din("su", [64, NTOK])
    s5row = din("s5row", [64, 3, 256])
    s5col = din("s5col", [128, 2, 3])
    s5bT = din("s5bT", [64, 2, 256])
    s5cT = din("s5cT", [128, 2, 2, 64])
    s5d = din("s5d", [64, 1])
    y5 = dout("y5", [64, NTOK])
    I32 = mybir.dt.int32
    st = []
    rowp = _alloc(nc, st, "sb", "s_rowp", [64, 3, 256], F32)
    colp = _alloc(nc, st, "sb", "s_colp", [128, 2, 3], F32)
    bT = _alloc(nc, st, "sb", "s_bT", [64, 2, 256], F32)
    cT = _alloc(nc, st, "sb", "s_cT", [128, 2, 2, 64], F32)
    sd = _alloc(nc, st, "sb", "s_sd", [64, 1], F32)
    BTr = _alloc(nc, st, "sb", "s_BTr", [64, 256], BF16)
    BTi = _alloc(nc, st, "sb", "s_BTi", [64, 256], BF16)
    cTr = _alloc(nc, st, "sb", "s_cTr", [128, 2, 64], BF16)
    cTi = _alloc(nc, st, "sb", "s_cTi", [128, 2, 64], BF16)
    rpool = Pool_(nc, st, "s_rp", [64, 256], 40)
    ripool = Pool_(nc, st, "s_rpi", [64, 256], 2, I32)
    cpool = Pool_(nc, st, "s_cp", [128, 2], 40)
    cipool = Pool_(nc, st, "s_cpi", [128, 2], 2, I32)
    Er = [_alloc(nc, st, "sb", "s_Er%d" % j, [128, T], F32) for j in range(2)]
    Ei = [_alloc(nc, st, "sb", "s_Ei%d" % j, [128, T], F32) for j in range(2)]
    Rt = [_alloc(nc, st, "sb", "s_Rt%d" % j, [128, T], F32) for j in range(2)]
    ttmp = _alloc(nc, st, "sb", "s_ttmp", [128, T], F32)
    wpow = [_alloc(nc, st, "sb", "s_wpow%d" % i, [128, 2, 2], F32) for i in range(2)]
    W512 = _alloc(nc, st, "sb", "s_W512", [128, 2, 2], F32)
    car = _alloc(nc, st, "sb", "s_car", [128, 2, 2 * NB], F32)
    ctmp = Pool_(nc, st, "s_ct", [128, 1], 6)
    uf = [_alloc(nc, st, "sb", "s_uf%d" % i, [64, T], F32) for i in range(2)]
    ub = [_alloc(nc, st, "sb", "s_ub%d" % i, [64, T], BF16) for i in range(2)]
    wk = Pool_(nc, st, "s_wk", [128, T], 12)
    hb = [[_alloc(nc, st, "sb", "s_hb%d%d" % (j, k), [128, T], BF16) for k in range(2)] for j in range(2)]
    yo = Pool_(nc, st, "s_yo", [64, T], 4)
    pbr = [_alloc(nc, st, "ps", "s_pbr%d" % j, [128, 512], F32) for j in range(2)]
    pbi = [_alloc(nc, st, "ps", "s_pbi%d" % j, [128, 512], F32) for j in range(2)]
    py = [_alloc(nc, st, "ps", "s_py%d" % j, [64, 512], F32) for j in range(2)]

    S.dma("sp", rowp[:], s5row, writes=["s_rowp"])
    S.dma("sp", colp[:], s5col, writes=["s_colp"])
    S.dma("sp", bT[:], s5bT, writes=["s_bT"])
    S.dma("sp", cT[:], s5cT, writes=["s_cT"])
    S.dma("sp", sd[:], s5d, writes=["s_sd"])
    ex = Ex(S, rpool, ripool, lambda t: t[:])
    _, _, _, kr, ki = ex.s5_params((rowp[:, 0, :], "s_rowp"), (rowp[:, 1, :], "s_rowp"), (rowp[:, 2, :], "s_rowp"))
    br, bi = (bT[:, 0, :], "s_bT"), (bT[:, 1, :], "s_bT")
    t1 = ex.sub(ex.mul(kr, br), ex.mul(ki, bi))
    t2 = ex.add(ex.mul(kr, bi), ex.mul(ki, br))
    S.act(BTr[:], t1[0], AF.Copy, [t1[1]], ["s_BTr"])
    S.act(BTi[:], t2[0], AF.Copy, [t2[1]], ["s_BTi"])
    S.act(cTr[:], cT[:, :, 0, :], AF.Copy, ["s_cT"], ["s_cTr"])
    S.act(cTi[:], cT[:, :, 1, :], AF.Copy, ["s_cT"], ["s_cTi"], scale=-1.0)
    ex = Ex(S, cpool, cipool, lambda t: t[:])
    rmag, cs, sn, _, _ = ex.s5_params((colp[:, :, 0], "s_colp"), (colp[:, :, 1], "s_colp"), (colp[:, :, 2], "s_colp"))
    k0 = _alloc(nc, st, "sb", "s_k0", [128, 2], F32)
    rmg = _alloc(nc, st, "sb", "s_rmg", [128, 2], F32)
    S.op("dve", lambda g: g.tensor_copy(out=rmg[:], in_=rmag[0]), reads=[rmag[1]], writes=["s_rmg"])
    dtc = ex.act((colp[:, :, 2], "s_colp"), AF.Exp)
    kk = ex.sc(ex.mul((colp[:, :, 1], "s_colp"), dtc), 1.0 / (2.0 * np.pi), ALU.mult)
    kf = ex.frac(kk)
    S.op("dve", lambda g: g.tensor_copy(out=k0[:], in_=kf[0]), reads=[kf[1]], writes=["s_k0"])
    sT, cT_ = ex.sincos_k(ex.sc((k0[:], "s_k0"), float(T), ALU.mult))
    S.op("dve", lambda g: g.tensor_copy(out=W512[:, 0, :], in_=cT_[0]), reads=[cT_[1]], writes=["s_W512"])
    S.op("dve", lambda g: g.tensor_copy(out=W512[:, 1, :], in_=sT[0]), reads=[sT[1], "s_W512"], writes=["s_W512"])
    tau = _alloc(nc, st, "sb", "s_tau", [128, T], F32)
    S.dma("sp", tau[:], din("s5tau", [128, T]), writes=["s_tau"])
    tpool = Pool_(nc, st, "s_tp", [128, T], 10)
    tipool = Pool_(nc, st, "s_tpi", [128, T], 2, I32)
    ext = Ex(S, tpool, tipool, lambda t: t[:])
    for j in range(2):
        ang = ext.new()
        S.ts("dve", ang[0], tau[:], k0[:, j:j + 1], ALU.mult, ["s_tau", "s_k0"], [ang[1]])
        sj, cj = ext.sincos_k(ang)
        S.op("dve", lambda g: g.tensor_copy(out=Er[j][:], in_=cj[0]), reads=[cj[1]], writes=["s_Er%d" % j])
        S.ts("dve", Ei[j][:], sj[0], -1.0, ALU.mult, [sj[1]], ["s_Ei%d" % j])
        S.ts("dve", Rt[j][:], tau[:], 0.0, ALU.mult, ["s_tau", "s_rmg"], ["s_Rt%d" % j], s2=rmg[:, j:j + 1], op1=ALU.add)
    S.op("dve", lambda g: g.memset(car[:], 0.0), writes=["s_car"])
    import os
    if os.environ.get("S5DBG"):
        d_er = dout("d_er", [128, T]); d_ei = dout("d_ei", [128, T]); d_rt = dout("d_rt", [128, T])
        d_col = dout("d_col", [128, 6]); d_bt = dout("d_bt", [64, 512])
        S.dma("sp", d_er, Er[0][:], reads=["s_Er0"], writes=["d_er"])
        S.dma("sp", d_ei, Ei[0][:], reads=["s_Ei0"], writes=["d_ei"])
        S.dma("sp", d_rt, Rt[0][:], reads=["s_Rt0"], writes=["d_rt"])
        S.dma("sp", d_col[:, 0:2], rmag[0], reads=[rmag[1]], writes=["d_col"])
        S.dma("sp", d_col[:, 2:4], cs[0], reads=[cs[1]], writes=["d_col2"])
        S.dma("sp", d_col[:, 4:6], sn[0], reads=[sn[1]], writes=["d_col3"])
        S.dma("sp", d_bt[:, 0:256], t1[0], reads=[t1[1]], writes=["d_bt"])
        S.dma("sp", d_bt[:, 256:512], t2[0], reads=[t2[1]], writes=["d_bt2"])

    k = 0
    for blk in range(NBLK):
        for b in range(NB):
            t0 = b * L + blk * T
            ui = k % 2
            k += 1
            ufn, ubn = "s_uf%d" % ui, "s_ub%d" % ui
            S.dma("sp", uf[ui][:], su[:, t0:t0 + T], writes=[ufn])
            S.act(ub[ui][:], uf[ui][:], AF.Copy, [ufn], [ubn])
            for j in range(2):
                js = slice(j * 128, (j + 1) * 128)
                S.mm(pbr[j][:], BTr[:, js], ub[ui][:], ["s_BTr", ubn], [("s_pbr", j)])
                S.mm(pbi[j][:], BTi[:, js], ub[ui][:], ["s_BTi", ubn], [("s_pbi", j)])
                ern, ein, rtn = "s_Er%d" % j, "s_Ei%d" % j, "s_Rt%d" % j
                a1, a1n = wk.get(); a2, a2n = wk.get(); a3, a3n = wk.get(); a4, a4n = wk.get()
                S.tt("dve", a1[:], pbr[j][:], Er[j][:], ALU.mult, [("s_pbr", j), ern], [a1n])
                S.tt("dve", a2[:], pbi[j][:], Ei[j][:], ALU.mult, [("s_pbi", j), ein], [a2n])
                S.tt("dve", a3[:], pbr[j][:], Ei[j][:], ALU.mult, [("s_pbr", j), ein], [a3n])
                S.tt("dve", a4[:], pbi[j][:], Er[j][:], ALU.mult, [("s_pbi", j), ern], [a4n])
                S.tt("pool", a1[:], a1[:], a2[:], ALU.subtract, [a1n, a2n], [a1n])
                S.tt("pool", a3[:], a3[:], a4[:], ALU.add, [a3n, a4n], [a3n])
                ci = j * NB + b
                Gr, Grn = wk.get(); Gi, Gin = wk.get()
                S.op("dve", lambda g: g.tensor_tensor_scan(out=Gr[:], data0=Rt[j][:], data1=a1[:], initial=car[:, 0, ci:ci + 1], op0=ALU.mult, op1=ALU.add),
                     reads=[rtn, a1n, "s_car"], writes=[Grn])
                S.op("dve", lambda g: g.tensor_tensor_scan(out=Gi[:], data0=Rt[j][:], data1=a3[:], initial=car[:, 1, ci:ci + 1], op0=ALU.mult, op1=ALU.add),
                     reads=[rtn, a3n, "s_car"], writes=[Gin])
                c1, c1n = ctmp.get(); c2, c2n = ctmp.get()
                Wr, Wi = W512[:, 0, j:j + 1], W512[:, 1, j:j + 1]
                S.ts("dve", c1[:], Gi[:, T - 1:T], Wi, ALU.mult, [Gin, "s_W512"], [c1n])
                S.ts("dve", c2[:], Gr[:, T - 1:T], Wi, ALU.mult, [Grn, "s_W512"], [c2n])
                S.stt(car[:, 0, ci:ci + 1], Gr[:, T - 1:T], Wr, c1[:], ALU.mult, ALU.subtract, [Grn, "s_W512", c1n, "s_car"], ["s_car"])
                S.stt(car[:, 1, ci:ci + 1], Gi[:, T - 1:T], Wr, c2[:], ALU.mult, ALU.add, [Gin, "s_W512", c2n, "s_car"], ["s_car"])
                S.tt("dve", a1[:], Gr[:], Er[j][:], ALU.mult, [Grn, ern], [a1n])
                S.tt("dve", a2[:], Gi[:], Ei[j][:], ALU.mult, [Gin, ein], [a2n])
                S.tt("dve", a3[:], Gi[:], Er[j][:], ALU.mult, [Gin, ern], [a3n])
                S.tt("dve", a4[:], Gr[:], Ei[j][:], ALU.mult, [Grn, ein], [a4n])
                S.tt("pool", hb[j][0][:], a1[:], a2[:], ALU.add, [a1n, a2n], [("s_hb", j, 0)])
                S.tt("pool", hb[j][1][:], a3[:], a4[:], ALU.subtract, [a3n, a4n], [("s_hb", j, 1)])
            pyi = k % 2
            n_ = 0
            for j in range(2):
                for c, ct in ((0, cTr), (1, cTi)):
                    S.mm(py[pyi][:], ct[:, j, :], hb[j][c][:], ["s_cTr", "s_cTi", ("s_hb", j, c)], [("s_py", pyi)], start=(n_ == 0), stop=(n_ == 3))
                    n_ += 1
            yv, yvn = yo.get(); tmp, tmpn = yo.get()
            S.stt(yv[:], uf[ui][:], sd[:, 0:1], py[pyi][:], ALU.mult, ALU.add, [ufn, "s_sd", ("s_py", pyi)], [yvn])
            emit_gelu(S, yv[:], yvn, yv[:], yvn, tmp[:], tmpn, eng2="dve")
            S.dma("sp", y5[:, t0:t0 + T], yv[:], reads=[yvn], writes=[("y5", t0)])
    S.barrier()
    for cm in reversed(st):
        cm.__exit__(None, None, None)


def emit_gdn(nc, S, L, NB, din, dout):
    NTOK = NB * L
    NT_ = NTOK // 128
    TSG = min(L, 1024)
    NSEG = L // TSG
    TPS = TSG // 128
    gq = din("gq", [128, 3, NTOK])
    gcw = din("gcw", [128, 3, 4])
    gz = din("gz", [NT_, 128, 128])
    gab = din("gab", [128, 2, NT_])
    gpar = din("gpar", [128, 2])
    gng = din("gng", [128, 128])
    gconst = din("gconst", [128, 8, 128])
    gy = dout("gy", [NT_, 128, 128])
    st = []
    A_ = lambda name, shape, dt=F32: _alloc(nc, st, "sb", name, shape, dt)
    cst = A_("g_cst", [128, 8, 128])
    ident, Ubd, Cbd, sel0, sel1, mSU, mIU, ones = [cst[:, i, :] for i in range(8)]
    cw = A_("g_cw", [128, 3, 4])
    ab = A_("g_ab", [128, 2, NT_])
    par = A_("g_par", [128, 2])
    ngb = A_("g_ngb", [128, 128])
    S.dma("sp", cst[:], gconst, writes=["g_cst"])
    S.dma("sp", cw[:], gcw, writes=["g_cw"])
    S.dma("sp", ab[:], gab, writes=["g_ab"])
    S.dma("sp", par[:], gpar, writes=["g_par"])
    S.dma("sp", ngb[:], gng, writes=["g_ngb"])
    pbig = _alloc(nc, st, "ps", "g_pbig", [128, 512], F32)
    pbanks = [_alloc(nc, st, "ps", "g_pb%d" % i, [128, 4, 128], F32) for i in range(6)]
    rslots = [[pbanks[4 + c][:, i, :] for i in range(4)] for c in range(2)]

    gp = Pool_(nc, st, "g_gp", [128, NT_], 12)
    sl = lambda t: t[:]
    beta = A_("g_beta", [128, NT_]); gc = A_("g_gc", [128, NT_]); kdsc = A_("g_kdsc", [128, NT_])
    egc = A_("g_egc", [128, NT_]); gcb = A_("g_gcb", [128, NT_]); begc = A_("g_begc", [128, NT_])
    gl = [A_("g_gl%d" % c, [128, NT_]) for c in range(2)]
    nea = A_("g_nea", [128, 1])
    S.act(beta[:], ab[:, 1, :], AF.Sigmoid, ["g_ab"], ["g_beta"])
    x_, xn_ = gp.get()
    S.ts("dve", x_[:], ab[:, 0, :], par[:, 1:2], ALU.add, ["g_ab", "g_par"], [xn_])
    sp_, spn_ = gp.get()
    emit_softplus(S, gp, x_[:], xn_, sp_[:], spn_, sl)
    S.act(nea[:], par[:, 0:1], AF.Exp, ["g_par"], ["g_nea"])
    NTP = max(NT_, 128)
    gpad = A_("g_gpad", [128, NTP])
    S.op("dve", lambda g: g.memset(gpad[:], 0.0), writes=["g_g"])
    g_ = gpad[:, 0:NT_]
    S.ts("dve", g_, sp_[:], nea[:, 0:1], ALU.mult, [spn_, "g_nea", "g_g"], ["g_g"], s2=-1.0, op1=ALU.mult)
    S.mm(pbig[:, 0:NTP], Ubd, gpad[:], ["g_cst", "g_g"], ["g_pbig"])
    S.act(gc[:], pbig[:, 0:NT_], AF.Copy, ["g_pbig"], ["g_gc"])
    S.mm(pbig[:, 0:NTP], Cbd, gpad[:], ["g_cst", "g_g"], ["g_pbig"])
    t_, tn_ = gp.get()
    S.tt("dve", t_[:], pbig[:, 0:NT_], gc[:], ALU.subtract, ["g_pbig", "g_gc"], [tn_])
    S.act(kdsc[:], t_[:], AF.Exp, [tn_], ["g_kdsc"])
    S.act(egc[:], gc[:], AF.Exp, ["g_gc"], ["g_egc"])
    S.act(t_[:], beta[:], AF.Ln, ["g_beta"], [tn_])
    S.tt("dve", gcb[:], gc[:], t_[:], ALU.add, ["g_gc", tn_], ["g_gcb"])
    S.tt("dve", begc[:], beta[:], egc[:], ALU.mult, ["g_beta", "g_egc"], ["g_begc"])
    for c, sel in ((0, sel0), (1, sel1)):
        S.mm(pbig[:, 0:NTP], sel, gpad[:], ["g_cst", "g_g"], ["g_pbig"])
        S.act(gl[c][:], pbig[:, 0:NT_], AF.Exp, ["g_pbig"], ["g_gl%d" % c])
    gcd = nc.dram_tensor("g_gcd", [2, NTP, 128], F32, kind="Internal").ap()
    gTt = [A_("g_gT%d" % i, [128, 128]) for i in range(2)]
    for w_, (src_, srcn_) in enumerate(((gc, "g_gc"), (gcb, "g_gcb"))):
        S.op("dve", lambda g: g.memset(gpad[:], 0.0), writes=["g_g"])
        S.op("dve", lambda g: g.tensor_copy(out=gpad[:, 0:NT_], in_=src_[:]), reads=[srcn_, "g_g"], writes=["g_g"])
        S.op("pe", lambda g: g.transpose(out=pbig[:, 0:128], in_=gpad[:, 0:128], identity=ident), reads=["g_g", "g_cst"], writes=["g_pbig"])
        gT, gTn = gTt[w_], "g_gT%d" % w_
        S.act(gT[:], pbig[:, 0:128], AF.Copy, ["g_pbig"], [gTn])
        S.dma("sp", gcd[w_, 0:128, :], gT[:], reads=[gTn], writes=[("g_gcd", w_)])

    xin = [A_("g_xin%d" % w, [128, TSG + 3]) for w in range(3)]
    cvs = [[A_("g_cv%d_%d" % (w, p), [128, TSG]) for w in range(3)] for p in range(2)]
    sq = A_("g_sq", [128, TSG])
    rn = A_("g_rn", [128, 512])
    qTbs = [A_("g_qTb%d" % p, [128, TSG], BF16) for p in range(2)]
    kTbs = [A_("g_kTb%d" % p, [128, TSG], BF16) for p in range(2)]
    Sst = A_("g_S", [128, 128]); Sbf = A_("g_Sbf", [128, 128], BF16)
    col = Pool_(nc, st, "g_col", [128, 1], 6)
    zt = [A_("g_zt%d" % i, [128, 128]) for i in range(2)]

    import os
    G = int(os.environ.get('GDN_G', '4'))
    NOINT = bool(os.environ.get('GDN_NOINT'))
    res = [None] * G
    pending_rec = [None]
    tps = [Pool_(nc, st, "g_tp%d_" % k, [128, 128], 26) for k in range(G)]
    tpbs = [Pool_(nc, st, "g_tpb%d_" % k, [128, 128], 6, BF16) for k in range(G)]
    rtp = Pool_(nc, st, "g_rtp", [128, 128], 8)
    rtpb = Pool_(nc, st, "g_rtpb", [128, 128], 4, BF16)
    cur = {}

    def tile_setup(ti, it, k, par):
        tp, tpb = tps[k], tpbs[k]
        cv_, qTb, kTb = cvs[par], qTbs[par], kTbs[par]
        CV1, CV2, QN, KN = "g_cv1_%d" % par, "g_cv2_%d" % par, "g_qTb%d" % par, "g_kTb%d" % par
        (ptk, ptv, pKK, pQK) = [pbanks[k][:, i, :] for i in range(4)]
        Bk = "g_pb%d" % k
        OLDS = bool(os.environ.get("GDN_OLDSLOTS"))
        kn_, vn_ = cv_[1], cv_[2]
        cs = slice(it * 128, (it + 1) * 128)
        tc_ = lambda a: a[:, ti:ti + 1]
        S.op("pe", lambda g: g.transpose(out=ptk[:], in_=kn_[:, cs], identity=ident), reads=[CV1, "g_cst"], writes=[Bk])
        kbg, kbgn = tp.get()
        S.ts("dve", kbg[:], ptk[:], tc_(begc), ALU.mult, [Bk, "g_begc"], [kbgn])
        kdec, kdecn = tpb.get()
        S.ts("dve", kdec[:], ptk[:], tc_(kdsc), ALU.mult, [Bk, "g_kdsc"], [kdecn])
        yield
        S.op("pe", lambda g: g.transpose(out=ptv[:], in_=vn_[:, cs], identity=ident), reads=[CV2, "g_cst"], writes=[Bk])
        bv, bvn = tp.get()
        S.ts("dve", bv[:], ptv[:], tc_(beta), ALU.mult, [Bk, "g_beta"], [bvn])
        dg1, dg1n = tp.get(); dg2, dg2n = tp.get()
        S.dma("sp", dg1[:], gcd[1, ti, :].partition_broadcast(128), reads=[("g_gcd", 1)], writes=[dg1n])
        S.dma("sp", dg2[:], gcd[0, ti, :].partition_broadcast(128), reads=[("g_gcd", 0)], writes=[dg2n])
        yield
        if OLDS:
            Bk = "g_pb1"
            pKK, pQK, pR1, pR2 = [pbanks[1][:, i, :] for i in range(4)]
        else:
            pKK, pQK, pR1, pR2 = ptk, ptv, pKK, pQK
        S.mm(pKK[:], kTb[:, cs], kTb[:, cs], [KN], [Bk])
        S.mm(pQK[:], kTb[:, cs], qTb[:, cs], [KN, QN], [Bk])
        E1, E1n = tp.get(); E2, E2n = tp.get()
        S.ts("dve", E1[:], dg1[:], tc_(gc), ALU.subtract, [dg1n, "g_gc"], [E1n], s2=0.0, op1=ALU.min)
        S.ts("dve", E2[:], dg2[:], tc_(gc), ALU.subtract, [dg2n, "g_gc"], [E2n], s2=0.0, op1=ALU.min)
        yield
        S.act(E1[:], E1[:], AF.Exp, [E1n], [E1n])
        S.act(E2[:], E2[:], AF.Exp, [E2n], [E2n])
        A0, A0n = tp.get()
        S.tt("dve", A0[:], pKK[:], E1[:], ALU.mult, [Bk, E1n], [A0n])
        S.tt("dve", E2[:], pQK[:], E2[:], ALU.mult, [Bk, E2n], [E2n])
        yield
        S.tt("pool", A0[:], A0[:], mSU, ALU.mult, [A0n, "g_cst"], [A0n])
        qkT, qkTn = tpb.get()
        S.tt("pool", qkT[:], E2[:], mIU, ALU.mult, [E2n, "g_cst"], [qkTn])
        yield
        pM, pP, pQ, pR = [pbanks[k][:, i, :] for i in range(4)]
        if OLDS:
            Bk = "g_pb2"
            pM, pP, pQ, pR = [pbanks[2][:, i, :] for i in range(4)]
        S.op("pe", lambda g: g.transpose(out=pM[:], in_=A0[:], identity=ident), reads=[A0n, "g_cst"], writes=[Bk])
        R, Rn = tp.get()
        S.tt("pool", R[:], ident, A0[:], ALU.subtract, ["g_cst", A0n], [Rn])
        M0, M0n = tp.get()
        S.act(M0[:], pM[:], AF.Copy, [Bk], [M0n])
        yield
        Pp, Ppn, Qp, Qpn = A0, A0n, M0, M0n
        for l in range(1, 6):
            S.mm(pQ[:], Pp[:], Qp[:], [Ppn, Qpn], [Bk])
            Qn_, Qnn = tp.get()
            S.act(Qn_[:], pQ[:], AF.Copy, [Bk], [Qnn])
            if l < 5:
                S.mm(pP[:], Qp[:], Pp[:], [Ppn, Qpn], [Bk])
                Pn_, Pnn = tp.get()
                S.op("dve", lambda g: g.tensor_copy(out=Pn_[:], in_=pP[:]), reads=[Bk], writes=[Pnn])
            yield
            S.mm(pR[:], Qn_[:], R[:], [Qnn, Rn], [Bk])
            R2, R2n = tp.get()
            S.tt("dve", R2[:], R[:], pR[:], ALU.add, [Rn, Bk], [R2n])
            R, Rn = R2, R2n
            Qp, Qpn = Qn_, Qnn
            if l < 5:
                Pp, Ppn = Pn_, Pnn
            yield
        pwT, pu = pM, pP
        if OLDS:
            pwT, pu = pM, pP
        S.mm(pwT[:], kbg[:], R[:], [kbgn, Rn], [Bk])
        wTb, wTbn = tpb.get()
        S.act(wTb[:], pwT[:], AF.Copy, [Bk], [wTbn])
        S.mm(pu[:], R[:], bv[:], [Rn, bvn], [Bk])
        u_, un_ = tp.get()
        S.op("dve", lambda g: g.tensor_copy(out=u_[:], in_=pu[:]), reads=[Bk], writes=[un_])
        res[k] = (kdec, kdecn, qkT, qkTn, wTb, wTbn, u_, un_)
        yield

    def recur_group(tis, its, ress, par):
        qTb, QN = qTbs[par], "g_qTb%d" % par
        for ti, it, (kdec, kdecn, qkT, qkTn, wTb, wTbn, u_, un_) in zip(tis, its, ress):
            zi = ti % 2
            S.dma("sp", zt[zi][:], gz[ti], writes=["g_zt%d" % zi])
            o_, on_ = rtp.get()
            vnew, vnewn = rtpb.get()
            tmp, tmpn = rtp.get()
            for c in range(2):
                p = slice(64 * c, 64 * c + 64)
                pwS, pqS, pqv, pdS = rslots[c]
                RB = "g_pb%d" % (4 + c)
                if os.environ.get("GDN_OLDSLOTS"):
                    pwS, pqS, pqv, pdS = [pbanks[3 + c][:, i, :] for i in range(4)]
                    RB = "g_pb%d" % (3 + c)
                tk = slice(it * 128 + 64 * c, it * 128 + 64 * c + 64)
                S.mm(pwS[p, :], wTb[:, p], Sbf[:], [wTbn, "g_Sbf"], [RB])
                S.mm(pqS[p, :], qTb[:, tk], Sbf[:], [QN, "g_Sbf"], [RB])
                S.tt("dve", vnew[p, :], u_[p, :], pwS[p, :], ALU.subtract, [un_, RB], [(vnewn, c)])
                yield
                S.mm(pdS[:], kdec[p, :], vnew[p, :], [kdecn, (vnewn, c)], [RB])
                S.mm(pqv[p, :], qkT[p, p], vnew[p, :], [qkTn, (vnewn, c)], [RB])
                S.stt(Sst[:], Sst[:], gl[c][:, ti:ti + 1], pdS[:], ALU.mult, ALU.add, ["g_S", "g_gl%d" % c, RB], ["g_S"])
                S.act(Sbf[:], Sst[:], AF.Copy, ["g_S"], ["g_Sbf"])
                yield
                S.act(tmp[p, :], pqv[p, :], AF.Copy, [RB], [(tmpn, c)])
                S.stt(o_[p, :], pqS[p, :], egc[p, ti:ti + 1], tmp[p, :], ALU.mult, ALU.add, [RB, "g_egc", (tmpn, c)], [(on_, c)])
                yield
            ss, ssn = col.get()
            junk, junkn = rtp.get()
            S.op("act", lambda g: g.activation(out=junk[:], in_=o_[:], func=AF.Square, accum_out=ss[:]),
                 reads=[(on_, 0), (on_, 1)], writes=[junkn, ssn])
            S.ts("dve", ss[:], ss[:], 1.0 / 128.0, ALU.mult, [ssn], [ssn], s2=1e-6, op1=ALU.add)
            S.act(ss[:], ss[:], AF.Sqrt, [ssn], [ssn])
            S.op("dve", lambda g: g.reciprocal(out=ss[:], in_=ss[:]), reads=[ssn], writes=[ssn])
            yield
            S.act(zt[zi][:], zt[zi][:], AF.Silu, ["g_zt%d" % zi], ["g_zt%d" % zi])
            y_, yn_ = rtp.get()
            S.stt(y_[:], o_[:], ss[:, 0:1], ngb[:], ALU.mult, ALU.mult, [(on_, 0), (on_, 1), ssn, "g_ngb"], [yn_])
            S.tt("pool", y_[:], y_[:], zt[zi][:], ALU.mult, [yn_, "g_zt%d" % zi], [yn_])
            S.dma("sp", gy[ti], y_[:], reads=[yn_], writes=[("gy", ti)])
            yield

    import os
    GSTOP = float(os.environ.get("GSTOP", "9"))
    def g1(b, s, par):
        cv_, qTb, kTb = cvs[par], qTbs[par], kTbs[par]
        tok0 = b * L + s * TSG
        for w in range(3):
            xn = "g_xin%d" % w
            cn = "g_cv%d_%d" % (w, par)
            if s == 0:
                S.op("dve", lambda g: g.memset(xin[w][:, 0:3], 0.0), writes=[xn])
                S.dma("sp", xin[w][:, 3:3 + TSG], gq[:, w, tok0:tok0 + TSG], reads=[xn], writes=[xn])
            else:
                S.dma("sp", xin[w][:, :], gq[:, w, tok0 - 3:tok0 + TSG], writes=[xn])
            S.ts("dve", cv_[w][:], xin[w][:, 3:3 + TSG], cw[:, w, 3:4], ALU.mult, [xn, "g_cw"], [cn])
            yield
            for k in (2, 1, 0):
                S.stt(cv_[w][:], xin[w][:, k:k + TSG], cw[:, w, k:k + 1], cv_[w][:], ALU.mult, ALU.add, [xn, "g_cw", cn], [cn])
                yield
            S.act(cv_[w][:], cv_[w][:], AF.Silu, [cn], [cn])
            yield
        for w in range(2):
            cn = "g_cv%d_%d" % (w, par)
            S.act(sq[:], cv_[w][:], AF.Square, [cn], ["g_sq"])
            QB = min(512, TSG)
            for q in range(TSG // QB):
                qs = slice(q * QB, (q + 1) * QB)
                S.mm(pbig[:, 0:QB], ones, sq[:, qs], ["g_cst", "g_sq"], ["g_pbig"])
                if w == 0:
                    S.act(rn[:, 0:QB], pbig[:, 0:QB], AF.Sqrt, ["g_pbig"], ["g_rn"], scale=128.0, bias=128.0 * 1e-6)
                else:
                    S.act(rn[:, 0:QB], pbig[:, 0:QB], AF.Sqrt, ["g_pbig"], ["g_rn"], scale=1.0, bias=1e-6)
                yield
                S.op("dve", lambda g: g.reciprocal(out=rn[:, 0:QB], in_=rn[:, 0:QB]), reads=["g_rn"], writes=["g_rn"])
                S.tt("dve", cv_[w][:, qs], cv_[w][:, qs], rn[:, 0:QB], ALU.mult, [cn, "g_rn"], [cn])
                yield
            S.act((qTb if w == 0 else kTb)[:], cv_[w][:], AF.Copy, [cn], ["g_qTb%d" % par if w == 0 else "g_kTb%d" % par])
            yield

    segs = [(b, s) for b in range(NB) for s in range(NSEG)]
    for _ in g1(segs[0][0], segs[0][1], 0):
        pass
    for si, (b, s) in enumerate(segs):
        par = si % 2
        tok0 = b * L + s * TSG
        if s == 0:
            S.op("dve", lambda g: g.memset(Sst[:], 0.0), writes=["g_S"])
            S.op("dve", lambda g: g.memset(Sbf[:], 0.0), writes=["g_Sbf"])
        g1n = [g1(segs[si + 1][0], segs[si + 1][1], 1 - par)] if si + 1 < len(segs) else []
        for g0 in range(0, TPS, G):
            its = list(range(g0, min(g0 + G, TPS)))
            gens = [tile_setup(tok0 // 128 + it, it, it % G, par) for it in its]
            live = ([pending_rec[0]] if pending_rec[0] is not None else []) + gens
            while live:
                nxt = []
                for gen in live:
                    try:
                        next(gen)
                        nxt.append(gen)
                    except StopIteration:
                        pass
                live = nxt
                if g1n:
                    try:
                        next(g1n[0])
                    except StopIteration:
                        g1n = []
            pending_rec[0] = recur_group([tok0 // 128 + it for it in its], its, [res[it % G] for it in its], par)
        if pending_rec[0] is not None:
            for _ in pending_rec[0]:
                if g1n:
                    try:
                        next(g1n[0])
                    except StopIteration:
                        g1n = []
        pending_rec[0] = None
        if g1n:
            for _ in g1n[0]:
                pass
    S.barrier()
    for cm in reversed(st):
        cm.__exit__(None, None, None)


def gdn_consts():
    i = np.arange(128)
    same = (i[:, None] // 64) == (i[None, :] // 64)
    c = np.zeros((128, 8, 128), np.float32)
    c[:, 0] = np.eye(128)
    c[:, 1] = (i[:, None] <= i[None, :]) & same
    c[:, 2] = same
    c[:, 3] = (i[:, None] < 64) * np.ones((1, 128))
    c[:, 4] = (i[:, None] >= 64) * np.ones((1, 128))
    c[:, 5] = (i[:, None] < i[None, :]) & same
    c[:, 6] = (i[:, None] <= i[None, :]) & same
    c[:, 7] = 1.0
    return c


_PROGS = {}


def _prog(key, fn):
    if key not in _PROGS:
        _PROGS[key] = fn()
    return _PROGS[key]


def _gbpack(g, b):
    return np.ascontiguousarray(np.concatenate([g.reshape(-1, 128).T, b.reshape(-1, 128).T], axis=1), dtype=np.float32)


def _tile_gu(w, FW=256):
    w = np.asarray(w, dtype=np.float32)
    D, F = w.shape
    nfb = (F + FW - 1) // FW
    if nfb * FW != F:
        w = np.concatenate([w, np.zeros((D, nfb * FW - F), np.float32)], axis=1)
    return np.ascontiguousarray(w.reshape(D // 128, 128, nfb, FW).transpose(2, 1, 0, 3))


def _tile_d(w):
    w = np.asarray(w, dtype=np.float32)
    F, D = w.shape
    FC = F // 128
    WDG = [k for k in (11, 4, 2, 1) if FC % k == 0][0]
    return np.ascontiguousarray(w.reshape(FC // WDG, WDG, 128, D // 128, 128).transpose(3, 0, 2, 1, 4))


def _run(nc, in_maps):
    res = run_bass_kernel_spmd(nc, in_maps, core_ids=list(range(len(in_maps))))
    return res.results


def _c(a):
    return np.ascontiguousarray(a, dtype=np.float32)


def kernel(x, ffn1_w_gate, ffn1_w_up, ffn1_w_down, ln1_g, ln1_b, w_in,
           s5_lambda_re, s5_lambda_im, s5_b_re, s5_b_im, s5_c_re, s5_c_im, s5_d, s5_log_step,
           s5_w_glu, s5_b_glu, gdn_conv_w, gdn_a_log, gdn_dt_bias, gdn_norm_g,
           lru_conv_w, lru_conv_b, lru_w_a, lru_b_a, lru_w_x, lru_b_x, lru_lambda,
           w_out, ln2_g, ln2_b, ffn2_w_gate, ffn2_w_up, ffn2_w_down, ln3_g, ln3_b, depth=None):
    x = np.asarray(x)
    B, L, D = x.shape
    depth = DEPTH if depth is None else depth
    NC = N_CORES
    NTOK = B * L
    NTc = NTOK // NC
    NT_ = NTOK // 128
    NPJ = w_in.shape[-1]
    A = lambda v: np.asarray(v)
    XT = _c(x.reshape(NTOK, D).T)
    tsl = lambda i: slice(i * NTc, (i + 1) * NTc)
    p_ffn1 = _prog(("ffn", NTc, NPJ), lambda: build_ffn(NTc, NPJ=NPJ))
    p_ffn2 = _prog(("ffn", NTc, 0), lambda: build_ffn(NTc))
    p_out = _prog(("out", NTc), lambda: build_ffn(NTc, F=D_MODEL, mode="mix", TB=min(1024, NTc)))
    p_mix = _prog(("mix", L, B), lambda: build_mix(L, B))
    gconst = gdn_consts()
    s5tau = _c(np.broadcast_to(np.arange(512, dtype=np.float32), (128, 512)))
    z64 = np.zeros((64, 64), np.float32)
    for l in range(depth):
        wg, wu, wd, wi = _tile_gu(A(ffn1_w_gate[l])), _tile_gu(A(ffn1_w_up[l])), _tile_d(A(ffn1_w_down[l])), _tile_gu(A(w_in[l]))
        gb = _gbpack(A(ln1_g[l]), A(ln1_b[l]))
        r = _run(p_ffn1, [dict(xT=_c(XT[:, tsl(i)]), wg=wg, wu=wu, wd=wd, gb=gb, win=wi) for i in range(NC)])
        X1T = np.concatenate([r[i]["yT"] for i in range(NC)], axis=1)
        PJ = np.concatenate([r[i]["pjT"] for i in range(NC)], axis=1)
        del r, wg, wu, wd, wi
        lre, lim, lst = A(s5_lambda_re[l]), A(s5_lambda_im[l]), A(s5_log_step[l])
        bre, bim, cre, cim = A(s5_b_re[l]), A(s5_b_im[l]), A(s5_c_re[l]), A(s5_c_im[l])
        gcwl = A(gdn_conv_w[l])
        lcw = A(lru_conv_w[l])
        ims = []
        for c in range(NC):
            r0, r1 = 4624 + c * 64, 5136 + c * 64
            im = {}
            im["lx"] = _c(np.concatenate([PJ[r0:r0 + 64, b * L:(b + 1) * L] for b in range(B)], axis=0))
            im["lg"] = _c(np.concatenate([PJ[r1:r1 + 64, b * L:(b + 1) * L] for b in range(B)], axis=0))
            cs = slice(c * 64, (c + 1) * 64)
            cols = [lcw[0, cs], lcw[1, cs], lcw[2, cs], lcw[3, cs], A(lru_conv_b[l])[cs], A(lru_lambda[l])[cs],
                    A(lru_b_a[l])[cs], A(lru_b_x[l])[cs]]
            im["lpar"] = _c(np.stack([np.tile(v, B) for v in cols], axis=1))
            wa, wx = A(lru_w_a[l][c]), A(lru_w_x[l][c])
            im["lwa"] = _c(np.block([[wa, z64], [z64, wa]]))
            im["lwx"] = _c(np.block([[wx, z64], [z64, wx]]))
            gs = slice(4 * c, 4 * c + 4)
            im["su"] = _c(PJ[c * 64:(c + 1) * 64, :])
            row = np.stack([lre[gs].reshape(-1), lim[gs].reshape(-1), np.repeat(lst[gs], 64)])
            im["s5row"] = _c(np.broadcast_to(row[None], (64, 3, 256)))
            col = np.zeros((128, 2, 3), np.float32)
            bT = np.zeros((64, 2, 256), np.float32)
            cT = np.zeros((128, 2, 2, 64), np.float32)
            for j in range(2):
                g2 = slice(4 * c + 2 * j, 4 * c + 2 * j + 2)
                col[:, j, 0] = lre[g2].reshape(-1)
                col[:, j, 1] = lim[g2].reshape(-1)
                col[:, j, 2] = np.repeat(lst[g2], 64)
            for g in range(4):
                gg = 4 * c + g
                bT[16 * g:16 * g + 16, 0, 64 * g:64 * g + 64] = bre[gg].T
                bT[16 * g:16 * g + 16, 1, 64 * g:64 * g + 64] = bim[gg].T
                j, rr = divmod(g, 2)
                cT[64 * rr:64 * rr + 64, j, 0, 16 * g:16 * g + 16] = cre[gg].T
                cT[64 * rr:64 * rr + 64, j, 1, 16 * g:16 * g + 16] = cim[gg].T
            im["s5col"], im["s5bT"], im["s5cT"] = col, bT, cT
            im["s5d"] = _c(A(s5_d[l])[c * 64:(c + 1) * 64].reshape(64, 1))
            im["s5tau"] = s5tau
            im["gq"] = _c(np.stack([PJ[512 + w * 1024 + c * 128:512 + w * 1024 + (c + 1) * 128, :] for w in range(3)], axis=1))
            im["gcw"] = _c(np.stack([gcwl[:, w * 1024 + c * 128:w * 1024 + (c + 1) * 128].T for w in range(3)], axis=1))
            im["gz"] = _c(PJ[3584 + c * 128:3584 + (c + 1) * 128, :].T.reshape(NT_, 128, 128))
            im["gab"] = _c(np.stack([PJ[4608 + c, :].reshape(NT_, 128).T, PJ[4616 + c, :].reshape(NT_, 128).T], axis=1))
            im["gpar"] = _c(np.tile(np.array([[A(gdn_a_log[l])[c], A(gdn_dt_bias[l])[c]]], np.float32), (128, 1)))
            im["gng"] = _c(np.tile(A(gdn_norm_g[l])[None, :], (128, 1)))
            im["gconst"] = gconst
            ims.append(im)
        r = _run(p_mix, ims)
        del ims, PJ
        YMT = np.empty((D_MODEL, NTOK), np.float32)
        for c in range(NC):
            YMT[c * 64:(c + 1) * 64, :] = r[c]["y5"]
            YMT[512 + c * 128:512 + (c + 1) * 128, :] = r[c]["gy"].reshape(NTOK, 128).T
            for b in range(B):
                YMT[1536 + c * 64:1536 + (c + 1) * 64, b * L:(b + 1) * L] = r[c]["ly"][b * 64:(b + 1) * 64, :]
        del r
        wgl, wo = _c(A(s5_w_glu[l])), _tile_d(A(w_out[l]))
        bgl = _c(A(s5_b_glu[l]).reshape(-1, 128).T)
        gb = _gbpack(A(ln2_g[l]), A(ln2_b[l]))
        r = _run(p_out, [dict(xT=_c(X1T[:, tsl(i)]), ymT=_c(YMT[:, tsl(i)]), wglu=wgl, bglu=bgl, wd=wo, gb=gb) for i in range(NC)])
        X2T = np.concatenate([r[i]["yT"] for i in range(NC)], axis=1)
        del r, YMT, X1T
        wg, wu, wd = _tile_gu(A(ffn2_w_gate[l])), _tile_gu(A(ffn2_w_up[l])), _tile_d(A(ffn2_w_down[l]))
        gb = _gbpack(A(ln3_g[l]), A(ln3_b[l]))
        r = _run(p_ffn2, [dict(xT=_c(X2T[:, tsl(i)]), wg=wg, wu=wu, wd=wd, gb=gb) for i in range(NC)])
        XT = np.concatenate([r[i]["yT"] for i in range(NC)], axis=1)
        del r, wg, wu, wd, X2T
    return np.ascontiguousarray(XT.T.reshape(B, L, D)).astype(np.float32)
```
